# Optimizing a Trainium2 kernel written in Bass

```python
import math
import jax, jax.numpy as jnp
from jax import lax
import numpy as np

D_MODEL = 1024
BATCH = 4
SEQ = 4096
DEPTH = 4
DEC_BATCH = 4
DEC_SEQ = 8192
PAST_LEN = 128

CHUNK = 128
EPS = 1e-6
N_BRANCH = 3
BRANCH_DIM = 1024
SSD_HEADS = 16
SSD_HEAD_DIM = 64
SSD_DIM = SSD_HEADS * SSD_HEAD_DIM
SSD_GROUPS = 2
SSD_STATE = 128
SSD_CONV = 4
SSD_CONV_CH = SSD_DIM + 2 * SSD_GROUPS * SSD_STATE
ML_HEADS = 4
ML_QK = 128
ML_V = 256
ML_QK_DIM = ML_HEADS * ML_QK
ML_V_DIM = ML_HEADS * ML_V
RET_HEADS = 4
RET_QK = 128
RET_V = 256
RET_QK_DIM = RET_HEADS * RET_QK
RET_V_DIM = RET_HEADS * RET_V
ROPE_BASE = 10000.0
FFN_DIM = -(-8 * D_MODEL // (3 * 256)) * 256
IN_SIZES = (SSD_DIM, SSD_CONV_CH, 2 * SSD_HEADS,
            ML_QK_DIM, ML_QK_DIM, ML_V_DIM, ML_V_DIM, 4 * ML_HEADS,
            RET_QK_DIM, RET_QK_DIM, RET_V_DIM, RET_V_DIM)
IN_COLS = sum(IN_SIZES)

kernel_name = "hybrid_bidir_ssd_mlstm_retention_encoder"

F32 = jnp.float32


def split_cols(t, sizes):
    out, start = [], 0
    for n in sizes:
        out.append(t[..., start:start + n])
        start += n
    return out


def _flip(t):
    return jnp.flip(t, axis=1)


def rms_norm(x, g):
    xf = x.astype(F32)
    y = xf * lax.rsqrt(jnp.mean(xf * xf, axis=-1, keepdims=True) + EPS)
    return (y * g.astype(F32)).astype(x.dtype)


def head_rms_norm(y, g):
    y = y * lax.rsqrt(jnp.mean(y * y, axis=-1, keepdims=True) + EPS)
    return y.reshape(*y.shape[:-2], -1) * g.astype(F32)


def centred_conv(u, w, b):
    s = u.shape[1]
    left = SSD_CONV // 2
    right = SSD_CONV - 1 - left
    up = jnp.pad(u, ((0, 0), (left, right), (0, 0)))
    out = b
    for t in range(SSD_CONV):
        out = out + up[:, t:t + s] * w[t]
    return out


def rotary(t, pos):
    half = t.shape[-1] // 2
    inv = ROPE_BASE ** (-jnp.arange(half, dtype=F32) / half)
    ang = pos[:, None] * inv
    cos = jnp.cos(ang)[None, :, None, :]
    sin = jnp.sin(ang)[None, :, None, :]
    t1, t2 = t[..., :half], t[..., half:]
    return jnp.concatenate([t1 * cos - t2 * sin, t1 * sin + t2 * cos], axis=-1)


def ssd_scan(x, dt, a, bm, cm):
    bsz, s = x.shape[0], x.shape[1]
    nc = s // CHUNK
    e = SSD_HEADS // SSD_GROUPS
    x = x.reshape(bsz, nc, CHUNK, SSD_GROUPS, e, SSD_HEAD_DIM)
    dt = dt.reshape(bsz, nc, CHUNK, SSD_GROUPS, e)
    bm = bm.reshape(bsz, nc, CHUNK, SSD_GROUPS, SSD_STATE)
    cm = cm.reshape(bsz, nc, CHUNK, SSD_GROUPS, SSD_STATE)
    cs = jnp.cumsum(dt * a.reshape(SSD_GROUPS, e), axis=2)
    xdt = x * dt[..., None]
    tri = jnp.tril(jnp.ones((CHUNK, CHUNK), dtype=bool))[:, :, None, None]
    seg = cs[:, :, :, None] - cs[:, :, None, :]
    decay = jnp.exp(jnp.where(tri, seg, -jnp.inf))
    cb = jnp.einsum('bclgn,bcsgn->bclsg', cm, bm)
    y_diag = jnp.einsum('bclsge,bcsgep->bclgep', cb[..., None] * decay, xdt)
    to_end = jnp.exp(cs[:, :, -1:] - cs)
    states = jnp.einsum('bclgn,bclge,bclgep->bcgepn', bm, to_end, xdt)
    chunk_decay = jnp.exp(cs[:, :, -1])

    def step(h, inp):
        st, dec = inp
        return h * dec[..., None, None] + st, h

    h0 = jnp.zeros((bsz, SSD_GROUPS, e, SSD_HEAD_DIM, SSD_STATE), F32)
    _, prev = lax.scan(step, h0, (jnp.moveaxis(states, 1, 0), jnp.moveaxis(chunk_decay, 1, 0)))
    prev = jnp.moveaxis(prev, 0, 1)
    y_off = jnp.einsum('bclgn,bcgepn,bclge->bclgep', cm, prev, jnp.exp(cs))
    return (y_diag + y_off).reshape(bsz, s, SSD_HEADS, SSD_HEAD_DIM)


def ssd_mixer(z, xbc, dt_raw, conv_w, conv_b, a_log, dt_bias, d_skip, norm_g):
    dtype = z.dtype
    bsz, s = z.shape[0], z.shape[1]
    xbc = jax.nn.silu(centred_conv(xbc.astype(F32), conv_w.astype(F32), conv_b.astype(F32)))
    xs, bm, cm = split_cols(xbc, (SSD_DIM, SSD_GROUPS * SSD_STATE, SSD_GROUPS * SSD_STATE))
    xs = xs.reshape(bsz, s, SSD_HEADS, SSD_HEAD_DIM)
    bm = bm.reshape(bsz, s, SSD_GROUPS, SSD_STATE)
    cm = cm.reshape(bsz, s, SSD_GROUPS, SSD_STATE)
    dt = jax.nn.softplus(dt_raw.astype(F32).reshape(bsz, s, 2, SSD_HEADS) + dt_bias.astype(F32))
    a = -jnp.exp(a_log.astype(F32))
    y_f = ssd_scan(xs, dt[:, :, 0], a[0], bm, cm)
    y_b = _flip(ssd_scan(_flip(xs), _flip(dt[:, :, 1]), a[1], _flip(bm), _flip(cm)))
    y = y_f + y_b + xs * d_skip.astype(F32)[:, None]
    y = y.reshape(bsz, s, SSD_DIM) * jax.nn.silu(z.astype(F32))
    return rms_norm(y, norm_g).astype(dtype)


def mlstm_scan(q, k, v, i_pre, f_pre):
    bsz, s = q.shape[0], q.shape[1]
    nc = s // CHUNK

    def chunks(t):
        return jnp.moveaxis(t.reshape(bsz, nc, CHUNK, *t.shape[2:]), 1, 0)

    tri = jnp.tril(jnp.ones((CHUNK, CHUNK), dtype=bool))[None, :, :, None]

    def step(carry, inp):
        c_st, n_st, m_st = carry
        qq, kk, vv, ii, lf = inp
        bcum = jnp.cumsum(lf, axis=1)
        logd = jnp.where(tri, bcum[:, :, None] - bcum[:, None] + ii[:, None], -jnp.inf)
        inter = bcum + m_st[:, None]
        m_t = jnp.maximum(inter, jnp.max(logd, axis=2))
        w = jnp.exp(logd - m_t[:, :, None])
        w_inter = jnp.exp(inter - m_t)
        scores = jnp.einsum('bthd,bshd->btsh', qq, kk) * w
        num = (jnp.einsum('btsh,bshv->bthv', scores, vv)
               + w_inter[..., None] * jnp.einsum('bthd,bhdv->bthv', qq, c_st))
        den = jnp.sum(scores, axis=2) + w_inter * jnp.einsum('bthd,bhd->bth', qq, n_st)
        h = num / jnp.maximum(jnp.abs(den), jnp.exp(-m_t))[..., None]
        btot = bcum[:, -1]
        log_end = btot[:, None] - bcum + ii
        m_new = jnp.maximum(btot + m_st, jnp.max(log_end, axis=1))
        w_end = jnp.exp(log_end - m_new[:, None])
        dec = jnp.exp(btot + m_st - m_new)
        c_new = dec[..., None, None] * c_st + jnp.einsum('blh,blhd,blhv->bhdv', w_end, kk, vv)
        n_new = dec[..., None] * n_st + jnp.einsum('blh,blhd->bhd', w_end, kk)
        return (c_new, n_new, m_new), h

    carry0 = (jnp.zeros((bsz, ML_HEADS, ML_QK, ML_V), F32),
              jnp.zeros((bsz, ML_HEADS, ML_QK), F32),
              jnp.zeros((bsz, ML_HEADS), F32))
    xs = (chunks(q), chunks(k), chunks(v), chunks(i_pre), chunks(jax.nn.log_sigmoid(f_pre)))
    _, h = lax.scan(step, carry0, xs)
    return jnp.moveaxis(h, 0, 1).reshape(bsz, s, ML_HEADS, ML_V)


def mlstm_mixer(q, k, v, o, gates, gate_bias, norm_g):
    dtype = o.dtype
    bsz, s = q.shape[0], q.shape[1]
    q = q.astype(F32).reshape(bsz, s, ML_HEADS, ML_QK) * (ML_QK ** -0.5)
    k = k.astype(F32).reshape(bsz, s, ML_HEADS, ML_QK)
    v = v.astype(F32).reshape(bsz, s, ML_HEADS, ML_V)
    g = gates.astype(F32).reshape(bsz, s, 4, ML_HEADS) + gate_bias.astype(F32)
    h_f = mlstm_scan(q, k, v, g[:, :, 0], g[:, :, 1])
    h_b = _flip(mlstm_scan(_flip(q), _flip(k), _flip(v), _flip(g[:, :, 2]), _flip(g[:, :, 3])))
    h = head_rms_norm(h_f + h_b, norm_g)
    return (jax.nn.sigmoid(o.astype(F32)) * h).astype(dtype)


def retention_scan(q, k, v, log_gamma, include_diag):
    bsz, s = q.shape[0], q.shape[1]
    nc = s // CHUNK
    qc = q.reshape(bsz, nc, CHUNK, RET_HEADS, RET_QK)
    kc = k.reshape(bsz, nc, CHUNK, RET_HEADS, RET_QK)
    vc = v.reshape(bsz, nc, CHUNK, RET_HEADS, RET_V)
    pos = jnp.arange(CHUNK, dtype=F32)
    dist = pos[:, None] - pos[None, :]
    mask = (dist >= 0) if include_diag else (dist > 0)
    dmat = jnp.where(mask[:, :, None], jnp.exp(jnp.maximum(dist, 0.0)[:, :, None] * log_gamma), 0.0)
    scores = jnp.einsum('bclhd,bcshd->bclsh', qc, kc) * dmat
    y_intra = jnp.einsum('bclsh,bcshv->bclhv', scores, vc)
    zeta = jnp.exp((CHUNK - 1 - pos)[:, None] * log_gamma)
    xi = jnp.exp((pos + 1)[:, None] * log_gamma)
    states = jnp.einsum('bcshd,sh,bcshv->bchdv', kc, zeta, vc)
    chunk_decay = jnp.exp(CHUNK * log_gamma)

    def step(r, st):
        return r * chunk_decay[:, None, None] + st, r

    r0 = jnp.zeros((bsz, RET_HEADS, RET_QK, RET_V), F32)
    _, prev = lax.scan(step, r0, jnp.moveaxis(states, 1, 0))
    prev = jnp.moveaxis(prev, 0, 1)
    y_cross = jnp.einsum('bclhd,bchdv,lh->bclhv', qc, prev, xi)
    return (y_intra + y_cross).reshape(bsz, s, RET_HEADS, RET_V)


def retention_mixer(q, k, v, g, norm_g):
    dtype = g.dtype
    bsz, s = q.shape[0], q.shape[1]
    pos = jnp.arange(s, dtype=F32)
    q = rotary(q.astype(F32).reshape(bsz, s, RET_HEADS, RET_QK), pos)
    k = rotary(k.astype(F32).reshape(bsz, s, RET_HEADS, RET_QK), pos) * (RET_QK ** -0.5)
    v = v.astype(F32).reshape(bsz, s, RET_HEADS, RET_V)
    log_gamma = jnp.log(1.0 - jnp.exp(jnp.linspace(math.log(1.0 / 32.0), math.log(1.0 / 512.0), RET_HEADS, dtype=F32)))
    y = (retention_scan(q, k, v, log_gamma, True)
         + _flip(retention_scan(_flip(q), _flip(k), _flip(v), log_gamma, False)))
    y = head_rms_norm(y, norm_g)
    return (jax.nn.silu(g.astype(F32)) * y).astype(dtype)


def encoder_trunk(x, norm_mix_g, w_in, w_gate, b_gate, conv_w, conv_b, ssd_a_log, ssd_dt_bias,
                  ssd_d, ssd_norm_g, mlstm_gate_bias, mlstm_norm_g, ret_norm_g, w_branch, w_out,
                  norm_ffn_g, w_ffn_gate, w_ffn_up, w_ffn_down, final_norm_g):
    for layer in range(DEPTH):
        u = rms_norm(x, norm_mix_g[layer])
        (ssd_z, ssd_xbc, ssd_dt, ml_q, ml_k, ml_v, ml_o, ml_g,
         rt_q, rt_k, rt_v, rt_g) = split_cols(u @ w_in[layer], IN_SIZES)
        gate = jax.nn.sigmoid((u @ w_gate[layer] + b_gate[layer]).astype(F32))
        g_a, g_b, g_c = split_cols(gate, (D_MODEL, D_MODEL, D_MODEL))
        y_a = ssd_mixer(ssd_z, ssd_xbc, ssd_dt, conv_w[layer], conv_b[layer], ssd_a_log[layer],
                        ssd_dt_bias[layer], ssd_d[layer], ssd_norm_g[layer]) @ w_branch[layer, 0]
        y_b = mlstm_mixer(ml_q, ml_k, ml_v, ml_o, ml_g, mlstm_gate_bias[layer],
                          mlstm_norm_g[layer]) @ w_branch[layer, 1]
        y_c = retention_mixer(rt_q, rt_k, rt_v, rt_g, ret_norm_g[layer]) @ w_branch[layer, 2]
        merged = (g_a * y_a + g_b * y_b + g_c * y_c).astype(x.dtype)
        x = x + merged @ w_out[layer]
        h = rms_norm(x, norm_ffn_g[layer])
        x = x + (jax.nn.silu(h @ w_ffn_gate[layer]) * (h @ w_ffn_up[layer])) @ w_ffn_down[layer]
    return rms_norm(x, final_norm_g)


def setup_inputs(seed: int = 0) -> dict:
    key = jax.random.key(seed)
    ks = jax.random.split(key, 24)

    def nrm(k, shape, scale):
        return jax.random.normal(k, shape, F32) * scale

    x_prompt = nrm(ks[0], (BATCH, SEQ, D_MODEL), 1.0)
    x_sample = nrm(ks[1], (DEC_BATCH, DEC_SEQ, D_MODEL), 1.0)
    norm_mix_g = 1.0 + nrm(ks[2], (DEPTH, D_MODEL), 0.01)
    w_in = nrm(ks[3], (DEPTH, D_MODEL, IN_COLS), D_MODEL ** -0.5)
    w_gate = nrm(ks[4], (DEPTH, D_MODEL, N_BRANCH * D_MODEL), D_MODEL ** -0.5)
    b_gate = nrm(ks[5], (DEPTH, N_BRANCH * D_MODEL), 0.01)
    conv_w = nrm(ks[6], (DEPTH, SSD_CONV, SSD_CONV_CH), SSD_CONV ** -0.5)
    conv_b = nrm(ks[7], (DEPTH, SSD_CONV_CH), 0.01)
    ssd_a_log = jnp.log(jax.random.uniform(ks[8], (DEPTH, 2, SSD_HEADS), F32, 1.0, 16.0))
    dt0 = jnp.exp(jax.random.uniform(ks[9], (DEPTH, 2, SSD_HEADS), F32, math.log(1e-3), math.log(1e-1)))
    ssd_dt_bias = dt0 + jnp.log(-jnp.expm1(-dt0))
    ssd_d = 1.0 + nrm(ks[10], (DEPTH, SSD_HEADS), 0.1)
    ssd_norm_g = 1.0 + nrm(ks[11], (DEPTH, SSD_DIM), 0.01)
    f_base = jnp.linspace(3.0, 6.0, ML_HEADS, dtype=F32)
    gate_base = jnp.stack([jnp.zeros_like(f_base), f_base, jnp.zeros_like(f_base), f_base])
    mlstm_gate_bias = gate_base + nrm(ks[12], (DEPTH, 4, ML_HEADS), 0.1)
    mlstm_norm_g = 1.0 + nrm(ks[13], (DEPTH, ML_V_DIM), 0.01)
    ret_norm_g = 1.0 + nrm(ks[14], (DEPTH, RET_V_DIM), 0.01)
    w_branch = nrm(ks[15], (DEPTH, N_BRANCH, BRANCH_DIM, D_MODEL), BRANCH_DIM ** -0.5)
    w_out = nrm(ks[16], (DEPTH, D_MODEL, D_MODEL), D_MODEL ** -0.5)
    norm_ffn_g = 1.0 + nrm(ks[17], (DEPTH, D_MODEL), 0.01)
    w_ffn_gate = nrm(ks[18], (DEPTH, D_MODEL, FFN_DIM), D_MODEL ** -0.5)
    w_ffn_up = nrm(ks[19], (DEPTH, D_MODEL, FFN_DIM), D_MODEL ** -0.5)
    w_ffn_down = nrm(ks[20], (DEPTH, FFN_DIM, D_MODEL), FFN_DIM ** -0.5)
    final_norm_g = 1.0 + nrm(ks[21], (D_MODEL,), 0.01)
    return {"x_prompt": x_prompt, "x_sample": x_sample, "norm_mix_g": norm_mix_g, "w_in": w_in,
            "w_gate": w_gate, "b_gate": b_gate, "conv_w": conv_w, "conv_b": conv_b,
            "ssd_a_log": ssd_a_log, "ssd_dt_bias": ssd_dt_bias, "ssd_d": ssd_d,
            "ssd_norm_g": ssd_norm_g, "mlstm_gate_bias": mlstm_gate_bias,
            "mlstm_norm_g": mlstm_norm_g, "ret_norm_g": ret_norm_g, "w_branch": w_branch,
            "w_out": w_out, "norm_ffn_g": norm_ffn_g, "w_ffn_gate": w_ffn_gate,
            "w_ffn_up": w_ffn_up, "w_ffn_down": w_ffn_down, "final_norm_g": final_norm_g}


def reference(x_prompt, x_sample, norm_mix_g, w_in, w_gate, b_gate, conv_w, conv_b, ssd_a_log,
              ssd_dt_bias, ssd_d, ssd_norm_g, mlstm_gate_bias, mlstm_norm_g, ret_norm_g, w_branch,
              w_out, norm_ffn_g, w_ffn_gate, w_ffn_up, w_ffn_down, final_norm_g):
    y_prompt = encoder_trunk(x_prompt, norm_mix_g, w_in, w_gate, b_gate, conv_w, conv_b, ssd_a_log,
                             ssd_dt_bias, ssd_d, ssd_norm_g, mlstm_gate_bias, mlstm_norm_g, ret_norm_g,
                             w_branch, w_out, norm_ffn_g, w_ffn_gate, w_ffn_up, w_ffn_down, final_norm_g)
    y_sample = encoder_trunk(x_sample, norm_mix_g, w_in, w_gate, b_gate, conv_w, conv_b, ssd_a_log,
                             ssd_dt_bias, ssd_d, ssd_norm_g, mlstm_gate_bias, mlstm_norm_g, ret_norm_g,
                             w_branch, w_out, norm_ffn_g, w_ffn_gate, w_ffn_up, w_ffn_down, final_norm_g)
    return (y_prompt, y_sample)
```

```python
import math
import numpy as np
import ml_dtypes
from contextlib import ExitStack
import concourse.bass as bass
import concourse.mybir as mybir
from concourse.bass_utils import run_bass_kernel_spmd

F32 = mybir.dt.float32
BF16 = mybir.dt.bfloat16
AF = mybir.ActivationFunctionType
ALU = mybir.AluOpType

ENGS = ("pe", "act", "dve", "pool", "sp")
NDMA = 24
EPS = 1e-6
D = 1024
FF = 2816
NEG = -60000.0
QS = 128 ** -0.5
A_COLS = 8704
OFF = dict(z=0, xbc=1024, mq=2560, mk=3072, mv=3584, mo=4608, rq=5632, rk=6144, rv=6656, rg=7680)
LOGG = [math.log(1.0 - math.exp(v)) for v in
        np.linspace(math.log(1.0 / 32.0), math.log(1.0 / 512.0), 4, dtype=np.float32).astype(np.float64)]


class Ctx:
    def __init__(self, nc, es):
        self.nc = nc
        self.es = es
        self.q = {e: [] for e in ENGS}
        self.cnt = {e: 0 for e in ENGS}
        self.seen = {e: {} for e in ENGS}
        self.lastw = {}
        self.rd = {}
        self.semh = {}
        for e in ENGS:
            self.semh[e] = es.enter_context(nc.semaphore("s_" + e))
        for i in range(NDMA):
            self.semh[("d", i)] = es.enter_context(nc.semaphore("s_d%d" % i))
        self.dma_cnt = [0] * NDMA
        self.dma_last = [None] * NDMA
        self.dma_rr = 0
        self.nops = 0
        self.pes = None

    def sb(self, name, shape, dtype=F32):
        st = self.pes if self.pes is not None else self.es
        self.nsb = getattr(self, "nsb", 0) + 1
        return st.enter_context(self.nc.sbuf_tensor("sb%d_%s" % (self.nsb, name), list(shape), dtype))

    def op(self, eng, fn, reads=(), writes=(), dma=False):
        evs = {}

        def add(ev):
            if ev is None:
                return
            s, v = ev
            if evs.get(s, 0) < v:
                evs[s] = v

        for k in reads:
            add(self.lastw.get(k))
        for k in writes:
            add(self.lastw.get(k))
            for s, v in self.rd.get(k, {}).items():
                add((s, v))
        if dma:
            i = self.dma_rr
            self.dma_rr = (i + 1) % NDMA
            add(self.dma_last[i])
            self.dma_cnt[i] += 16
            ev = (("d", i), self.dma_cnt[i])
            self.dma_last[i] = ev
        else:
            self.cnt[eng] += 1
            ev = (eng, self.cnt[eng])
        waits = []
        seen = self.seen[eng]
        for s, v in evs.items():
            if eng == "pe" and s == "pe":
                continue
            if seen.get(s, 0) < v:
                waits.append((s, v))
                seen[s] = v
        self.q[eng].append((fn, waits, ev, dma))
        for k in writes:
            self.lastw[k] = ev
            self.rd[k] = {}
        for k in reads:
            d = self.rd.setdefault(k, {})
            s, v = ev
            if d.get(s, 0) < v:
                d[s] = v
        self.nops += 1

    def dma(self, out, in_, r=(), w=(), eng="sp"):
        self.op(eng, lambda e: e.dma_start(out=out, in_=in_), r, w, dma=True)

    def mm(self, out, lhsT, rhs, start=True, stop=True, r=(), w=()):
        self.op("pe", lambda e: e.matmul(out, lhsT, rhs, start=start, stop=stop), r, w)

    def tr(self, out, in_, ident, r=(), w=()):
        self.op("pe", lambda e: e.transpose(out, in_, ident), r, w)

    def act(self, out, in_, func, r=(), w=(), **kw):
        self.op("act", lambda e: e.activation(out, in_, func, **kw), r, w)

    def tt(self, out, in0, in1, op, r=(), w=(), eng="dve"):
        self.op(eng, lambda e: e.tensor_tensor(out, in0, in1, op), r, w)

    def ts(self, out, in0, s1, s2, op0, op1=None, r=(), w=(), eng="dve"):
        if op1 is None:
            self.op(eng, lambda e: e.tensor_scalar(out, in0, s1, None, op0), r, w)
        else:
            self.op(eng, lambda e: e.tensor_scalar(out, in0, s1, s2, op0, op1), r, w)

    def stt(self, out, in0, sc, in1, op0, op1, r=(), w=()):
        self.op("dve", lambda e: e.scalar_tensor_tensor(out, in0, sc, in1, op0, op1), r, w)

    def cp(self, out, in_, r=(), w=(), eng="dve"):
        if eng == "act":
            self.op("act", lambda e: e.copy(out, in_), r, w)
        else:
            self.op(eng, lambda e: e.tensor_copy(out, in_), r, w)

    def memset(self, ap, val, w=(), eng="pool"):
        self.op(eng, lambda e: e.memset(ap, val), (), w)

    def recip(self, out, in_, r=(), w=()):
        self.op("dve", lambda e: e.reciprocal(out, in_), r, w)

    def emit(self, final=False):
        nc = self.nc
        fence = []
        for i in range(NDMA):
            if self.dma_cnt[i]:
                fence.append((("d", i), self.dma_cnt[i]))
        for e in ENGS:
            if self.cnt[e]:
                fence.append((e, self.cnt[e]))
        qs = self.q
        with nc.Block() as block:
            def run(name, e):
                for fn, waits, ev, dma in qs[name]:
                    for s, v in waits:
                        e.wait_ge(self.semh[s], v)
                    ins = fn(e)
                    ins.then_inc(self.semh[ev[0]], 16 if dma else 1)
                for s, v in fence:
                    if s != name:
                        e.wait_ge(self.semh[s], v)

            @block.tensor
            def _(e):
                run("pe", e)

            @block.scalar
            def _(e):
                run("act", e)

            @block.vector
            def _(e):
                run("dve", e)

            @block.gpsimd
            def _(e):
                run("pool", e)

            @block.sync
            def _(e):
                run("sp", e)
        self.q = {e: [] for e in ENGS}
        for e in ENGS:
            for s, v in fence:
                self.seen[e][s] = v
        self.lastw = {}
        self.rd = {}


def host_consts():
    j = np.arange(128)
    tri = (j[:, None] <= j[None, :]).astype(np.float32)
    triT = (j[:, None] >= j[None, :]).astype(np.float32)
    ident = np.eye(128, dtype=np.float32)
    ones = np.ones((128, 128), np.float32)
    nmf = np.where(j[:, None] <= j[None, :], 0.0, NEG).astype(np.float32)
    nmb = np.where(j[:, None] >= j[None, :], 0.0, NEG).astype(np.float32)
    cb16 = np.concatenate([ident, tri, triT, ones, np.tile(nmf, (1, 4)), np.tile(nmb, (1, 4))], axis=1)
    lg = np.array(LOGG, np.float64)
    dist = np.abs(j[:, None] - j[None, :]).astype(np.float64)
    dtab = np.concatenate([np.exp(dist * lg[h]) for h in range(4)], axis=1)
    xif = np.tile(np.concatenate([np.exp((j + 1) * lg[h]) for h in range(4)])[None, :], (128, 1))
    xib = np.tile(np.concatenate([np.exp((128 - j) * lg[h]) for h in range(4)])[None, :], (128, 1))
    zf = np.stack([np.exp((127 - j) * lg[h]) for h in range(4)], axis=1)
    zb = np.stack([np.exp(j * lg[h]) for h in range(4)], axis=1)
    cf32 = np.concatenate([dtab, xif, xib, zf, zb, ident], axis=1).astype(np.float32)
    return cb16.astype(ml_dtypes.bfloat16), cf32


C16 = dict(ident=0, tri=128, triT=256, ones=384, nmf=512, nmb=1024)
C32 = dict(dtab=0, xif=512, xib=1024, zf=1536, zb=1540, identf=1544)
NC16 = 1536
NC32 = 1672


def rope_tables(seglen, nseg_positions):
    half = 64
    inv = (10000.0 ** (-np.arange(half, dtype=np.float32) / half)).astype(np.float32)
    pos = nseg_positions.astype(np.float32)
    ang = pos[:, None] * inv[None, :]
    cos = np.cos(ang).astype(np.float32)
    sin = np.sin(ang).astype(np.float32)
    return np.concatenate([cos, sin, cos * np.float32(QS), sin * np.float32(QS)], axis=1)


def build(DEPTH, SEG, debug=False):
    NCHS = SEG // 128
    NCH = 2 * NCHS
    NT = 2 * SEG
    NBLK = NT // 512
    XBR = 2 * (SEG + 3)
    nc = bass.Bass("TRN2", target_bir_lowering=False)

    def din(name, shape, dt=F32):
        return nc.dram_tensor(name, list(shape), dt, kind="ExternalInput").ap()

    def dscr(name, shape, dt):
        return nc.dram_tensor(name, list(shape), dt, kind=("ExternalOutput" if debug else "Internal")).ap()

    x_in = din("x", [NT, D])
    flag_in = din("flag", [128, 1])
    rope_in = din("rope", [NT, 256])
    cb16_in = din("cb16", [128, NC16], BF16)
    cf32_in = din("cf32", [128, NC32])
    w_in = din("w_in", [DEPTH, D, 8752])
    w_gate = din("w_gate", [DEPTH, D, 3072])
    w_br = din("w_branch", [DEPTH, 3, D, D])
    w_out = din("w_out", [DEPTH, D, D])
    w_fg = din("w_ffn_gate", [DEPTH, D, FF])
    w_fu = din("w_ffn_up", [DEPTH, D, FF])
    w_fd = din("w_ffn_down", [DEPTH, FF, D])
    vecs = {}
    for name, n in [("norm_mix_g", D), ("b_gate", 3072), ("conv_w", 4 * 1536), ("conv_b", 1536),
                    ("ssd_a_log", 32), ("ssd_dt_bias", 32), ("ssd_d", 16), ("ssd_norm_g", D),
                    ("mlstm_gate_bias", 16), ("mlstm_norm_g", D), ("ret_norm_g", D), ("norm_ffn_g", D)]:
        vecs[name] = din(name, [DEPTH, n])
    fin_g = din("final_norm_g", [D])
    y_out = nc.dram_tensor("y", [NT, D], F32, kind="ExternalOutput").ap()

    wb_in = dscr("wb_in", [DEPTH, D, 8752], BF16) if False else nc.dram_tensor("wb_in", [DEPTH, D, 8752], BF16, kind="Internal").ap()
    wb_gate = nc.dram_tensor("wb_gate", [DEPTH, D, 3072], BF16, kind="Internal").ap()
    wb_br = nc.dram_tensor("wb_br", [DEPTH, 3, D, D], BF16, kind="Internal").ap()
    wb_out = nc.dram_tensor("wb_out", [DEPTH, D, D], BF16, kind="Internal").ap()
    wb_fg = nc.dram_tensor("wb_fg", [DEPTH, D, FF], BF16, kind="Internal").ap()
    wb_fu = nc.dram_tensor("wb_fu", [DEPTH, D, FF], BF16, kind="Internal").ap()
    wb_fd = nc.dram_tensor("wb_fd", [DEPTH, FF, D], BF16, kind="Internal").ap()
    XS = dscr("XS", [NT, D], F32)
    A_ = dscr("A_", [NT, A_COLS], BF16)
    XB = dscr("XB", [XBR, 1536], BF16)
    G_ = dscr("G_", [NT, 48], F32)
    GT = dscr("GT", [NT, 3072], BF16)
    SB = dscr("SBst", [NCH, 128, 3076], BF16)
    MO = dscr("MO", [NT, 3072], BF16)

    def xbrow(tok):
        s = tok // SEG
        return s * (SEG + 3) + 2 + (tok - s * SEG)

    with ExitStack() as es:
        c = Ctx(nc, es)
        c16 = c.sb("c16", [128, NC16], BF16)
        c32 = c.sb("c32", [128, NC32], F32)
        flag = c.sb("flag", [128, 1], F32)
        pbs = [es.enter_context(nc.psum_tensor("pb%d" % i, [128, 512], F32)) for i in range(8)]

        def C16s(name, n=128):
            return c16[:, C16[name]:C16[name] + n]

        ident = C16s("ident")
        identf = c32[:, C32["identf"]:C32["identf"] + 128]

        with ExitStack() as pes:
            c.pes = pes
            c.dma(c16[:], cb16_in, w=["c16"])
            c.dma(c32[:], cf32_in, w=["c32"])
            c.dma(flag[:], flag_in, w=["flag"])
            for src, dst in [(w_in, wb_in), (w_gate, wb_gate), (w_out, wb_out), (w_fg, wb_fg),
                             (w_fu, wb_fu), (w_fd, wb_fd)]:
                for l in range(DEPTH):
                    s2 = src[l].rearrange("k c -> (k c)").rearrange("(r e) -> r e", e=1024)
                    d2 = dst[l].rearrange("k c -> (k c)").rearrange("(r e) -> r e", e=1024)
                    c.dma(d2, s2, w=["wcast"], eng="pool")
            for l in range(DEPTH):
                s2 = w_br[l].rearrange("b k c -> (b k c)").rearrange("(r e) -> r e", e=1024)
                d2 = wb_br[l].rearrange("b k c -> (b k c)").rearrange("(r e) -> r e", e=1024)
                c.dma(d2, s2, w=["wcast"], eng="pool")
            zt = c.sb("zt", [4, 1536], BF16)
            c.memset(zt[:], 0.0, w=["zt"])
            for s in range(2):
                c.dma(XB[s * (SEG + 3):s * (SEG + 3) + 2, :], zt[0:2, :], r=["zt"], w=["XBpad"])
                c.dma(XB[s * (SEG + 3) + 2 + SEG:s * (SEG + 3) + 3 + SEG, :], zt[0:1, :], r=["zt"], w=["XBpad"])
            c.emit()
            c.pes = None

        def bcast_load(t, src1d, key):
            c.dma(t, src1d.partition_broadcast(128), w=[key])

        def rmsnorm_rows(xt, key_x, gt, out, key_out, tmpf, ss, n=D):
            c.act(tmpf, xt, AF.Square, r=[key_x], w=["nrm_tmp", "nrm_ss"], accum_out=ss)
            c.ts(ss, ss, 1.0 / n, EPS, ALU.mult, ALU.add, r=["nrm_ss"], w=["nrm_ss"])
            c.act(ss, ss, AF.Sqrt, r=["nrm_ss"], w=["nrm_ss"])
            c.recip(ss, ss, r=["nrm_ss"], w=["nrm_ss"])
            c.stt(out, xt, ss, gt, ALU.mult, ALU.mult, r=[key_x, "nrm_ss", "ltab"], w=[key_out])

        wcount = [0]

        def dense(actT, akey, KT, wsrc, ncols, evac, fmajor=False, wt=None):
            wv = wsrc.rearrange("(kt p) c -> p kt c", p=128)
            ngrp = (ncols + 511) // 512
            for g in range(ngrp):
                c0 = g * 512
                cw = min(512, ncols - c0)
                wi = wcount[0] % 2
                wcount[0] += 1
                wtile = wt[wi]
                c.dma(wtile[:, 0:KT, 0:cw], wv[:, :, c0:c0 + cw], r=["wcast"], w=["wt%d" % wi])
                if not fmajor:
                    for t in range(4):
                        pi = (g * 4 + t) % 2
                        pb = pbs[pi]
                        for kt in range(KT):
                            c.mm(pb[:, 0:cw], actT[:, kt, t * 128:(t + 1) * 128], wtile[:, kt, 0:cw],
                                 start=(kt == 0), stop=(kt == KT - 1), r=[akey, "wt%d" % wi], w=["pb%d" % pi])
                        evac(c0, cw, t, pb[:, 0:cw], "pb%d" % pi)
                else:
                    for m in range(cw // 128):
                        pi = (g * 4 + m) % 2
                        pb = pbs[pi]
                        for kt in range(KT):
                            c.mm(pb[:, :], wtile[:, kt, m * 128:(m + 1) * 128], actT[:, kt, :],
                                 start=(kt == 0), stop=(kt == KT - 1), r=[akey, "wt%d" % wi], w=["pb%d" % pi])
                        evac(c0 + m * 128, 128, None, pb[:, :], "pb%d" % pi)

        def transpose_to(dstT, dkey, src, skey, ntile, t):
            pv = pbs[2][:].bitcast(BF16)
            for k0 in range(0, ntile, 8):
                n = min(8, ntile - k0)
                for k in range(n):
                    c.tr(pv[:, k * 128:(k + 1) * 128], src[:, (k0 + k) * 128:(k0 + k + 1) * 128], ident,
                         r=[skey, "c16"], w=["pb2"])
                c.cp(dstT[:, k0:k0 + n, t * 128:(t + 1) * 128],
                     pv[:, 0:n * 128].rearrange("p (k q) -> p k q", k=n), r=["pb2"], w=[dkey], eng="act")

        for l in range(DEPTH):
            xsrc = x_in if l == 0 else XS
            last = (l == DEPTH - 1)
            with ExitStack() as pes:
                c.pes = pes
                g_mix = c.sb("g_mix", [128, D])
                bg = c.sb("bg", [128, 3072])
                bcast_load(g_mix[:], vecs["norm_mix_g"][l], "ltab")
                bcast_load(bg[:], vecs["b_gate"][l], "ltab")
                xt = [c.sb("xt%d" % i, [128, D]) for i in range(2)]
                ut = c.sb("ut", [128, D], BF16)
                uT = c.sb("uT", [128, 8, 512], BF16)
                wt = [c.sb("wt%d" % i, [128, 8, 512], BF16) for i in range(2)]
                oev = [c.sb("oev%d" % i, [128, 512], BF16) for i in range(2)]
                oef = c.sb("oef", [128, 512], F32)
                og = c.sb("og", [128, 48], F32)
                rp = c.sb("rp", [128, 4, 256], F32)
                rt = [c.sb("rt%d" % i, [128, 4, 64], F32) for i in range(4)]
                tmpf = c.sb("tmpf", [128, D], F32)
                ss = c.sb("ss", [128, 1], F32)
                ecnt = [0]
                for b in range(NBLK):
                    r0 = b * 512
                    c.dma(rp[:], rope_in[r0:r0 + 512, :].rearrange("(t p) e -> p t e", p=128), w=["rp"])
                    for t in range(4):
                        xi = t % 2
                        c.dma(xt[xi][:], xsrc[r0 + t * 128:r0 + (t + 1) * 128, :], r=["XS"], w=["xt%d" % xi])
                        rmsnorm_rows(xt[xi][:], "xt%d" % xi, g_mix[:], ut[:], "ut", tmpf[:], ss[:])
                        transpose_to(uT, "uT", ut, "ut", 8, t)

                    def evacA(c0, cw, t, ps, pkey, r0=r0):
                        rows = slice(r0 + t * 128, r0 + (t + 1) * 128)
                        oi = ecnt[0] % 2
                        ecnt[0] += 1
                        o = oev[oi]
                        okey = "oev%d" % oi
                        if c0 == OFF["mq"]:
                            c.act(o[:], ps, AF.Copy, r=[pkey], w=[okey], scale=QS)
                        elif c0 in (OFF["rq"], OFF["rk"]):
                            co = rp[:, t, 0:64] if c0 == OFF["rq"] else rp[:, t, 128:192]
                            si = rp[:, t, 64:128] if c0 == OFF["rq"] else rp[:, t, 192:256]
                            cob = co.unsqueeze(1).to_broadcast([128, 4, 64])
                            sib = si.unsqueeze(1).to_broadcast([128, 4, 64])
                            p3 = ps.rearrange("p (h d) -> p h d", h=4)
                            o3 = o[:].rearrange("p (h d) -> p h d", h=4)
                            x1 = p3[:, :, 0:64]
                            x2 = p3[:, :, 64:128]
                            c.tt(rt[0][:], x1, cob, ALU.mult, r=[pkey, "rp"], w=["rt0"])
                            c.tt(rt[1][:], x2, sib, ALU.mult, r=[pkey, "rp"], w=["rt1"])
                            c.tt(rt[2][:], x1, sib, ALU.mult, r=[pkey, "rp"], w=["rt2"])
                            c.tt(rt[3][:], x2, cob, ALU.mult, r=[pkey, "rp"], w=["rt3"])
                            c.tt(o3[:, :, 0:64], rt[0][:], rt[1][:], ALU.subtract, r=["rt0", "rt1"], w=[okey], eng="pool")
                            c.tt(o3[:, :, 64:128], rt[2][:], rt[3][:], ALU.add, r=["rt2", "rt3"], w=[okey], eng="pool")
                        else:
                            c.cp(o[:], ps, r=[pkey], w=[okey], eng=("act" if (ecnt[0] % 2) else "dve"))
                        if OFF["xbc"] <= c0 < OFF["mq"]:
                            xr = xbrow(r0 + t * 128)
                            c.dma(XB[xr:xr + 128, c0 - OFF["xbc"]:c0 - OFF["xbc"] + cw], o[:, 0:cw], r=[okey], w=["XB"])
                        else:
                            c.dma(A_[rows, c0:c0 + cw], o[:, 0:cw], r=[okey], w=["A_"])

                    dense(uT, "uT", 8, wb_in[l][:, 0:A_COLS], A_COLS, evacA, wt=wt)

                    def evacB(c0, cw, t, ps, pkey, r0=r0):
                        rows = slice(r0 + t * 128, r0 + (t + 1) * 128)
                        c.cp(og[:], ps, r=[pkey], w=["og"])
                        c.dma(G_[rows, :], og[:], r=["og"], w=["G_"])

                    dense(uT, "uT", 8, wb_in[l][:, A_COLS:8752], 48, evacB, wt=wt)

                    def evacG(c0, cw, t, ps, pkey, r0=r0):
                        rows = slice(r0 + t * 128, r0 + (t + 1) * 128)
                        oi = ecnt[0] % 2
                        ecnt[0] += 1
                        o = oev[oi]
                        okey = "oev%d" % oi
                        c.tt(oef[:], ps, bg[:, c0:c0 + cw], ALU.add, r=[pkey, "ltab"], w=["oef"])
                        c.act(o[:], oef[:], AF.Sigmoid, r=["oef"], w=[okey])
                        c.dma(GT[rows, c0:c0 + cw], o[:], r=[okey], w=["GT"])

                    dense(uT, "uT", 8, wb_gate[l], 3072, evacG, wt=wt)
                hp = c.sb("hp", [4, 1536], BF16)
                hq = c.sb("hq", [4, 1536], BF16)
                c.dma(hp[0:2, :], XB[xbrow(SEG - 2):xbrow(SEG - 2) + 2, :], r=["XB"], w=["hp"])
                c.ts(hq[0:2, :], hp[0:2, :], flag[0:2, 0:1], None, ALU.mult, r=["hp", "flag"], w=["hq"])
                c.dma(XB[(SEG + 3):(SEG + 3) + 2, :], hq[0:2, :], r=["hq"], w=["XB"])
                hp2 = c.sb("hp2", [4, 1536], BF16)
                hq2 = c.sb("hq2", [4, 1536], BF16)
                c.dma(hp2[0:1, :], XB[xbrow(SEG):xbrow(SEG) + 1, :], r=["XB"], w=["hp2"])
                c.ts(hq2[0:1, :], hp2[0:1, :], flag[0:1, 0:1], None, ALU.mult, r=["hp2", "flag"], w=["hq2"])
                c.dma(XB[2 + SEG:3 + SEG, :], hq2[0:1, :], r=["hq2"], w=["XB"])
                c.emit()
                c.pes = None

            for mode in ("S", "M"):
                with ExitStack() as pes:
                    c.pes = pes
                    isM = (mode == "M")
                    cwt = c.sb("cwt", [128, 4, 1536], F32)
                    cbt = c.sb("cbt", [128, 1536], F32)
                    dtb = c.sb("dtb", [128, 32], F32)
                    aneg = c.sb("aneg", [128, 32], F32)
                    mgb = c.sb("mgb", [128, 16], F32)
                    bcast_load(cwt[:].rearrange("p t c -> p (t c)"), vecs["conv_w"][l], "ltab")
                    bcast_load(cbt[:], vecs["conv_b"][l], "ltab")
                    bcast_load(dtb[:], vecs["ssd_dt_bias"][l], "ltab")
                    bcast_load(aneg[:], vecs["ssd_a_log"][l], "ltab")
                    bcast_load(mgb[:], vecs["mlstm_gate_bias"][l], "ltab")
                    c.act(aneg[:], aneg[:], AF.Exp, r=["ltab"], w=["ltab"])
                    c.ts(aneg[:], aneg[:], -1.0, None, ALU.mult, r=["ltab"], w=["ltab"])
                    if isM:
                        dsk = c.sb("dsk", [128, 16], F32)
                        nga = c.sb("nga", [128, D], F32)
                        ngb = c.sb("ngb", [128, D], F32)
                        ngc = c.sb("ngc", [128, D], F32)
                        bcast_load(dsk[:], vecs["ssd_d"][l], "ltab")
                        bcast_load(nga[:], vecs["ssd_norm_g"][l], "ltab")
                        bcast_load(ngb[:], vecs["mlstm_norm_g"][l], "ltab")
                        bcast_load(ngc[:], vecs["ret_norm_g"][l], "ltab")
                    xb4 = [c.sb("xb4_%d" % i, [128, 1536], BF16) for i in range(4)]
                    cacc = c.sb("cacc", [128, 1536], F32)
                    ctmp = c.sb("ctmp", [128, 1536], F32)
                    xact = c.sb("xact", [128, 1536], BF16)
                    gt = c.sb("gt", [128, 48], F32)
                    sm = {n: c.sb("sm_" + n, [128, 40], F32) for n in
                          ("dtp", "dt", "ld", "li", "csx", "tot", "bw", "wend", "ecs", "dec", "gml", "sp")}
                    ldb = c.sb("ldb", [128, 40], BF16)
                    st = c.sb("st", [128, 3076], F32)
                    stb = c.sb("stb", [128, 3076], BF16)
                    vaug = c.sb("vaug", [128, 4, 257], BF16)
                    xw = c.sb("xw", [128, D], BF16)
                    kw = c.sb("kw", [128, 128], BF16)
                    c.memset(vaug[:], 1.0, w=["vaug"])
                    c.memset(st[:], 0.0, w=["st"])
                    c.memset(stb[:], 0.0, w=["stb"])
                    if isM:
                        at = c.sb("at", [128, A_COLS], BF16)
                        sbl = c.sb("sbl", [128, 3076], BF16)
                        R = c.sb("R", [128, 40, 128], BF16)
                        W = c.sb("W", [128, 40, 128], BF16)
                        Mb = c.sb("Mb", [128, 16, 128], BF16)
                        bcT = c.sb("bcT", [128, 4, 128], BF16)
                        qkT = c.sb("qkT", [128, 16, 128], BF16)
                        f1 = c.sb("f1", [128, D], F32)
                        f2 = c.sb("f2", [128, D], F32)
                        f3 = c.sb("f3", [128, D], F32)
                        Pm = c.sb("Pm", [128, 8, 128], BF16)
                        nd = c.sb("nd", [128, 257], F32)
                        rr = c.sb("rr", [128, 1], F32)
                        hacc = c.sb("hacc", [128, 4, 256], F32)
                        Pr = c.sb("Pr", [128, 4, 128], BF16)
                        qxf = c.sb("qxf", [128, 4, 128], BF16)
                        qxb = c.sb("qxb", [128, 4, 128], BF16)
                        mot = c.sb("mot", [128, 3072], BF16)
                        ss4 = c.sb("ss4", [128, 4], F32)
                        ss1 = c.sb("ss1", [128, 1], F32)
                    else:
                        kv = c.sb("kv", [128, 3072], BF16)

                    order = range(NCH) if isM else range(NCH - 1, -1, -1)
                    fo = 0 if isM else 20
                    for ch in order:
                        r0 = ch * 128
                        xr = xbrow(r0)
                        for t in range(4):
                            c.dma(xb4[t][:], XB[xr + t - 2:xr + t - 2 + 128, :], r=["XB"], w=["xb4_%d" % t])
                        c.dma(gt[:], G_[r0:r0 + 128, :], r=["G_"], w=["gt"])
                        if isM:
                            c.dma(at[:], A_[r0:r0 + 128, :], r=["A_"], w=["at"])
                            c.dma(sbl[:], SB[ch], r=["SB"], w=["sbl"])
                            mk = at[:, OFF["mk"]:OFF["mk"] + 512]
                            mv = at[:, OFF["mv"]:OFF["mv"] + 1024]
                            rk = at[:, OFF["rk"]:OFF["rk"] + 512]
                            rv = at[:, OFF["rv"]:OFF["rv"] + 1024]
                            kvkey = "at"
                        else:
                            c.dma(kv[:, 0:1536], A_[r0:r0 + 128, OFF["mk"]:OFF["mk"] + 1536], r=["A_"], w=["kv"])
                            c.dma(kv[:, 1536:3072], A_[r0:r0 + 128, OFF["rk"]:OFF["rk"] + 1536], r=["A_"], w=["kv"])
                            mk = kv[:, 0:512]
                            mv = kv[:, 512:1536]
                            rk = kv[:, 1536:2048]
                            rv = kv[:, 2048:3072]
                            kvkey = "kv"
                            c.dma(SB[ch], stb[:], r=["stb"], w=["SB"])
                        c.tt(cacc[:], xb4[0][:], cwt[:, 0, :], ALU.mult, r=["xb4_0", "ltab"], w=["cacc"], eng="pool")
                        for t in range(1, 4):
                            c.tt(ctmp[:], xb4[t][:], cwt[:, t, :], ALU.mult, r=["xb4_%d" % t, "ltab"], w=["ctmp"], eng="pool")
                            c.tt(cacc[:], cacc[:], ctmp[:], ALU.add, r=["cacc", "ctmp"], w=["cacc"])
                        c.tt(cacc[:], cacc[:], cbt[:], ALU.add, r=["cacc", "ltab"], w=["cacc"])
                        c.act(xact[:], cacc[:], AF.Silu, r=["cacc"], w=["xact"])
                        xs = xact[:, 0:1024]
                        Bm = xact[:, 1024:1280]
                        Cm = xact[:, 1280:1536]
                        S = sm
                        c.tt(S["dtp"][:, 0:32], gt[:, 0:32], dtb[:], ALU.add, r=["gt", "ltab"], w=["dtp"])
                        c.act(S["dtp"][:, 0:32], S["dtp"][:, 0:32], AF.Exp, r=["dtp"], w=["dtp"])
                        c.act(S["dt"][:, 0:32], S["dtp"][:, 0:32], AF.Ln, r=["dtp"], w=["dt"], bias=1.0)
                        c.tt(S["gml"][:, 0:16], gt[:, 32:48], mgb[:], ALU.add, r=["gt", "ltab"], w=["gml"])
                        c.act(S["sp"][:, 0:8], S["gml"][:, 0:8], AF.Exp, r=["gml"], w=["sp"], scale=-1.0)
                        c.act(S["sp"][:, 0:8], S["sp"][:, 0:8], AF.Ln, r=["sp"], w=["sp"], bias=1.0)
                        c.act(S["li"][:, 0:16], S["dt"][:, 0:16], AF.Ln, r=["dt"], w=["li"])
                        c.act(S["li"][:, 20:36], S["dt"][:, 16:32], AF.Ln, r=["dt"], w=["li"])
                        c.cp(S["li"][:, 16:20], S["gml"][:, 8:12], r=["gml"], w=["li"])
                        c.cp(S["li"][:, 36:40], S["gml"][:, 12:16], r=["gml"], w=["li"])
                        c.tt(S["ld"][:, 0:16], S["dt"][:, 0:16], aneg[:, 0:16], ALU.mult, r=["dt", "ltab"], w=["ld"])
                        c.tt(S["ld"][:, 20:36], S["dt"][:, 16:32], aneg[:, 16:32], ALU.mult, r=["dt", "ltab"], w=["ld"])
                        c.ts(S["ld"][:, 16:20], S["sp"][:, 0:4], -1.0, None, ALU.mult, r=["sp"], w=["ld"])
                        c.ts(S["ld"][:, 36:40], S["sp"][:, 4:8], -1.0, None, ALU.mult, r=["sp"], w=["ld"])
                        c.cp(ldb[:], S["ld"][:], r=["ld"], w=["ldb"])
                        pc = pbs[3]
                        c.mm(pc[:, 0:40], C16s("tri"), ldb[:], r=["c16", "ldb"], w=["pb3"])
                        c.mm(pc[:, 40:80], C16s("triT"), ldb[:], r=["c16", "ldb"], w=["pb3"])
                        c.mm(pc[:, 80:120], C16s("ones"), ldb[:], r=["c16", "ldb"], w=["pb3"])
                        c.cp(S["csx"][:, 0:20], pc[:, 0:20], r=["pb3"], w=["csx"])
                        c.cp(S["csx"][:, 20:40], pc[:, 60:80], r=["pb3"], w=["csx"])
                        c.cp(S["tot"][:], pc[:, 80:120], r=["pb3"], w=["tot"], eng="act")
                        c.tt(S["bw"][:], S["li"][:], S["csx"][:], ALU.subtract, r=["li", "csx"], w=["bw"])
                        c.tt(S["wend"][:], S["bw"][:], S["tot"][:], ALU.add, r=["bw", "tot"], w=["wend"])
                        c.act(S["wend"][:], S["wend"][:], AF.Exp, r=["wend"], w=["wend"])
                        c.act(S["ecs"][:], S["csx"][:], AF.Exp, r=["csx"], w=["ecs"])
                        c.act(S["dec"][:], S["tot"][:], AF.Exp, r=["tot"], w=["dec"])

                        if isM:
                            c.tt(R[:, 0:20, :], ldb[:, 0:20].unsqueeze(2).to_broadcast([128, 20, 128]),
                                 C16s("tri").unsqueeze(1).to_broadcast([128, 20, 128]), ALU.mult,
                                 r=["ldb", "c16"], w=["R"], eng="pool")
                            c.tt(R[:, 20:40, :], ldb[:, 20:40].unsqueeze(2).to_broadcast([128, 20, 128]),
                                 C16s("triT").unsqueeze(1).to_broadcast([128, 20, 128]), ALU.mult,
                                 r=["ldb", "c16"], w=["R"], eng="pool")
                            for g in range(10):
                                pi = g % 2
                                pw = pbs[pi]
                                nm = C16s("nmf", 512) if g < 5 else C16s("nmb", 512)
                                c.mm(pw[:, :], C16s("ones"), R[:, 4 * g:4 * g + 4, :].rearrange("p a b -> p (a b)"),
                                     start=True, stop=False, r=["c16", "R"], w=["pb%d" % pi])
                                c.mm(pw[:, :], ident, nm, start=False, stop=True, r=["c16"], w=["pb%d" % pi])
                                for j in range(4):
                                    cc = 4 * g + j
                                    c.act(W[:, cc, :], pw[:, j * 128:(j + 1) * 128], AF.Exp,
                                          r=["pb%d" % pi, "bw"], w=["W"], bias=S["bw"][:, cc:cc + 1])
                            pv = pbs[2][:].bitcast(BF16)
                            for k in range(4):
                                src = xact[:, 1024 + k * 128:1024 + (k + 1) * 128]
                                c.tr(pv[:, k * 128:(k + 1) * 128], src, ident, r=["xact", "c16"], w=["pb2"])
                            c.cp(bcT[:], pv[:, 0:512].rearrange("p (k q) -> p k q", k=4), r=["pb2"], w=["bcT"], eng="act")
                            for grp, base in enumerate([OFF["mq"], OFF["rq"]]):
                                for k in range(8):
                                    src = at[:, base + k * 128:base + (k + 1) * 128]
                                    c.tr(pv[:, k * 128:(k + 1) * 128], src, ident, r=["at", "c16"], w=["pb2"])
                                c.cp(qkT[:, grp * 8:grp * 8 + 8, :], pv[:, :].rearrange("p (k q) -> p k q", k=8),
                                     r=["pb2"], w=["qkT"], eng="act")
                            pg = pbs[3]
                            for g in range(2):
                                c.mm(pg[:, 128 + g * 128:256 + g * 128], bcT[:, g, :], bcT[:, 2 + g, :],
                                     r=["bcT"], w=["pb3g"])
                            c.tt(W[:, 0:16, :], W[:, 0:16, :], W[:, 20:36, :], ALU.add, r=["W"], w=["W"], eng="pool")
                            for g in range(2):
                                c.tt(Mb[:, 8 * g:8 * g + 8, :], W[:, 8 * g:8 * g + 8, :],
                                     pg[:, 128 + g * 128:256 + g * 128].unsqueeze(1).to_broadcast([128, 8, 128]),
                                     ALU.mult, r=["W", "pb3g"], w=["Mb"])
                            for h in range(16):
                                pyb = pbs[4 + h // 8]
                                c.mm(pyb[:, (h % 8) * 64:(h % 8 + 1) * 64], Mb[:, h, :], xact[:, h * 64:(h + 1) * 64],
                                     r=["Mb", "xact"], w=["pb%d" % (4 + h // 8)])
                            for g in range(2):
                                c.mm(pbs[6 + g][:, :], bcT[:, 2 + g, :], stb[:, g * 512:(g + 1) * 512],
                                     r=["bcT", "stb"], w=["pb%d" % (6 + g)])
                            f1v = f1[:].rearrange("p (h d) -> p h d", h=16)
                            f2v = f2[:].rearrange("p (h d) -> p h d", h=16)
                            for g in range(2):
                                c.tt(f1v[:, 8 * g:8 * g + 8, :], pbs[6 + g][:, :].rearrange("p (h d) -> p h d", h=8),
                                     S["ecs"][:, 8 * g:8 * g + 8].unsqueeze(2).to_broadcast([128, 8, 64]), ALU.mult,
                                     r=["pb%d" % (6 + g), "ecs"], w=["f1"])
                            for g in range(2):
                                c.mm(pbs[6 + g][:, :], bcT[:, 2 + g, :], sbl[:, g * 512:(g + 1) * 512],
                                     r=["bcT", "sbl"], w=["pb%d" % (6 + g)])
                            for g in range(2):
                                c.tt(f2v[:, 8 * g:8 * g + 8, :], pbs[6 + g][:, :].rearrange("p (h d) -> p h d", h=8),
                                     S["ecs"][:, 20 + 8 * g:20 + 8 * g + 8].unsqueeze(2).to_broadcast([128, 8, 64]), ALU.mult,
                                     r=["pb%d" % (6 + g), "ecs"], w=["f2"])
                            c.tt(f1[:], f1[:], f2[:], ALU.add, r=["f1", "f2"], w=["f1"], eng="pool")
                            for g in range(2):
                                c.tt(f1[:, g * 512:(g + 1) * 512], f1[:, g * 512:(g + 1) * 512], pbs[4 + g][:, :], ALU.add,
                                     r=["f1", "pb%d" % (4 + g)], w=["f1"])
                            c.tt(f2v, xact[:, 0:1024].rearrange("p (h d) -> p h d", h=16),
                                 dsk[:].unsqueeze(2).to_broadcast([128, 16, 64]), ALU.mult, r=["xact", "ltab"], w=["f2"], eng="pool")
                            c.tt(f1[:], f1[:], f2[:], ALU.add, r=["f1", "f2"], w=["f1"])
                            c.act(f2[:], at[:, 0:1024], AF.Silu, r=["at"], w=["f2"])
                            c.tt(f1[:], f1[:], f2[:], ALU.mult, r=["f1", "f2"], w=["f1"])
                            c.act(f3[:], f1[:], AF.Square, r=["f1"], w=["f3", "ss1"], accum_out=ss1[:])
                            c.ts(ss1[:], ss1[:], 1.0 / D, EPS, ALU.mult, ALU.add, r=["ss1"], w=["ss1"])
                            c.act(ss1[:], ss1[:], AF.Sqrt, r=["ss1"], w=["ss1"])
                            c.recip(ss1[:], ss1[:], r=["ss1"], w=["ss1"])
                            c.stt(mot[:, 0:1024], f1[:], ss1[:], nga[:], ALU.mult, ALU.mult, r=["f1", "ss1", "ltab"], w=["mot"])

                        c.tt(xw[:].rearrange("p (h d) -> p h d", h=16), xact[:, 0:1024].rearrange("p (h d) -> p h d", h=16),
                             S["wend"][:, fo:fo + 16].unsqueeze(2).to_broadcast([128, 16, 64]), ALU.mult,
                             r=["xact", "wend"], w=["xw"], eng="pool")
                        for g in range(2):
                            c.mm(pbs[6 + g][:, :], xact[:, 1024 + g * 128:1024 + (g + 1) * 128], xw[:, g * 512:(g + 1) * 512],
                                 r=["xact", "xw"], w=["pb%d" % (6 + g)])
                        c.tt(st[:, 0:1024].rearrange("p (h d) -> p h d", h=16), st[:, 0:1024].rearrange("p (h d) -> p h d", h=16),
                             S["dec"][:, fo:fo + 16].unsqueeze(2).to_broadcast([128, 16, 64]), ALU.mult,
                             r=["st", "dec"], w=["st"])
                        for g in range(2):
                            c.tt(st[:, g * 512:(g + 1) * 512], st[:, g * 512:(g + 1) * 512], pbs[6 + g][:, :], ALU.add,
                                 r=["st", "pb%d" % (6 + g)], w=["st"])

                        c.cp(vaug[:, :, 0:256], mv.rearrange("p (h v) -> p h v", h=4), r=[kvkey], w=["vaug"], eng="pool")
                        if isM:
                            for h in range(4):
                                c.mm(pbs[0][:, h * 128:(h + 1) * 128], qkT[:, 4 + h, :], qkT[:, h, :], r=["qkT"], w=["pb0"])
                            p03 = pbs[0][:, :].rearrange("p (h l) -> p h l", h=4)
                            c.tt(Pm[:, 0:4, :], p03, W[:, 16:20, :], ALU.mult, r=["pb0", "W"], w=["Pm"])
                            c.tt(Pm[:, 4:8, :], p03, W[:, 36:40, :], ALU.mult, r=["pb0", "W"], w=["Pm"])
                            for h in range(4):
                                for d_ in range(2):
                                    cc = 16 + h if d_ == 0 else 36 + h
                                    pvb = pbs[4 + d_]
                                    pxb = pbs[6 + d_]
                                    stsrc = stb if d_ == 0 else sbl
                                    skey = "stb" if d_ == 0 else "sbl"
                                    c.mm(pvb[:, 0:257], Pm[:, 4 * d_ + h, :], vaug[:, h, :], r=["Pm", "vaug"], w=["pb%d" % (4 + d_)])
                                    c.mm(pxb[:, 0:257], qkT[:, h, :], stsrc[:, 1024 + h * 257:1024 + (h + 1) * 257],
                                         r=["qkT", skey], w=["pb%d" % (6 + d_)])
                                    c.ts(nd[:], pxb[:, 0:257], S["ecs"][:, cc:cc + 1], None, ALU.mult,
                                         r=["pb%d" % (6 + d_), "ecs"], w=["nd"])
                                    c.tt(nd[:], nd[:], pvb[:, 0:257], ALU.add, r=["nd", "pb%d" % (4 + d_)], w=["nd"])
                                    c.act(rr[:], nd[:, 256:257], AF.Abs, r=["nd"], w=["rr"])
                                    c.ts(rr[:], rr[:], 1.0, None, ALU.max, r=["rr"], w=["rr"])
                                    c.recip(rr[:], rr[:], r=["rr"], w=["rr"])
                                    if d_ == 0:
                                        c.ts(hacc[:, h, :], nd[:, 0:256], rr[:], None, ALU.mult, r=["nd", "rr"], w=["hacc"])
                                    else:
                                        c.stt(hacc[:, h, :], nd[:, 0:256], rr[:], hacc[:, h, :], ALU.mult, ALU.add,
                                              r=["nd", "rr", "hacc"], w=["hacc"])
                            for h in range(4):
                                c.act(f3[:, 0:256], hacc[:, h, :], AF.Square, r=["hacc"], w=["f3", "ss4"], accum_out=ss4[:, h:h + 1])
                            c.ts(ss4[:], ss4[:], 1.0 / 256, EPS, ALU.mult, ALU.add, r=["ss4"], w=["ss4"])
                            c.act(ss4[:], ss4[:], AF.Sqrt, r=["ss4"], w=["ss4"])
                            c.recip(ss4[:], ss4[:], r=["ss4"], w=["ss4"])
                            f2h = f2[:].rearrange("p (h v) -> p h v", h=4)
                            c.tt(f2h, hacc[:], ss4[:].unsqueeze(2).to_broadcast([128, 4, 256]), ALU.mult, r=["hacc", "ss4"], w=["f2"])
                            c.tt(f2[:], f2[:], ngb[:], ALU.mult, r=["f2", "ltab"], w=["f2"], eng="pool")
                            c.act(f3[:], at[:, OFF["mo"]:OFF["mo"] + 1024], AF.Sigmoid, r=["at"], w=["f3"])
                            c.tt(mot[:, 1024:2048], f2[:], f3[:], ALU.mult, r=["f2", "f3"], w=["mot"])
                        mo_ = 16 + fo
                        for h in range(4):
                            c.ts(kw[:], mk[:, h * 128:(h + 1) * 128], S["wend"][:, mo_ + h:mo_ + h + 1], None, ALU.mult,
                                 r=[kvkey, "wend"], w=["kw"])
                            pi = 6 + (h % 2)
                            c.mm(pbs[pi][:, 0:257], kw[:], vaug[:, h, :], r=["kw", "vaug"], w=["pb%d" % pi])
                            sl = st[:, 1024 + h * 257:1024 + (h + 1) * 257]
                            c.stt(sl, sl, S["dec"][:, mo_ + h:mo_ + h + 1], pbs[pi][:, 0:257], ALU.mult, ALU.add,
                                  r=["st", "dec", "pb%d" % pi], w=["st"])

                        if isM:
                            for h in range(4):
                                c.mm(pbs[1][:, h * 128:(h + 1) * 128], qkT[:, 12 + h, :], qkT[:, 8 + h, :], r=["qkT"], w=["pb1"])
                            c.tt(Pr[:].rearrange("p h l -> p (h l)"), pbs[1][:, :], c32[:, C32["dtab"]:C32["dtab"] + 512], ALU.mult,
                                 r=["pb1", "c32"], w=["Pr"])
                            c.tt(qxf[:].rearrange("p h l -> p (h l)"), qkT[:, 8:12, :].rearrange("p h l -> p (h l)"),
                                 c32[:, C32["xif"]:C32["xif"] + 512], ALU.mult, r=["qkT", "c32"], w=["qxf"], eng="pool")
                            c.tt(qxb[:].rearrange("p h l -> p (h l)"), qkT[:, 8:12, :].rearrange("p h l -> p (h l)"),
                                 c32[:, C32["xib"]:C32["xib"] + 512], ALU.mult, r=["qkT", "c32"], w=["qxb"], eng="pool")
                            for h in range(4):
                                pvb = pbs[4 + h % 2]
                                pk = "pb%d" % (4 + h % 2)
                                c.mm(pvb[:, 0:256], Pr[:, h, :], rv[:, h * 256:(h + 1) * 256], start=True, stop=False,
                                     r=["Pr", "at"], w=[pk])
                                c.mm(pvb[:, 0:256], qxf[:, h, :], stb[:, 2052 + h * 256:2052 + (h + 1) * 256], start=False, stop=False,
                                     r=["qxf", "stb"], w=[pk])
                                c.mm(pvb[:, 0:256], qxb[:, h, :], sbl[:, 2052 + h * 256:2052 + (h + 1) * 256], start=False, stop=True,
                                     r=["qxb", "sbl"], w=[pk])
                                c.cp(f1[:, h * 256:(h + 1) * 256], pvb[:, 0:256], r=[pk], w=["f1"], eng="act")
                                c.act(f3[:, 0:256], f1[:, h * 256:(h + 1) * 256], AF.Square, r=["f1"], w=["f3", "ss4"],
                                      accum_out=ss4[:, h:h + 1])
                            c.ts(ss4[:], ss4[:], 1.0 / 256, EPS, ALU.mult, ALU.add, r=["ss4"], w=["ss4"])
                            c.act(ss4[:], ss4[:], AF.Sqrt, r=["ss4"], w=["ss4"])
                            c.recip(ss4[:], ss4[:], r=["ss4"], w=["ss4"])
                            f2h = f2[:].rearrange("p (h v) -> p h v", h=4)
                            c.tt(f2h, f1[:].rearrange("p (h v) -> p h v", h=4), ss4[:].unsqueeze(2).to_broadcast([128, 4, 256]),
                                 ALU.mult, r=["f1", "ss4"], w=["f2"])
                            c.tt(f2[:], f2[:], ngc[:], ALU.mult, r=["f2", "ltab"], w=["f2"], eng="pool")
                            c.act(f3[:], at[:, OFF["rg"]:OFF["rg"] + 1024], AF.Silu, r=["at"], w=["f3"])
                            c.tt(mot[:, 2048:3072], f2[:], f3[:], ALU.mult, r=["f2", "f3"], w=["mot"])
                            c.dma(MO[r0:r0 + 128, :], mot[:], r=["mot"], w=["MO"])
                        zo = C32["zf"] if isM else C32["zb"]
                        for h in range(4):
                            c.ts(kw[:], rk[:, h * 128:(h + 1) * 128], c32[:, zo + h:zo + h + 1], None, ALU.mult,
                                 r=[kvkey, "c32"], w=["kw"])
                            pi = 6 + (h % 2)
                            c.mm(pbs[pi][:, 0:256], kw[:], rv[:, h * 256:(h + 1) * 256], r=["kw", kvkey], w=["pb%d" % pi])
                            sl = st[:, 2052 + h * 256:2052 + (h + 1) * 256]
                            c.stt(sl, sl, float(math.exp(128 * LOGG[h])), pbs[pi][:, 0:256], ALU.mult, ALU.add,
                                  r=["st", "pb%d" % pi], w=["st"])
                        if (isM and ch == NCHS - 1) or ((not isM) and ch == NCHS):
                            c.ts(st[:], st[:], flag[:, 0:1], None, ALU.mult, r=["st", "flag"], w=["st"])
                        c.cp(stb[:], st[:], r=["st"], w=["stb"], eng="act")
                    c.emit()
                    c.pes = None

            with ExitStack() as pes:
                c.pes = pes
                nfg = c.sb("nfg", [128, D])
                fng = c.sb("fng", [128, D])
                bcast_load(nfg[:], vecs["norm_ffn_g"][l], "ltab")
                bcast_load(fng[:], fin_g, "ltab")
                wt = [c.sb("wtd%d" % i, [128, 22, 512], BF16) for i in range(2)]
                mo4 = c.sb("mo4", [128, 4, 3072], BF16)
                gt4 = c.sb("gt4", [128, 4, 3072], BF16)
                x4 = c.sb("x4", [128, 4, D], F32)
                brT = c.sb("brT", [128, 8, 512], BF16)
                mrg = c.sb("mrg", [128, 4, D], F32)
                tmpd = c.sb("tmpd", [128, 512], F32)
                mrb = c.sb("mrb", [128, D], BF16)
                hT = c.sb("hT", [128, 8, 512], BF16)
                actT = c.sb("actT", [128, 22, 512], BF16)
                tmpf = c.sb("tmpfd", [128, D], F32)
                ss = c.sb("ssd_", [128, 1], F32)
                yo = c.sb("yo", [128, D], F32)
                for b in range(NBLK):
                    r0 = b * 512
                    c.dma(mo4[:], MO[r0:r0 + 512, :].rearrange("(t p) e -> p t e", p=128), r=["MO"], w=["mo4"])
                    c.dma(gt4[:], GT[r0:r0 + 512, :].rearrange("(t p) e -> p t e", p=128), r=["GT"], w=["gt4"])
                    c.dma(x4[:], xsrc[r0:r0 + 512, :].rearrange("(t p) e -> p t e", p=128), r=["XS"], w=["x4"])
                    for br in range(3):
                        for t in range(4):
                            transpose_to(brT, "brT", mo4[:, t, br * 1024:(br + 1) * 1024], "mo4", 8, t)

                        def evacBr(c0, cw, t, ps, pkey, br=br):
                            gsl = gt4[:, t, br * 1024 + c0:br * 1024 + c0 + cw]
                            if br == 0:
                                c.tt(mrg[:, t, c0:c0 + cw], ps, gsl, ALU.mult, r=[pkey, "gt4"], w=["mrg"])
                            else:
                                c.tt(tmpd[:, 0:cw], ps, gsl, ALU.mult, r=[pkey, "gt4"], w=["tmpd"])
                                c.tt(mrg[:, t, c0:c0 + cw], mrg[:, t, c0:c0 + cw], tmpd[:, 0:cw], ALU.add,
                                     r=["mrg", "tmpd"], w=["mrg"], eng="pool")

                        dense(brT, "brT", 8, wb_br[l][br], D, evacBr, wt=wt)
                    for t in range(4):
                        c.cp(mrb[:], mrg[:, t, :], r=["mrg"], w=["mrb"], eng="act")
                        transpose_to(brT, "brT", mrb, "mrb", 8, t)

                    def evacO(c0, cw, t, ps, pkey):
                        c.tt(x4[:, t, c0:c0 + cw], x4[:, t, c0:c0 + cw], ps, ALU.add, r=["x4", pkey], w=["x4"])

                    dense(brT, "brT", 8, wb_out[l], D, evacO, wt=wt)
                    for t in range(4):
                        rmsnorm_rows(x4[:, t, :], "x4", nfg[:], mrb[:], "mrb", tmpf[:], ss[:])
                        transpose_to(hT, "hT", mrb, "mrb", 8, t)

                    def evacFg(c0, cw, t, ps, pkey):
                        c.act(actT[:, c0 // 128, :], ps, AF.Silu, r=[pkey], w=["actT"])

                    dense(hT, "hT", 8, wb_fg[l], FF, evacFg, fmajor=True, wt=wt)

                    def evacFu(c0, cw, t, ps, pkey):
                        c.tt(actT[:, c0 // 128, :], ps, actT[:, c0 // 128, :], ALU.mult, r=[pkey, "actT"], w=["actT"])

                    dense(hT, "hT", 8, wb_fu[l], FF, evacFu, fmajor=True, wt=wt)

                    def evacFd(c0, cw, t, ps, pkey):
                        c.tt(x4[:, t, c0:c0 + cw], x4[:, t, c0:c0 + cw], ps, ALU.add, r=["x4", pkey], w=["x4"])

                    dense(actT, "actT", 22, wb_fd[l], D, evacFd, wt=wt)
                    if last:
                        for t in range(4):
                            rmsnorm_rows(x4[:, t, :], "x4", fng[:], yo[:], "yo", tmpf[:], ss[:])
                            c.dma(y_out[r0 + t * 128:r0 + (t + 1) * 128, :], yo[:], r=["yo"], w=["y"])
                    else:
                        c.dma(XS[r0:r0 + 512, :].rearrange("(t p) e -> p t e", p=128), x4[:], r=["x4"], w=["XS"])
                c.emit()
                c.pes = None
    return nc


_PERM = None


def _perm():
    global _PERM
    if _PERM is None:
        o = dict(z=0, xbc=1024, dt=2560, mq=2592, mk=3104, mv=3616, mo=4640, mg=5664, rq=5680, rk=6192, rv=6704, rg=7728)
        n = dict(z=1024, xbc=1536, dt=32, mq=512, mk=512, mv=1024, mo=1024, mg=16, rq=512, rk=512, rv=1024, rg=1024)
        idx = []
        for k in ("z", "xbc", "mq", "mk", "mv", "mo", "rq", "rk", "rv", "rg", "dt"):
            idx.extend(range(o[k], o[k] + n[k]))
        mgp = [4, 5, 6, 7, 12, 13, 14, 15, 0, 1, 2, 3, 8, 9, 10, 11]
        idx.extend([o["mg"] + i for i in mgp])
        _PERM = (np.array(idx), np.array(mgp))
    return _PERM


def run_cores(core_x, core_flag, core_pos, weights, DEPTH, SEG, debug=False):
    perm, mgp = _perm()
    cb16, cf32 = host_consts()
    nc = build(DEPTH, SEG, debug=debug)
    w = weights
    shared = {
        "cb16": cb16, "cf32": cf32,
        "w_in": np.ascontiguousarray(w["w_in"][:, :, perm]),
        "w_gate": w["w_gate"], "w_branch": w["w_branch"], "w_out": w["w_out"],
        "w_ffn_gate": w["w_ffn_gate"], "w_ffn_up": w["w_ffn_up"], "w_ffn_down": w["w_ffn_down"],
        "norm_mix_g": w["norm_mix_g"], "b_gate": w["b_gate"],
        "conv_w": np.ascontiguousarray(w["conv_w"].reshape(DEPTH, -1)), "conv_b": w["conv_b"],
        "ssd_a_log": np.ascontiguousarray(w["ssd_a_log"].reshape(DEPTH, -1)),
        "ssd_dt_bias": np.ascontiguousarray(w["ssd_dt_bias"].reshape(DEPTH, -1)),
        "ssd_d": w["ssd_d"], "ssd_norm_g": w["ssd_norm_g"],
        "mlstm_gate_bias": np.ascontiguousarray(w["mlstm_gate_bias"].reshape(DEPTH, -1)[:, mgp]),
        "mlstm_norm_g": w["mlstm_norm_g"], "ret_norm_g": w["ret_norm_g"], "norm_ffn_g": w["norm_ffn_g"],
        "final_norm_g": w["final_norm_g"],
    }
    shared = {k: np.ascontiguousarray(np.asarray(v)) for k, v in shared.items()}
    in_maps = []
    for i in range(len(core_x)):
        m = dict(shared)
        m["x"] = np.ascontiguousarray(core_x[i], dtype=np.float32)
        m["flag"] = np.full((128, 1), core_flag[i], np.float32)
        m["rope"] = rope_tables(SEG, core_pos[i])
        in_maps.append(m)
    res = run_bass_kernel_spmd(nc, in_maps, core_ids=list(range(len(core_x))))
    return res.results


def kernel(**inputs):
    xp = np.asarray(inputs["x_prompt"], dtype=np.float32)
    xsm = np.asarray(inputs["x_sample"], dtype=np.float32)
    SEG = 4096
    DEPTH = 4
    core_x, core_flag, core_pos = [], [], []
    for i in range(4):
        core_x.append(xsm[i])
        core_flag.append(1.0)
        core_pos.append(np.arange(8192))
    for i in range(2):
        core_x.append(np.concatenate([xp[2 * i], xp[2 * i + 1]], axis=0))
        core_flag.append(0.0)
        core_pos.append(np.concatenate([np.arange(4096), np.arange(4096)]))
    for i in range(2):
        core_x.append(core_x[4 + i])
        core_flag.append(0.0)
        core_pos.append(core_pos[4 + i])
    weights = {k: np.asarray(v) for k, v in inputs.items() if k not in ("x_prompt", "x_sample")}
    res = run_cores(core_x, core_flag, core_pos, weights, DEPTH, SEG)
    y_sample = np.stack([res[i]["y"] for i in range(4)], axis=0).astype(np.float32)
    yp = []
    for i in range(2):
        yy = res[4 + i]["y"]
        yp.append(yy[0:4096])
        yp.append(yy[4096:8192])
    y_prompt = np.stack(yp, axis=0).astype(np.float32)
    return (y_prompt, y_sample)
```

```python
import math
import numpy as np
import ml_dtypes
from contextlib import ExitStack
import concourse.bass as bass
import concourse.mybir as mybir
from concourse.bass_utils import run_bass_kernel_spmd

F32 = mybir.dt.float32
BF16 = mybir.dt.bfloat16
AF = mybir.ActivationFunctionType
ALU = mybir.AluOpType

ENGS = ("pe", "act", "dve", "pool", "sp")
NDMA = 24
INTERLEAVE = True
EPS = 1e-6
D = 1024
FF = 2816
NEG = -60000.0
QS = 128 ** -0.5
A_COLS = 8704
OFF = dict(z=0, xbc=1024, mq=2560, mk=3072, mv=3584, mo=4608, rq=5632, rk=6144, rv=6656, rg=7680)
LOGG = [math.log(1.0 - math.exp(v)) for v in
        np.linspace(math.log(1.0 / 32.0), math.log(1.0 / 512.0), 4, dtype=np.float32).astype(np.float64)]


class Ctx:
    def __init__(self, nc, es):
        self.nc = nc
        self.es = es
        self.q = {e: [] for e in ENGS}
        self.cnt = {e: 0 for e in ENGS}
        self.seen = {e: {} for e in ENGS}
        self.lastw = {}
        self.rd = {}
        self.semh = {}
        for e in ENGS:
            self.semh[e] = es.enter_context(nc.semaphore("s_" + e))
        for i in range(NDMA):
            self.semh[("d", i)] = es.enter_context(nc.semaphore("s_d%d" % i))
        self.dma_cnt = [0] * NDMA
        self.dma_last = [None] * NDMA
        self.dma_rr = 0
        self.nops = 0
        self.pes = None
        self.cap = None

    def capture(self, f):
        self.cap = []
        f()
        out = self.cap
        self.cap = None
        return out

    def replay_interleaved(self, streams):
        streams = [s_ for s_ in streams if s_]
        if not streams:
            return
        n = max(len(s_) for s_ in streams)
        pos = [0] * len(streams)
        for k in range(1, n + 1):
            for si, s_ in enumerate(streams):
                tgt = (k * len(s_) + n - 1) // n
                while pos[si] < tgt:
                    self.op(*s_[pos[si]])
                    pos[si] += 1

    def sb(self, name, shape, dtype=F32):
        st = self.pes if self.pes is not None else self.es
        self.nsb = getattr(self, "nsb", 0) + 1
        return st.enter_context(self.nc.sbuf_tensor("sb%d_%s" % (self.nsb, name), list(shape), dtype))

    def op(self, eng, fn, reads=(), writes=(), dma=False):
        if self.cap is not None:
            self.cap.append((eng, fn, tuple(reads), tuple(writes), dma))
            return
        evs = {}

        def add(ev):
            if ev is None:
                return
            s, v = ev
            if evs.get(s, 0) < v:
                evs[s] = v

        for k in reads:
            add(self.lastw.get(k))
        for k in writes:
            add(self.lastw.get(k))
            for s, v in self.rd.get(k, {}).items():
                add((s, v))
        if dma:
            i = self.dma_rr
            self.dma_rr = (i + 1) % NDMA
            add(self.dma_last[i])
            self.dma_cnt[i] += 16
            ev = (("d", i), self.dma_cnt[i])
            self.dma_last[i] = ev
        else:
            self.cnt[eng] += 1
            ev = (eng, self.cnt[eng])
        waits = []
        seen = self.seen[eng]
        for s, v in evs.items():
            if eng == "pe" and s == "pe":
                continue
            if seen.get(s, 0) < v:
                waits.append((s, v))
                seen[s] = v
        self.q[eng].append((fn, waits, ev, dma))
        for k in writes:
            self.lastw[k] = ev
            self.rd[k] = {}
        for k in reads:
            d = self.rd.setdefault(k, {})
            s, v = ev
            if d.get(s, 0) < v:
                d[s] = v
        self.nops += 1

    def dma(self, out, in_, r=(), w=(), eng="sp"):
        self.op(eng, lambda e: e.dma_start(out=out, in_=in_), r, w, dma=True)

    def mm(self, out, lhsT, rhs, start=True, stop=True, r=(), w=()):
        self.op("pe", lambda e: e.matmul(out, lhsT, rhs, start=start, stop=stop), r, w)

    def tr(self, out, in_, ident, r=(), w=()):
        self.op("pe", lambda e: e.transpose(out, in_, ident), r, w)

    def act(self, out, in_, func, r=(), w=(), **kw):
        self.op("act", lambda e: e.activation(out, in_, func, **kw), r, w)

    def tt(self, out, in0, in1, op, r=(), w=(), eng="dve"):
        self.op(eng, lambda e: e.tensor_tensor(out, in0, in1, op), r, w)

    def ts(self, out, in0, s1, s2, op0, op1=None, r=(), w=(), eng="dve"):
        if op1 is None:
            self.op(eng, lambda e: e.tensor_scalar(out, in0, s1, None, op0), r, w)
        else:
            self.op(eng, lambda e: e.tensor_scalar(out, in0, s1, s2, op0, op1), r, w)

    def stt(self, out, in0, sc, in1, op0, op1, r=(), w=()):
        self.op("dve", lambda e: e.scalar_tensor_tensor(out, in0, sc, in1, op0, op1), r, w)

    def cp(self, out, in_, r=(), w=(), eng="dve"):
        if eng == "act":
            self.op("act", lambda e: e.copy(out, in_), r, w)
        else:
            self.op(eng, lambda e: e.tensor_copy(out, in_), r, w)

    def memset(self, ap, val, w=(), eng="pool"):
        self.op(eng, lambda e: e.memset(ap, val), (), w)

    def recip(self, out, in_, r=(), w=()):
        self.op("dve", lambda e: e.reciprocal(out, in_), r, w)

    def emit(self, final=False):
        nc = self.nc
        fence = []
        for i in range(NDMA):
            if self.dma_cnt[i]:
                fence.append((("d", i), self.dma_cnt[i]))
        for e in ENGS:
            if self.cnt[e]:
                fence.append((e, self.cnt[e]))
        qs = self.q
        with nc.Block() as block:
            def run(name, e):
                for fn, waits, ev, dma in qs[name]:
                    for s, v in waits:
                        e.wait_ge(self.semh[s], v)
                    ins = fn(e)
                    ins.then_inc(self.semh[ev[0]], 16 if dma else 1)
                for s, v in fence:
                    if s != name:
                        e.wait_ge(self.semh[s], v)

            @block.tensor
            def _(e):
                run("pe", e)

            @block.scalar
            def _(e):
                run("act", e)

            @block.vector
            def _(e):
                run("dve", e)

            @block.gpsimd
            def _(e):
                run("pool", e)

            @block.sync
            def _(e):
                run("sp", e)
        self.q = {e: [] for e in ENGS}
        for e in ENGS:
            for s, v in fence:
                self.seen[e][s] = v
        self.lastw = {}
        self.rd = {}


def host_consts():
    j = np.arange(128)
    tri = (j[:, None] <= j[None, :]).astype(np.float32)
    triT = (j[:, None] >= j[None, :]).astype(np.float32)
    ident = np.eye(128, dtype=np.float32)
    ones = np.ones((128, 128), np.float32)
    nmf = np.where(j[:, None] <= j[None, :], 0.0, NEG).astype(np.float32)
    nmb = np.where(j[:, None] >= j[None, :], 0.0, NEG).astype(np.float32)
    cb16 = np.concatenate([ident, tri, triT, ones, np.tile(nmf, (1, 4)), np.tile(nmb, (1, 4))], axis=1)
    lg = np.array(LOGG, np.float64)
    dist = np.abs(j[:, None] - j[None, :]).astype(np.float64)
    dtab = np.concatenate([np.exp(dist * lg[h]) for h in range(4)], axis=1)
    xif = np.tile(np.concatenate([np.exp((j + 1) * lg[h]) for h in range(4)])[None, :], (128, 1))
    xib = np.tile(np.concatenate([np.exp((128 - j) * lg[h]) for h in range(4)])[None, :], (128, 1))
    zf = np.stack([np.exp((127 - j) * lg[h]) for h in range(4)], axis=1)
    zb = np.stack([np.exp(j * lg[h]) for h in range(4)], axis=1)
    cf32 = np.concatenate([dtab, xif, xib, zf, zb, ident], axis=1).astype(np.float32)
    return cb16.astype(ml_dtypes.bfloat16), cf32


C16 = dict(ident=0, tri=128, triT=256, ones=384, nmf=512, nmb=1024)
C32 = dict(dtab=0, xif=512, xib=1024, zf=1536, zb=1540, identf=1544)
NC16 = 1536
NC32 = 1672


def rope_tables(seglen, nseg_positions):
    half = 64
    inv = (10000.0 ** (-np.arange(half, dtype=np.float32) / half)).astype(np.float32)
    pos = nseg_positions.astype(np.float32)
    ang = pos[:, None] * inv[None, :]
    cos = np.cos(ang).astype(np.float32)
    sin = np.sin(ang).astype(np.float32)
    return np.concatenate([cos, sin, cos * np.float32(QS), sin * np.float32(QS)], axis=1)


def build(DEPTH, SEG, debug=False, phases="PSMD"):
    NCHS = SEG // 128
    NCH = 2 * NCHS
    NT = 2 * SEG
    NBLK = NT // 512
    XBR = 2 * (SEG + 3)
    nc = bass.Bass("TRN2", target_bir_lowering=False)

    def din(name, shape, dt=F32):
        return nc.dram_tensor(name, list(shape), dt, kind="ExternalInput").ap()

    def dscr(name, shape, dt):
        return nc.dram_tensor(name, list(shape), dt, kind=("ExternalOutput" if debug else "Internal")).ap()

    x_in = din("x", [NT, D])
    flag_in = din("flag", [128, 1])
    rope_in = din("rope", [NT, 256])
    cb16_in = din("cb16", [128, NC16], BF16)
    cf32_in = din("cf32", [128, NC32])
    w_in = din("w_in", [DEPTH, D, 8752])
    w_gate = din("w_gate", [DEPTH, D, 3072])
    w_br = din("w_branch", [DEPTH, 3, D, D])
    w_out = din("w_out", [DEPTH, D, D])
    w_fg = din("w_ffn_gate", [DEPTH, D, FF])
    w_fu = din("w_ffn_up", [DEPTH, D, FF])
    w_fd = din("w_ffn_down", [DEPTH, FF, D])
    vecs = {}
    for name, n in [("norm_mix_g", D), ("b_gate", 3072), ("conv_w", 4 * 1536), ("conv_b", 1536),
                    ("ssd_a_log", 32), ("ssd_dt_bias", 32), ("ssd_d", 16), ("ssd_norm_g", D),
                    ("mlstm_gate_bias", 16), ("mlstm_norm_g", D), ("ret_norm_g", D), ("norm_ffn_g", D)]:
        vecs[name] = din(name, [DEPTH, n])
    fin_g = din("final_norm_g", [D])
    y_out = nc.dram_tensor("y", [NT, D], F32, kind="ExternalOutput").ap()

    wb_in = dscr("wb_in", [DEPTH, D, 8752], BF16) if False else nc.dram_tensor("wb_in", [DEPTH, D, 8752], BF16, kind="Internal").ap()
    wb_gate = nc.dram_tensor("wb_gate", [DEPTH, D, 3072], BF16, kind="Internal").ap()
    wb_br = nc.dram_tensor("wb_br", [DEPTH, 3, D, D], BF16, kind="Internal").ap()
    wb_out = nc.dram_tensor("wb_out", [DEPTH, D, D], BF16, kind="Internal").ap()
    wb_fg = nc.dram_tensor("wb_fg", [DEPTH, D, FF], BF16, kind="Internal").ap()
    wb_fu = nc.dram_tensor("wb_fu", [DEPTH, D, FF], BF16, kind="Internal").ap()
    wb_fd = nc.dram_tensor("wb_fd", [DEPTH, FF, D], BF16, kind="Internal").ap()
    XS = dscr("XS", [NT, D], F32)
    A_ = dscr("A_", [NT, A_COLS], BF16)
    XB = dscr("XB", [XBR, 1536], BF16)
    G_ = dscr("G_", [NT, 48], F32)
    GT = dscr("GT", [NT, 3072], BF16)
    SB = dscr("SBst", [NCH, 128, 3076], BF16)
    MO = dscr("MO", [NT, 3072], BF16)

    def xbrow(tok):
        s = tok // SEG
        return s * (SEG + 3) + 2 + (tok - s * SEG)

    with ExitStack() as es:
        c = Ctx(nc, es)
        c16 = c.sb("c16", [128, NC16], BF16)
        c32 = c.sb("c32", [128, NC32], F32)
        flag = c.sb("flag", [128, 1], F32)
        pbs = [es.enter_context(nc.psum_tensor("pb%d" % i, [128, 512], F32)) for i in range(8)]

        def C16s(name, n=128):
            return c16[:, C16[name]:C16[name] + n]

        ident = C16s("ident")
        identf = c32[:, C32["identf"]:C32["identf"] + 128]

        with ExitStack() as pes:
            c.pes = pes
            c.dma(c16[:], cb16_in, w=["c16"])
            c.dma(c32[:], cf32_in, w=["c32"])
            c.dma(flag[:], flag_in, w=["flag"])
            for src, dst in [(w_in, wb_in), (w_gate, wb_gate), (w_out, wb_out), (w_fg, wb_fg),
                             (w_fu, wb_fu), (w_fd, wb_fd)]:
                for l in range(DEPTH):
                    s2 = src[l].rearrange("k c -> (k c)").rearrange("(r e) -> r e", e=1024)
                    d2 = dst[l].rearrange("k c -> (k c)").rearrange("(r e) -> r e", e=1024)
                    c.dma(d2, s2, w=["wcast"], eng="pool")
            for l in range(DEPTH):
                s2 = w_br[l].rearrange("b k c -> (b k c)").rearrange("(r e) -> r e", e=1024)
                d2 = wb_br[l].rearrange("b k c -> (b k c)").rearrange("(r e) -> r e", e=1024)
                c.dma(d2, s2, w=["wcast"], eng="pool")
            zt = c.sb("zt", [4, 1536], BF16)
            c.memset(zt[:], 0.0, w=["zt"])
            for s in range(2):
                c.dma(XB[s * (SEG + 3):s * (SEG + 3) + 2, :], zt[0:2, :], r=["zt"], w=["XBpad"])
                c.dma(XB[s * (SEG + 3) + 2 + SEG:s * (SEG + 3) + 3 + SEG, :], zt[0:1, :], r=["zt"], w=["XBpad"])
            c.emit()
            c.pes = None

        def bcast_load(t, src1d, key):
            c.dma(t, src1d.partition_broadcast(128), w=[key])

        def rmsnorm_rows(xt, key_x, gt, out, key_out, tmpf, ss, n=D):
            c.act(tmpf, xt, AF.Square, r=[key_x], w=["nrm_tmp", "nrm_ss"], accum_out=ss)
            c.ts(ss, ss, 1.0 / n, EPS, ALU.mult, ALU.add, r=["nrm_ss"], w=["nrm_ss"])
            c.act(ss, ss, AF.Sqrt, r=["nrm_ss"], w=["nrm_ss"])
            c.recip(ss, ss, r=["nrm_ss"], w=["nrm_ss"])
            c.stt(out, xt, ss, gt, ALU.mult, ALU.mult, r=[key_x, "nrm_ss", "ltab"], w=[key_out])

        wcount = [0]

        DB = [0, 1, 3, 4, 5, 6, 7]
        pcount = [0]

        def dense(actT, akey, KT, wsrc, ncols, evac, fmajor=False, wt=None):
            wv = wsrc.rearrange("(kt p) c -> p kt c", p=128)
            ngrp = (ncols + 511) // 512

            def load(g):
                c0 = g * 512
                cw = min(512, ncols - c0)
                wi = wcount[0] % 2
                wcount[0] += 1
                c.dma(wt[wi][:, 0:KT, 0:cw], wv[:, :, c0:c0 + cw], r=["wcast"], w=["wt%d" % wi])
                return wi

            nxt = load(0)
            for g in range(ngrp):
                c0 = g * 512
                cw = min(512, ncols - c0)
                wi = nxt
                wtile = wt[wi]
                if g + 1 < ngrp:
                    nxt = load(g + 1)
                if not fmajor:
                    for t in range(4):
                        pi = DB[pcount[0] % len(DB)]
                        pcount[0] += 1
                        pb = pbs[pi]
                        for kt in range(KT):
                            c.mm(pb[:, 0:cw], actT[:, kt, t * 128:(t + 1) * 128], wtile[:, kt, 0:cw],
                                 start=(kt == 0), stop=(kt == KT - 1), r=[akey, "wt%d" % wi], w=["pb%d" % pi])
                        evac(c0, cw, t, pb[:, 0:cw], "pb%d" % pi)
                else:
                    for m in range(cw // 128):
                        pi = DB[pcount[0] % len(DB)]
                        pcount[0] += 1
                        pb = pbs[pi]
                        for kt in range(KT):
                            c.mm(pb[:, :], wtile[:, kt, m * 128:(m + 1) * 128], actT[:, kt, :],
                                 start=(kt == 0), stop=(kt == KT - 1), r=[akey, "wt%d" % wi], w=["pb%d" % pi])
                        evac(c0 + m * 128, 128, None, pb[:, :], "pb%d" % pi)

        def transpose_to(dstT, dkey, src, skey, ntile, t):
            pv = pbs[2][:].bitcast(BF16)
            for k0 in range(0, ntile, 8):
                n = min(8, ntile - k0)
                for k in range(n):
                    c.tr(pv[:, k * 128:(k + 1) * 128], src[:, (k0 + k) * 128:(k0 + k + 1) * 128], ident,
                         r=[skey, "c16"], w=["pb2"])
                c.cp(dstT[:, k0:k0 + n, t * 128:(t + 1) * 128],
                     pv[:, 0:n * 128].rearrange("p (k q) -> p k q", k=n), r=["pb2"], w=[dkey], eng="act")

        for l in range(DEPTH):
            xsrc = x_in if l == 0 else XS
            last = (l == DEPTH - 1)
            with ExitStack() as pes:
              if "P" in phases:
                c.pes = pes
                g_mix = c.sb("g_mix", [128, D])
                bg = c.sb("bg", [128, 3072])
                bcast_load(g_mix[:], vecs["norm_mix_g"][l], "ltab")
                bcast_load(bg[:], vecs["b_gate"][l], "ltab")
                xt = [c.sb("xt%d" % i, [128, D]) for i in range(2)]
                ut = c.sb("ut", [128, D], BF16)
                uT = c.sb("uT", [128, 8, 512], BF16)
                wt = [c.sb("wt%d" % i, [128, 8, 512], BF16) for i in range(2)]
                NOE = 6
                oev = [c.sb("oev%d" % i, [128, 512], BF16) for i in range(NOE)]
                oef = c.sb("oef", [128, 512], F32)
                og = c.sb("og", [128, 48], F32)
                rp = c.sb("rp", [128, 4, 256], F32)
                rt = [c.sb("rt%d" % i, [128, 4, 64], F32) for i in range(4)]
                tmpf = c.sb("tmpf", [128, D], F32)
                ss = c.sb("ss", [128, 1], F32)
                ecnt = [0]
                for b in range(NBLK):
                    r0 = b * 512
                    c.dma(rp[:], rope_in[r0:r0 + 512, :].rearrange("(t p) e -> p t e", p=128), w=["rp"])
                    for t in range(4):
                        xi = t % 2
                        c.dma(xt[xi][:], xsrc[r0 + t * 128:r0 + (t + 1) * 128, :], r=["XS"], w=["xt%d" % xi])
                        rmsnorm_rows(xt[xi][:], "xt%d" % xi, g_mix[:], ut[:], "ut", tmpf[:], ss[:])
                        transpose_to(uT, "uT", ut, "ut", 8, t)

                    def evacA(c0, cw, t, ps, pkey, r0=r0):
                        rows = slice(r0 + t * 128, r0 + (t + 1) * 128)
                        oi = ecnt[0] % NOE
                        ecnt[0] += 1
                        o = oev[oi]
                        okey = "oev%d" % oi
                        if c0 == OFF["mq"]:
                            c.act(o[:], ps, AF.Copy, r=[pkey], w=[okey], scale=QS)
                        elif c0 in (OFF["rq"], OFF["rk"]):
                            co = rp[:, t, 0:64] if c0 == OFF["rq"] else rp[:, t, 128:192]
                            si = rp[:, t, 64:128] if c0 == OFF["rq"] else rp[:, t, 192:256]
                            cob = co.unsqueeze(1).to_broadcast([128, 4, 64])
                            sib = si.unsqueeze(1).to_broadcast([128, 4, 64])
                            p3 = ps.rearrange("p (h d) -> p h d", h=4)
                            o3 = o[:].rearrange("p (h d) -> p h d", h=4)
                            x1 = p3[:, :, 0:64]
                            x2 = p3[:, :, 64:128]
                            c.tt(rt[0][:], x1, cob, ALU.mult, r=[pkey, "rp"], w=["rt0"])
                            c.tt(rt[1][:], x2, sib, ALU.mult, r=[pkey, "rp"], w=["rt1"])
                            c.tt(rt[2][:], x1, sib, ALU.mult, r=[pkey, "rp"], w=["rt2"])
                            c.tt(rt[3][:], x2, cob, ALU.mult, r=[pkey, "rp"], w=["rt3"])
                            c.tt(o3[:, :, 0:64], rt[0][:], rt[1][:], ALU.subtract, r=["rt0", "rt1"], w=[okey], eng="pool")
                            c.tt(o3[:, :, 64:128], rt[2][:], rt[3][:], ALU.add, r=["rt2", "rt3"], w=[okey], eng="pool")
                        else:
                            c.cp(o[:], ps, r=[pkey], w=[okey], eng=("act" if (ecnt[0] % 2) else "dve"))
                        if OFF["xbc"] <= c0 < OFF["mq"]:
                            xr = xbrow(r0 + t * 128)
                            c.dma(XB[xr:xr + 128, c0 - OFF["xbc"]:c0 - OFF["xbc"] + cw], o[:, 0:cw], r=[okey], w=["XB"], eng="act")
                        else:
                            c.dma(A_[rows, c0:c0 + cw], o[:, 0:cw], r=[okey], w=["A_"], eng="act")

                    dense(uT, "uT", 8, wb_in[l][:, 0:A_COLS], A_COLS, evacA, wt=wt)

                    def evacB(c0, cw, t, ps, pkey, r0=r0):
                        rows = slice(r0 + t * 128, r0 + (t + 1) * 128)
                        c.cp(og[:], ps, r=[pkey], w=["og"])
                        c.dma(G_[rows, :], og[:], r=["og"], w=["G_"], eng="act")

                    dense(uT, "uT", 8, wb_in[l][:, A_COLS:8752], 48, evacB, wt=wt)

                    def evacG(c0, cw, t, ps, pkey, r0=r0):
                        rows = slice(r0 + t * 128, r0 + (t + 1) * 128)
                        oi = ecnt[0] % NOE
                        ecnt[0] += 1
                        o = oev[oi]
                        okey = "oev%d" % oi
                        c.tt(oef[:], ps, bg[:, c0:c0 + cw], ALU.add, r=[pkey, "ltab"], w=["oef"])
                        c.act(o[:], oef[:], AF.Sigmoid, r=["oef"], w=[okey])
                        c.dma(GT[rows, c0:c0 + cw], o[:], r=[okey], w=["GT"], eng="act")

                    dense(uT, "uT", 8, wb_gate[l], 3072, evacG, wt=wt)
                hp = c.sb("hp", [4, 1536], BF16)
                hq = c.sb("hq", [4, 1536], BF16)
                c.dma(hp[0:2, :], XB[xbrow(SEG - 2):xbrow(SEG - 2) + 2, :], r=["XB"], w=["hp"])
                c.ts(hq[0:2, :], hp[0:2, :], flag[0:2, 0:1], None, ALU.mult, r=["hp", "flag"], w=["hq"])
                c.dma(XB[(SEG + 3):(SEG + 3) + 2, :], hq[0:2, :], r=["hq"], w=["XB"])
                hp2 = c.sb("hp2", [4, 1536], BF16)
                hq2 = c.sb("hq2", [4, 1536], BF16)
                c.dma(hp2[0:1, :], XB[xbrow(SEG):xbrow(SEG) + 1, :], r=["XB"], w=["hp2"])
                c.ts(hq2[0:1, :], hp2[0:1, :], flag[0:1, 0:1], None, ALU.mult, r=["hp2", "flag"], w=["hq2"])
                c.dma(XB[2 + SEG:3 + SEG, :], hq2[0:1, :], r=["hq2"], w=["XB"])
                c.emit()
                c.pes = None

            for mode in ("S", "M"):
                if mode not in phases:
                    continue
                with ExitStack() as pes:
                    c.pes = pes
                    isM = (mode == "M")
                    cwt = c.sb("cwt", [128, 4, 1536], F32)
                    cbt = c.sb("cbt", [128, 1536], F32)
                    dtb = c.sb("dtb", [128, 32], F32)
                    aneg = c.sb("aneg", [128, 32], F32)
                    mgb = c.sb("mgb", [128, 16], F32)
                    bcast_load(cwt[:].rearrange("p t c -> p (t c)"), vecs["conv_w"][l], "ltab")
                    bcast_load(cbt[:], vecs["conv_b"][l], "ltab")
                    bcast_load(dtb[:], vecs["ssd_dt_bias"][l], "ltab")
                    bcast_load(aneg[:], vecs["ssd_a_log"][l], "ltab")
                    bcast_load(mgb[:], vecs["mlstm_gate_bias"][l], "ltab")
                    c.act(aneg[:], aneg[:], AF.Exp, r=["ltab"], w=["ltab"])
                    c.ts(aneg[:], aneg[:], -1.0, None, ALU.mult, r=["ltab"], w=["ltab"])
                    if isM:
                        dsk = c.sb("dsk", [128, 16], F32)
                        nga = c.sb("nga", [128, D], BF16)
                        ngb = c.sb("ngb", [128, D], BF16)
                        ngc = c.sb("ngc", [128, D], BF16)
                        bcast_load(dsk[:], vecs["ssd_d"][l], "ltab")
                        c.dma(nga[:], vecs["ssd_norm_g"][l].partition_broadcast(128), w=["ltab"], eng="pool")
                        c.dma(ngb[:], vecs["mlstm_norm_g"][l].partition_broadcast(128), w=["ltab"], eng="pool")
                        c.dma(ngc[:], vecs["ret_norm_g"][l].partition_broadcast(128), w=["ltab"], eng="pool")
                    xb4 = [c.sb("xb4_%d" % i, [128, 1536], BF16) for i in range(4)]
                    cacc = c.sb("cacc", [128, 1536], F32)
                    ctmp = c.sb("ctmp", [128, 1536], F32)
                    xactL = [c.sb("xact%d" % i, [128, 1536], BF16) for i in range(2)]
                    gtL = [c.sb("gt%d" % i, [128, 48], F32) for i in range(2)]
                    smL = [{n: c.sb("sm%d_%s" % (i, n), [128, 40], F32) for n in
                            ("dtp", "dt", "ld", "li", "csx", "tot", "bw", "wend", "ecs", "dec", "gml", "sp")} for i in range(2)]
                    ldbL = [c.sb("ldb%d" % i, [128, 40], BF16) for i in range(2)]
                    st = c.sb("st", [128, 3076], F32)
                    stbL = [c.sb("stb%d" % i, [128, 3076], BF16) for i in range(2)]
                    vaug = c.sb("vaug", [128, 4, 257], BF16)
                    xw = c.sb("xw", [128, D], BF16)
                    kwL = [c.sb("kw%d" % i, [128, 128], BF16) for i in range(2)]
                    c.memset(vaug[:], 1.0, w=["vaug"])
                    c.memset(st[:], 0.0, w=["st"])
                    c.memset(stbL[0][:], 0.0, w=["stb0"])
                    c.memset(stbL[1][:], 0.0, w=["stb1"])
                    if isM:
                        at = c.sb("at", [128, A_COLS - 1536], BF16)
                        sbl = c.sb("sbl", [128, 3076], BF16)
                        R = c.sb("R", [128, 40, 128], BF16)
                        WL = [c.sb("W%d" % i, [128, 40, 128], BF16) for i in range(2)]
                        Mb = c.sb("Mb", [128, 16, 128], BF16)
                        bcT = c.sb("bcT", [128, 4, 128], BF16)
                        qkT = c.sb("qkT", [128, 16, 128], BF16)
                        f1 = c.sb("f1", [128, D], F32)
                        f2 = c.sb("f2", [128, D], F32)
                        jk1 = c.sb("jk1", [128, D], BF16)
                        jk2 = c.sb("jk2", [128, 256], BF16)
                        jk3 = c.sb("jk3", [128, 256], BF16)
                        g1 = c.sb("g1", [128, D], BF16)
                        g2 = c.sb("g2", [128, D], BF16)
                        ry = c.sb("ry", [128, D], F32)
                        Pm = c.sb("Pm", [128, 8, 128], BF16)
                        ndL = [c.sb("nd%d" % i, [128, 257], F32) for i in range(2)]
                        rrL = [c.sb("rr%d" % i, [128, 1], F32) for i in range(2)]
                        hacc = c.sb("hacc", [128, 4, 256], F32)
                        Pr = c.sb("Pr", [128, 4, 128], BF16)
                        qxf = c.sb("qxf", [128, 4, 128], BF16)
                        qxb = c.sb("qxb", [128, 4, 128], BF16)
                        mot = c.sb("mot", [128, 3072], BF16)
                        ss4 = c.sb("ss4", [128, 4], F32)
                        ss4r = c.sb("ss4r", [128, 4], F32)
                        ss1 = c.sb("ss1", [128, 1], F32)
                    else:
                        kv = c.sb("kv", [128, 3072], BF16)

                    order = list(range(NCH)) if isM else list(range(NCH - 1, -1, -1))
                    fo = 0 if isM else 20

                    def loadsA(ch, p):
                        r0 = ch * 128
                        xr = xbrow(r0)
                        for t in range(4):
                            c.dma(xb4[t][:], XB[xr + t - 2:xr + t - 2 + 128, :], r=["XB"], w=["xb4_%d" % t])
                        c.dma(gtL[p][:], G_[r0:r0 + 128, :], r=["G_"], w=["gt%d" % p])

                    def prepA(ch, p):
                        xact = xactL[p]
                        xk = "xact%d" % p
                        gt = gtL[p]
                        gk = "gt%d" % p
                        S = smL[p]
                        ldb = ldbL[p]

                        def k(n):
                            return "%s%d" % (n, p)

                        c.tt(cacc[:], xb4[0][:], cwt[:, 0, :], ALU.mult, r=["xb4_0", "ltab"], w=["cacc"], eng="pool")
                        for t in range(1, 4):
                            c.tt(ctmp[:], xb4[t][:], cwt[:, t, :], ALU.mult, r=["xb4_%d" % t, "ltab"], w=["ctmp"], eng="pool")
                            c.tt(cacc[:], cacc[:], ctmp[:], ALU.add, r=["cacc", "ctmp"], w=["cacc"])
                        c.tt(cacc[:], cacc[:], cbt[:], ALU.add, r=["cacc", "ltab"], w=["cacc"])
                        c.act(xact[:], cacc[:], AF.Silu, r=["cacc"], w=[xk])
                        c.tt(S["dtp"][:, 0:32], gt[:, 0:32], dtb[:], ALU.add, r=[gk, "ltab"], w=[k("dtp")])
                        c.act(S["dtp"][:, 0:32], S["dtp"][:, 0:32], AF.Exp, r=[k("dtp")], w=[k("dtp")])
                        c.act(S["dt"][:, 0:32], S["dtp"][:, 0:32], AF.Ln, r=[k("dtp")], w=[k("dt")], bias=1.0)
                        c.tt(S["gml"][:, 0:16], gt[:, 32:48], mgb[:], ALU.add, r=[gk, "ltab"], w=[k("gml")])
                        c.act(S["sp"][:, 0:8], S["gml"][:, 0:8], AF.Exp, r=[k("gml")], w=[k("sp")], scale=-1.0)
                        c.act(S["sp"][:, 0:8], S["sp"][:, 0:8], AF.Ln, r=[k("sp")], w=[k("sp")], bias=1.0)
                        c.act(S["li"][:, 0:16], S["dt"][:, 0:16], AF.Ln, r=[k("dt")], w=[k("li")])
                        c.act(S["li"][:, 20:36], S["dt"][:, 16:32], AF.Ln, r=[k("dt")], w=[k("li")])
                        c.cp(S["li"][:, 16:20], S["gml"][:, 8:12], r=[k("gml")], w=[k("li")])
                        c.cp(S["li"][:, 36:40], S["gml"][:, 12:16], r=[k("gml")], w=[k("li")])
                        c.tt(S["ld"][:, 0:16], S["dt"][:, 0:16], aneg[:, 0:16], ALU.mult, r=[k("dt"), "ltab"], w=[k("ld")])
                        c.tt(S["ld"][:, 20:36], S["dt"][:, 16:32], aneg[:, 16:32], ALU.mult, r=[k("dt"), "ltab"], w=[k("ld")])
                        c.ts(S["ld"][:, 16:20], S["sp"][:, 0:4], -1.0, None, ALU.mult, r=[k("sp")], w=[k("ld")])
                        c.ts(S["ld"][:, 36:40], S["sp"][:, 4:8], -1.0, None, ALU.mult, r=[k("sp")], w=[k("ld")])
                        c.cp(ldb[:], S["ld"][:], r=[k("ld")], w=[k("ldb")])
                        pc = pbs[0]
                        c.mm(pc[:, 0:40], C16s("tri"), ldb[:], r=["c16", k("ldb")], w=["pb0"])
                        c.mm(pc[:, 40:80], C16s("triT"), ldb[:], r=["c16", k("ldb")], w=["pb0"])
                        c.mm(pc[:, 80:120], C16s("ones"), ldb[:], r=["c16", k("ldb")], w=["pb0"])
                        c.cp(S["csx"][:, 0:20], pc[:, 0:20], r=["pb0"], w=[k("csx")])
                        c.cp(S["csx"][:, 20:40], pc[:, 60:80], r=["pb0"], w=[k("csx")])
                        c.cp(S["tot"][:], pc[:, 80:120], r=["pb0"], w=[k("tot")], eng="act")
                        c.tt(S["bw"][:], S["li"][:], S["csx"][:], ALU.subtract, r=[k("li"), k("csx")], w=[k("bw")])
                        c.tt(S["wend"][:], S["bw"][:], S["tot"][:], ALU.add, r=[k("bw"), k("tot")], w=[k("wend")])
                        c.act(S["wend"][:], S["wend"][:], AF.Exp, r=[k("wend")], w=[k("wend")])
                        c.act(S["ecs"][:], S["csx"][:], AF.Exp, r=[k("csx")], w=[k("ecs")])
                        c.act(S["dec"][:], S["tot"][:], AF.Exp, r=[k("tot")], w=[k("dec")])
                        if isM:
                            W = WL[p]
                            wk = "W%d" % p
                            c.tt(R[:, 0:20, :], ldb[:, 0:20].unsqueeze(2).to_broadcast([128, 20, 128]),
                                 C16s("tri").unsqueeze(1).to_broadcast([128, 20, 128]), ALU.mult,
                                 r=[k("ldb"), "c16"], w=["R"], eng="pool")
                            c.tt(R[:, 20:40, :], ldb[:, 20:40].unsqueeze(2).to_broadcast([128, 20, 128]),
                                 C16s("triT").unsqueeze(1).to_broadcast([128, 20, 128]), ALU.mult,
                                 r=[k("ldb"), "c16"], w=["R"], eng="pool")
                            for g in range(10):
                                pw = pbs[0]
                                nm = C16s("nmf", 512) if g < 5 else C16s("nmb", 512)
                                c.mm(pw[:, :], C16s("ones"), R[:, 4 * g:4 * g + 4, :].rearrange("p a b -> p (a b)"),
                                     start=True, stop=False, r=["c16", "R"], w=["pb0"])
                                c.mm(pw[:, :], ident, nm, start=False, stop=True, r=["c16"], w=["pb0"])
                                for j in range(4):
                                    cc = 4 * g + j
                                    c.act(W[:, cc, :], pw[:, j * 128:(j + 1) * 128], AF.Exp,
                                          r=["pb0", k("bw")], w=[wk], bias=S["bw"][:, cc:cc + 1])
                            c.tt(W[:, 0:16, :], W[:, 0:16, :], W[:, 20:36, :], ALU.add, r=[wk], w=[wk], eng="pool")

                    def mainA(ch, p, pin, pout):
                        r0 = ch * 128
                        xact = xactL[p]
                        xk = "xact%d" % p
                        S = smL[p]

                        def k(n):
                            return "%s%d" % (n, p)

                        stb = stbL[pin]
                        sk = "stb%d" % pin
                        if isM:
                            W = WL[p]
                            wk = "W%d" % p
                            c.dma(at[:, 0:1024], A_[r0:r0 + 128, 0:1024], r=["A_"], w=["at"])
                            c.dma(at[:, 1024:A_COLS - 1536], A_[r0:r0 + 128, 2560:A_COLS], r=["A_"], w=["at"])
                            c.dma(sbl[:], SB[ch], r=["SB"], w=["sbl"])
                            mk = at[:, OFF["mk"] - 1536:OFF["mk"] - 1536 + 512]
                            mv = at[:, OFF["mv"] - 1536:OFF["mv"] - 1536 + 1024]
                            rk = at[:, OFF["rk"] - 1536:OFF["rk"] - 1536 + 512]
                            rv = at[:, OFF["rv"] - 1536:OFF["rv"] - 1536 + 1024]
                            kvkey = "at"
                        else:
                            c.dma(kv[:, 0:1536], A_[r0:r0 + 128, OFF["mk"]:OFF["mk"] + 1536], r=["A_"], w=["kv"])
                            c.dma(kv[:, 1536:3072], A_[r0:r0 + 128, OFF["rk"]:OFF["rk"] + 1536], r=["A_"], w=["kv"])
                            mk = kv[:, 0:512]
                            mv = kv[:, 512:1536]
                            rk = kv[:, 1536:2048]
                            rv = kv[:, 2048:3072]
                            kvkey = "kv"
                            c.dma(SB[ch], stb[:], r=[sk], w=["SB"], eng="pool")
                        c.cp(vaug[:, :, 0:256], mv.rearrange("p (h v) -> p h v", h=4), r=[kvkey], w=["vaug"], eng="pool")

                        c.tt(xw[:].rearrange("p (h d) -> p h d", h=16), xact[:, 0:1024].rearrange("p (h d) -> p h d", h=16),
                             S["wend"][:, fo:fo + 16].unsqueeze(2).to_broadcast([128, 16, 64]), ALU.mult,
                             r=[xk, k("wend")], w=["xw"], eng="pool")
                        for g in range(2):
                            c.mm(pbs[6 + g][:, :], xact[:, 1024 + g * 128:1024 + (g + 1) * 128], xw[:, g * 512:(g + 1) * 512],
                                 r=[xk, "xw"], w=["pb%d" % (6 + g)])
                        c.tt(st[:, 0:1024].rearrange("p (h d) -> p h d", h=16), st[:, 0:1024].rearrange("p (h d) -> p h d", h=16),
                             S["dec"][:, fo:fo + 16].unsqueeze(2).to_broadcast([128, 16, 64]), ALU.mult,
                             r=["st", k("dec")], w=["st"])
                        for g in range(2):
                            c.tt(st[:, g * 512:(g + 1) * 512], st[:, g * 512:(g + 1) * 512], pbs[6 + g][:, :], ALU.add,
                                 r=["st", "pb%d" % (6 + g)], w=["st"])
                        mo_ = 16 + fo
                        for h in range(4):
                            kw = kwL[h % 2]
                            kk = "kw%d" % (h % 2)
                            c.ts(kw[:], mk[:, h * 128:(h + 1) * 128], S["wend"][:, mo_ + h:mo_ + h + 1], None, ALU.mult,
                                 r=[kvkey, k("wend")], w=[kk])
                            pi = 6 + (h % 2)
                            c.mm(pbs[pi][:, 0:257], kw[:], vaug[:, h, :], r=[kk, "vaug"], w=["pb%d" % pi])
                            sl = st[:, 1024 + h * 257:1024 + (h + 1) * 257]
                            c.stt(sl, sl, S["dec"][:, mo_ + h:mo_ + h + 1], pbs[pi][:, 0:257], ALU.mult, ALU.add,
                                  r=["st", k("dec"), "pb%d" % pi], w=["st"])
                        zo = C32["zf"] if isM else C32["zb"]
                        for h in range(4):
                            kw = kwL[h % 2]
                            kk = "kw%d" % (h % 2)
                            c.ts(kw[:], rk[:, h * 128:(h + 1) * 128], c32[:, zo + h:zo + h + 1], None, ALU.mult,
                                 r=[kvkey, "c32"], w=[kk])
                            pi = 6 + (h % 2)
                            c.mm(pbs[pi][:, 0:256], kw[:], rv[:, h * 256:(h + 1) * 256], r=[kk, kvkey], w=["pb%d" % pi])
                            sl = st[:, 2052 + h * 256:2052 + (h + 1) * 256]
                            c.stt(sl, sl, float(math.exp(128 * LOGG[h])), pbs[pi][:, 0:256], ALU.mult, ALU.add,
                                  r=["st", "pb%d" % pi], w=["st"])
                        if (isM and ch == NCHS - 1) or ((not isM) and ch == NCHS):
                            c.ts(st[:], st[:], flag[:, 0:1], None, ALU.mult, r=["st", "flag"], w=["st"])
                        c.cp(stbL[pout][:], st[:], r=["st"], w=["stb%d" % pout], eng="act")
                        if not isM:
                            return

                        pv = pbs[2][:].bitcast(BF16)
                        for kk_ in range(4):
                            src = xact[:, 1024 + kk_ * 128:1024 + (kk_ + 1) * 128]
                            c.tr(pv[:, kk_ * 128:(kk_ + 1) * 128], src, ident, r=[xk, "c16"], w=["pb2"])
                        c.cp(bcT[:], pv[:, 0:512].rearrange("p (k q) -> p k q", k=4), r=["pb2"], w=["bcT"], eng="act")
                        for grp, base in enumerate([OFF["mq"] - 1536, OFF["rq"] - 1536]):
                            for kk_ in range(8):
                                src = at[:, base + kk_ * 128:base + (kk_ + 1) * 128]
                                c.tr(pv[:, kk_ * 128:(kk_ + 1) * 128], src, ident, r=["at", "c16"], w=["pb2"])
                            c.cp(qkT[:, grp * 8:grp * 8 + 8, :], pv[:, :].rearrange("p (k q) -> p k q", k=8),
                                 r=["pb2"], w=["qkT"], eng="act")
                        pg = pbs[3]
                        for g in range(2):
                            c.mm(pg[:, 128 + g * 128:256 + g * 128], bcT[:, g, :], bcT[:, 2 + g, :], r=["bcT"], w=["pb3g"])
                        for g in range(2):
                            c.tt(Mb[:, 8 * g:8 * g + 8, :], W[:, 8 * g:8 * g + 8, :],
                                 pg[:, 128 + g * 128:256 + g * 128].unsqueeze(1).to_broadcast([128, 8, 128]),
                                 ALU.mult, r=[wk, "pb3g"], w=["Mb"])
                        for h in range(16):
                            pyb = pbs[4 + h // 8]
                            c.mm(pyb[:, (h % 8) * 64:(h % 8 + 1) * 64], Mb[:, h, :], xact[:, h * 64:(h + 1) * 64],
                                 r=["Mb", xk], w=["pb%d" % (4 + h // 8)])
                        f1v = f1[:].rearrange("p (h d) -> p h d", h=16)
                        f2v = f2[:].rearrange("p (h d) -> p h d", h=16)
                        c.tt(f1v, xact[:, 0:1024].rearrange("p (h d) -> p h d", h=16),
                             dsk[:].unsqueeze(2).to_broadcast([128, 16, 64]), ALU.mult, r=[xk, "ltab"], w=["f1"], eng="pool")
                        for g in range(2):
                            c.tt(f1[:, g * 512:(g + 1) * 512], f1[:, g * 512:(g + 1) * 512], pbs[4 + g][:, :], ALU.add,
                                 r=["f1", "pb%d" % (4 + g)], w=["f1"])
                        for d_, (stsrc, skey, eo) in enumerate([(stb, sk, 0), (sbl, "sbl", 20)]):
                            for g in range(2):
                                c.mm(pbs[4 + g][:, :], bcT[:, 2 + g, :], stsrc[:, g * 512:(g + 1) * 512],
                                     r=["bcT", skey], w=["pb%d" % (4 + g)])
                            for g in range(2):
                                c.tt(f2v[:, 8 * g:8 * g + 8, :], pbs[4 + g][:, :].rearrange("p (h d) -> p h d", h=8),
                                     S["ecs"][:, eo + 8 * g:eo + 8 * g + 8].unsqueeze(2).to_broadcast([128, 8, 64]), ALU.mult,
                                     r=["pb%d" % (4 + g), k("ecs")], w=["f2"])
                            c.tt(f1[:], f1[:], f2[:], ALU.add, r=["f1", "f2"], w=["f1"], eng="pool")
                        c.act(f2[:], at[:, 0:1024], AF.Silu, r=["at"], w=["f2"])
                        c.tt(f1[:], f1[:], f2[:], ALU.mult, r=["f1", "f2"], w=["f1"])
                        c.act(jk1[:], f1[:], AF.Square, r=["f1"], w=["jk1", "ss1"], accum_out=ss1[:])
                        c.ts(ss1[:], ss1[:], 1.0 / D, EPS, ALU.mult, ALU.add, r=["ss1"], w=["ss1"])
                        c.act(ss1[:], ss1[:], AF.Sqrt, r=["ss1"], w=["ss1"])
                        c.recip(ss1[:], ss1[:], r=["ss1"], w=["ss1"])
                        c.stt(mot[:, 0:1024], f1[:], ss1[:], nga[:], ALU.mult, ALU.mult, r=["f1", "ss1", "ltab"], w=["motA"])

                        for h in range(4):
                            c.mm(pbs[1][:, h * 128:(h + 1) * 128], qkT[:, 4 + h, :], qkT[:, h, :], r=["qkT"], w=["pb1"])
                        p03 = pbs[1][:, :].rearrange("p (h l) -> p h l", h=4)
                        c.tt(Pm[:, 0:4, :], p03, W[:, 16:20, :], ALU.mult, r=["pb1", wk], w=["Pm"])
                        c.tt(Pm[:, 4:8, :], p03, W[:, 36:40, :], ALU.mult, r=["pb1", wk], w=["Pm"])
                        for h in range(4):
                            for d_ in range(2):
                                cc = 16 + h if d_ == 0 else 36 + h
                                nd = ndL[d_]
                                nk = "nd%d" % d_
                                rr = rrL[d_]
                                rk_ = "rr%d" % d_
                                pvb = pbs[6]
                                pxb = pbs[7]
                                stsrc = stb if d_ == 0 else sbl
                                skey = sk if d_ == 0 else "sbl"
                                c.mm(pvb[:, 0:257], Pm[:, 4 * d_ + h, :], vaug[:, h, :], r=["Pm", "vaug"], w=["pb6"])
                                c.mm(pxb[:, 0:257], qkT[:, h, :], stsrc[:, 1024 + h * 257:1024 + (h + 1) * 257],
                                     r=["qkT", skey], w=["pb7"])
                                c.ts(nd[:], pxb[:, 0:257], S["ecs"][:, cc:cc + 1], None, ALU.mult,
                                     r=["pb7", k("ecs")], w=[nk])
                                c.tt(nd[:], nd[:], pvb[:, 0:257], ALU.add, r=[nk, "pb6"], w=[nk])
                                c.act(rr[:], nd[:, 256:257], AF.Abs, r=[nk], w=[rk_])
                                c.ts(rr[:], rr[:], 1.0, None, ALU.max, r=[rk_], w=[rk_])
                                c.recip(rr[:], rr[:], r=[rk_], w=[rk_])
                                if d_ == 0:
                                    c.ts(hacc[:, h, :], nd[:, 0:256], rr[:], None, ALU.mult, r=[nk, rk_], w=["hacc"])
                                else:
                                    c.stt(hacc[:, h, :], nd[:, 0:256], rr[:], hacc[:, h, :], ALU.mult, ALU.add,
                                          r=[nk, rk_, "hacc"], w=["hacc"])
                        for h in range(4):
                            c.act(jk2[:], hacc[:, h, :], AF.Square, r=["hacc"], w=["jk2", "ss4"], accum_out=ss4[:, h:h + 1])
                        c.ts(ss4[:], ss4[:], 1.0 / 256, EPS, ALU.mult, ALU.add, r=["ss4"], w=["ss4"])
                        c.act(ss4[:], ss4[:], AF.Sqrt, r=["ss4"], w=["ss4"])
                        c.recip(ss4[:], ss4[:], r=["ss4"], w=["ss4"])
                        c.tt(hacc[:], hacc[:], ss4[:].unsqueeze(2).to_broadcast([128, 4, 256]), ALU.mult, r=["hacc", "ss4"], w=["hacc"])
                        hf = hacc[:].rearrange("p h v -> p (h v)")
                        c.tt(hf, hf, ngb[:], ALU.mult, r=["hacc", "ltab"], w=["hacc"], eng="pool")
                        c.act(g1[:], at[:, OFF["mo"] - 1536:OFF["mo"] - 1536 + 1024], AF.Sigmoid, r=["at"], w=["g1"])
                        c.tt(mot[:, 1024:2048], hf, g1[:], ALU.mult, r=["hacc", "g1"], w=["motB"])

                        for h in range(4):
                            c.mm(pbs[1][:, h * 128:(h + 1) * 128], qkT[:, 12 + h, :], qkT[:, 8 + h, :], r=["qkT"], w=["pb1"])
                        c.tt(Pr[:].rearrange("p h l -> p (h l)"), pbs[1][:, :], c32[:, C32["dtab"]:C32["dtab"] + 512], ALU.mult,
                             r=["pb1", "c32"], w=["Pr"])
                        c.tt(qxf[:].rearrange("p h l -> p (h l)"), qkT[:, 8:12, :].rearrange("p h l -> p (h l)"),
                             c32[:, C32["xif"]:C32["xif"] + 512], ALU.mult, r=["qkT", "c32"], w=["qxf"], eng="pool")
                        c.tt(qxb[:].rearrange("p h l -> p (h l)"), qkT[:, 8:12, :].rearrange("p h l -> p (h l)"),
                             c32[:, C32["xib"]:C32["xib"] + 512], ALU.mult, r=["qkT", "c32"], w=["qxb"], eng="pool")
                        for h in range(4):
                            pvb = pbs[4 + h % 2]
                            pk = "pb%d" % (4 + h % 2)
                            c.mm(pvb[:, 0:256], Pr[:, h, :], rv[:, h * 256:(h + 1) * 256], start=True, stop=False,
                                 r=["Pr", "at"], w=[pk])
                            c.mm(pvb[:, 0:256], qxf[:, h, :], stb[:, 2052 + h * 256:2052 + (h + 1) * 256], start=False, stop=False,
                                 r=["qxf", sk], w=[pk])
                            c.mm(pvb[:, 0:256], qxb[:, h, :], sbl[:, 2052 + h * 256:2052 + (h + 1) * 256], start=False, stop=True,
                                 r=["qxb", "sbl"], w=[pk])
                            c.cp(ry[:, h * 256:(h + 1) * 256], pvb[:, 0:256], r=[pk], w=["ry"], eng="act")
                            c.act(jk3[:], ry[:, h * 256:(h + 1) * 256], AF.Square, r=["ry"], w=["jk3", "ss4r"],
                                  accum_out=ss4r[:, h:h + 1])
                        c.ts(ss4r[:], ss4r[:], 1.0 / 256, EPS, ALU.mult, ALU.add, r=["ss4r"], w=["ss4r"])
                        c.act(ss4r[:], ss4r[:], AF.Sqrt, r=["ss4r"], w=["ss4r"])
                        c.recip(ss4r[:], ss4r[:], r=["ss4r"], w=["ss4r"])
                        c.tt(ry[:].rearrange("p (h v) -> p h v", h=4), ry[:].rearrange("p (h v) -> p h v", h=4),
                             ss4r[:].unsqueeze(2).to_broadcast([128, 4, 256]), ALU.mult, r=["ry", "ss4r"], w=["ry"])
                        c.tt(ry[:], ry[:], ngc[:], ALU.mult, r=["ry", "ltab"], w=["ry"], eng="pool")
                        c.act(g2[:], at[:, OFF["rg"] - 1536:OFF["rg"] - 1536 + 1024], AF.Silu, r=["at"], w=["g2"])
                        c.tt(mot[:, 2048:3072], ry[:], g2[:], ALU.mult, r=["ry", "g2"], w=["motC"])
                        c.dma(MO[r0:r0 + 128, :], mot[:], r=["motA", "motB", "motC"], w=["MO", "motA", "motB", "motC"], eng="pool")

                    n_ = len(order)
                    loadsA(order[0], 0)
                    prepA(order[0], 0)
                    for i, ch in enumerate(order):
                        sA = []
                        if i + 1 < n_:
                            loadsA(order[i + 1], (i + 1) % 2)
                            sA = c.capture(lambda: prepA(order[i + 1], (i + 1) % 2))
                        sB = c.capture(lambda: mainA(ch, i % 2, i % 2, (i + 1) % 2))
                        if INTERLEAVE:
                            c.replay_interleaved([sA, sB])
                        else:
                            for o_ in sA + sB:
                                c.op(*o_)
                    c.emit()
                    c.pes = None

            with ExitStack() as pes:
              if "D" in phases:
                c.pes = pes
                nfg = c.sb("nfg", [128, D])
                fng = c.sb("fng", [128, D])
                bcast_load(nfg[:], vecs["norm_ffn_g"][l], "ltab")
                bcast_load(fng[:], fin_g, "ltab")
                wt = [c.sb("wtd%d" % i, [128, 22, 512], BF16) for i in range(2)]
                mo4 = c.sb("mo4", [128, 4, 3072], BF16)
                gt4 = c.sb("gt4", [128, 4, 3072], BF16)
                x4 = c.sb("x4", [128, 4, D], F32)
                brT = c.sb("brT", [128, 8, 512], BF16)
                mrg = c.sb("mrg", [128, 4, D], F32)
                tmpd = c.sb("tmpd", [128, 512], F32)
                mrb = c.sb("mrb", [128, D], BF16)
                hT = c.sb("hT", [128, 8, 512], BF16)
                actT = c.sb("actT", [128, 22, 512], BF16)
                tmpf = c.sb("tmpfd", [128, D], F32)
                ss = c.sb("ssd_", [128, 1], F32)
                yo = c.sb("yo", [128, D], F32)
                for b in range(NBLK):
                    r0 = b * 512
                    c.dma(mo4[:], MO[r0:r0 + 512, :].rearrange("(t p) e -> p t e", p=128), r=["MO"], w=["mo4"])
                    c.dma(gt4[:], GT[r0:r0 + 512, :].rearrange("(t p) e -> p t e", p=128), r=["GT"], w=["gt4"])
                    c.dma(x4[:], xsrc[r0:r0 + 512, :].rearrange("(t p) e -> p t e", p=128), r=["XS"], w=["x4"])
                    for br in range(3):
                        for t in range(4):
                            transpose_to(brT, "brT", mo4[:, t, br * 1024:(br + 1) * 1024], "mo4", 8, t)

                        def evacBr(c0, cw, t, ps, pkey, br=br):
                            gsl = gt4[:, t, br * 1024 + c0:br * 1024 + c0 + cw]
                            if br == 0:
                                c.tt(mrg[:, t, c0:c0 + cw], ps, gsl, ALU.mult, r=[pkey, "gt4"], w=["mrg"])
                            else:
                                c.tt(tmpd[:, 0:cw], ps, gsl, ALU.mult, r=[pkey, "gt4"], w=["tmpd"])
                                c.tt(mrg[:, t, c0:c0 + cw], mrg[:, t, c0:c0 + cw], tmpd[:, 0:cw], ALU.add,
                                     r=["mrg", "tmpd"], w=["mrg"], eng="pool")

                        dense(brT, "brT", 8, wb_br[l][br], D, evacBr, wt=wt)
                    for t in range(4):
                        c.cp(mrb[:], mrg[:, t, :], r=["mrg"], w=["mrb"], eng="act")
                        transpose_to(brT, "brT", mrb, "mrb", 8, t)

                    def evacO(c0, cw, t, ps, pkey):
                        c.tt(x4[:, t, c0:c0 + cw], x4[:, t, c0:c0 + cw], ps, ALU.add, r=["x4", pkey], w=["x4"])

                    dense(brT, "brT", 8, wb_out[l], D, evacO, wt=wt)
                    for t in range(4):
                        rmsnorm_rows(x4[:, t, :], "x4", nfg[:], mrb[:], "mrb", tmpf[:], ss[:])
                        transpose_to(hT, "hT", mrb, "mrb", 8, t)

                    def evacFg(c0, cw, t, ps, pkey):
                        c.act(actT[:, c0 // 128, :], ps, AF.Silu, r=[pkey], w=["actT"])

                    dense(hT, "hT", 8, wb_fg[l], FF, evacFg, fmajor=True, wt=wt)

                    def evacFu(c0, cw, t, ps, pkey):
                        c.tt(actT[:, c0 // 128, :], ps, actT[:, c0 // 128, :], ALU.mult, r=[pkey, "actT"], w=["actT"])

                    dense(hT, "hT", 8, wb_fu[l], FF, evacFu, fmajor=True, wt=wt)

                    def evacFd(c0, cw, t, ps, pkey):
                        c.tt(x4[:, t, c0:c0 + cw], x4[:, t, c0:c0 + cw], ps, ALU.add, r=["x4", pkey], w=["x4"])

                    dense(actT, "actT", 22, wb_fd[l], D, evacFd, wt=wt)
                    if last:
                        for t in range(4):
                            rmsnorm_rows(x4[:, t, :], "x4", fng[:], yo[:], "yo", tmpf[:], ss[:])
                            c.dma(y_out[r0 + t * 128:r0 + (t + 1) * 128, :], yo[:], r=["yo"], w=["y"], eng="act")
                    else:
                        c.dma(XS[r0:r0 + 512, :].rearrange("(t p) e -> p t e", p=128), x4[:], r=["x4"], w=["XS"], eng="act")
                c.emit()
                c.pes = None
    return nc


_PERM = None


def _perm():
    global _PERM
    if _PERM is None:
        o = dict(z=0, xbc=1024, dt=2560, mq=2592, mk=3104, mv=3616, mo=4640, mg=5664, rq=5680, rk=6192, rv=6704, rg=7728)
        n = dict(z=1024, xbc=1536, dt=32, mq=512, mk=512, mv=1024, mo=1024, mg=16, rq=512, rk=512, rv=1024, rg=1024)
        idx = []
        for k in ("z", "xbc", "mq", "mk", "mv", "mo", "rq", "rk", "rv", "rg", "dt"):
            idx.extend(range(o[k], o[k] + n[k]))
        mgp = [4, 5, 6, 7, 12, 13, 14, 15, 0, 1, 2, 3, 8, 9, 10, 11]
        idx.extend([o["mg"] + i for i in mgp])
        _PERM = (np.array(idx), np.array(mgp))
    return _PERM


def run_cores(core_x, core_flag, core_pos, weights, DEPTH, SEG, debug=False, phases="PSMD", trace=False):
    perm, mgp = _perm()
    cb16, cf32 = host_consts()
    nc = build(DEPTH, SEG, debug=debug, phases=phases)
    w = weights
    shared = {
        "cb16": cb16, "cf32": cf32,
        "w_in": np.ascontiguousarray(w["w_in"][:, :, perm]),
        "w_gate": w["w_gate"], "w_branch": w["w_branch"], "w_out": w["w_out"],
        "w_ffn_gate": w["w_ffn_gate"], "w_ffn_up": w["w_ffn_up"], "w_ffn_down": w["w_ffn_down"],
        "norm_mix_g": w["norm_mix_g"], "b_gate": w["b_gate"],
        "conv_w": np.ascontiguousarray(w["conv_w"].reshape(DEPTH, -1)), "conv_b": w["conv_b"],
        "ssd_a_log": np.ascontiguousarray(w["ssd_a_log"].reshape(DEPTH, -1)),
        "ssd_dt_bias": np.ascontiguousarray(w["ssd_dt_bias"].reshape(DEPTH, -1)),
        "ssd_d": w["ssd_d"], "ssd_norm_g": w["ssd_norm_g"],
        "mlstm_gate_bias": np.ascontiguousarray(w["mlstm_gate_bias"].reshape(DEPTH, -1)[:, mgp]),
        "mlstm_norm_g": w["mlstm_norm_g"], "ret_norm_g": w["ret_norm_g"], "norm_ffn_g": w["norm_ffn_g"],
        "final_norm_g": w["final_norm_g"],
    }
    shared = {k: np.ascontiguousarray(np.asarray(v)) for k, v in shared.items()}
    in_maps = []
    for i in range(len(core_x)):
        m = dict(shared)
        m["x"] = np.ascontiguousarray(core_x[i], dtype=np.float32)
        m["flag"] = np.full((128, 1), core_flag[i], np.float32)
        m["rope"] = rope_tables(SEG, core_pos[i])
        in_maps.append(m)
    res = run_bass_kernel_spmd(nc, in_maps, core_ids=list(range(len(core_x))), trace=trace)
    if trace:
        print("EXEC_TIME_NS", res.exec_time_ns, "nops", nc.n_instructions if hasattr(nc, "n_instructions") else None)
    return res.results


def kernel(**inputs):
    xp = np.asarray(inputs["x_prompt"], dtype=np.float32)
    xsm = np.asarray(inputs["x_sample"], dtype=np.float32)
    SEG = 4096
    DEPTH = 4
    core_x, core_flag, core_pos = [], [], []
    for i in range(4):
        core_x.append(xsm[i])
        core_flag.append(1.0)
        core_pos.append(np.arange(8192))
    for i in range(2):
        core_x.append(np.concatenate([xp[2 * i], xp[2 * i + 1]], axis=0))
        core_flag.append(0.0)
        core_pos.append(np.concatenate([np.arange(4096), np.arange(4096)]))
    for i in range(2):
        core_x.append(core_x[4 + i])
        core_flag.append(0.0)
        core_pos.append(core_pos[4 + i])
    weights = {k: np.asarray(v) for k, v in inputs.items() if k not in ("x_prompt", "x_sample")}
    res = run_cores(core_x, core_flag, core_pos, weights, DEPTH, SEG)
    y_sample = np.stack([res[i]["y"] for i in range(4)], axis=0).astype(np.float32)
    yp = []
    for i in range(2):
        yy = res[4 + i]["y"]
        yp.append(yy[0:4096])
        yp.append(yy[4096:8192])
    y_prompt = np.stack(yp, axis=0).astype(np.float32)
    return (y_prompt, y_sample)
```

```python
import math
import numpy as np
import ml_dtypes
from contextlib import ExitStack
import concourse.bass as bass
import concourse.mybir as mybir
from concourse.bass_utils import run_bass_kernel_spmd

F32 = mybir.dt.float32
BF16 = mybir.dt.bfloat16
AF = mybir.ActivationFunctionType
ALU = mybir.AluOpType

ENGS = ("pe", "act", "dve", "pool", "sp")
NDMA = 24
INTERLEAVE = True
NOSELF = False
EPS = 1e-6
D = 1024
FF = 2816
NEG = -60000.0
QS = 128 ** -0.5
A_COLS = 8704
OFF = dict(z=0, xbc=1024, mq=2560, mk=3072, mv=3584, mo=4608, rq=5632, rk=6144, rv=6656, rg=7680)
LOGG = [math.log(1.0 - math.exp(v)) for v in
        np.linspace(math.log(1.0 / 32.0), math.log(1.0 / 512.0), 4, dtype=np.float32).astype(np.float64)]


class Ctx:
    def __init__(self, nc, es):
        self.nc = nc
        self.es = es
        self.q = {e: [] for e in ENGS}
        self.cnt = {e: 0 for e in ENGS}
        self.seen = {e: {} for e in ENGS}
        self.lastw = {}
        self.rd = {}
        self.semh = {}
        for e in ENGS:
            self.semh[e] = es.enter_context(nc.semaphore("s_" + e))
        for i in range(NDMA):
            self.semh[("d", i)] = es.enter_context(nc.semaphore("s_d%d" % i))
        self.nsw = 0
        self.dma_cnt = [0] * NDMA
        self.dma_last = [None] * NDMA
        self.dma_rr = 0
        self.nops = 0
        self.pes = None
        self.cap = None

    def capture(self, f):
        self.cap = []
        f()
        out = self.cap
        self.cap = None
        return out

    def replay_interleaved(self, streams):
        streams = [s_ for s_ in streams if s_]
        if not streams:
            return
        n = max(len(s_) for s_ in streams)
        pos = [0] * len(streams)
        for k in range(1, n + 1):
            for si, s_ in enumerate(streams):
                tgt = (k * len(s_) + n - 1) // n
                while pos[si] < tgt:
                    self.op(*s_[pos[si]])
                    pos[si] += 1

    def sb(self, name, shape, dtype=F32):
        st = self.pes if self.pes is not None else self.es
        self.nsb = getattr(self, "nsb", 0) + 1
        return st.enter_context(self.nc.sbuf_tensor("sb%d_%s" % (self.nsb, name), list(shape), dtype))

    def op(self, eng, fn, reads=(), writes=(), dma=False):
        if self.cap is not None:
            self.cap.append((eng, fn, tuple(reads), tuple(writes), dma))
            return
        evs = {}

        def add(ev):
            if ev is None:
                return
            s, v = ev
            if evs.get(s, 0) < v:
                evs[s] = v

        for k in reads:
            add(self.lastw.get(k))
        for k in writes:
            add(self.lastw.get(k))
            for s, v in self.rd.get(k, {}).items():
                add((s, v))
        if dma and eng == "pool":
            key = ("sw", self.nsw)
            self.nsw += 1
            self.semh[key] = self.es.enter_context(self.nc.semaphore("s_sw%d" % key[1]))
            ev = (key, 16)
        elif dma:
            i = self.dma_rr
            self.dma_rr = (i + 1) % NDMA
            add(self.dma_last[i])
            self.dma_cnt[i] += 16
            ev = (("d", i), self.dma_cnt[i])
            self.dma_last[i] = ev
        else:
            self.cnt[eng] += 1
            ev = (eng, self.cnt[eng])
        waits = []
        seen = self.seen[eng]
        for s, v in evs.items():
            if eng == "pe" and s == "pe":
                continue
            if NOSELF and s == eng:
                continue
            if seen.get(s, 0) < v:
                waits.append((s, v))
                seen[s] = v
        self.q[eng].append((fn, waits, ev, dma))
        for k in writes:
            self.lastw[k] = ev
            self.rd[k] = {}
        for k in reads:
            d = self.rd.setdefault(k, {})
            s, v = ev
            if d.get(s, 0) < v:
                d[s] = v
        self.nops += 1

    def dma(self, out, in_, r=(), w=(), eng="sp"):
        self.op(eng, lambda e: e.dma_start(out=out, in_=in_), r, w, dma=True)

    def mm(self, out, lhsT, rhs, start=True, stop=True, r=(), w=()):
        self.op("pe", lambda e: e.matmul(out, lhsT, rhs, start=start, stop=stop), r, w)

    def tr(self, out, in_, ident, r=(), w=()):
        self.op("pe", lambda e: e.transpose(out, in_, ident), r, w)

    def act(self, out, in_, func, r=(), w=(), **kw):
        self.op("act", lambda e: e.activation(out, in_, func, **kw), r, w)

    def tt(self, out, in0, in1, op, r=(), w=(), eng="dve"):
        self.op(eng, lambda e: e.tensor_tensor(out, in0, in1, op), r, w)

    def ts(self, out, in0, s1, s2, op0, op1=None, r=(), w=(), eng="dve"):
        if op1 is None:
            self.op(eng, lambda e: e.tensor_scalar(out, in0, s1, None, op0), r, w)
        else:
            self.op(eng, lambda e: e.tensor_scalar(out, in0, s1, s2, op0, op1), r, w)

    def stt(self, out, in0, sc, in1, op0, op1, r=(), w=()):
        self.op("dve", lambda e: e.scalar_tensor_tensor(out, in0, sc, in1, op0, op1), r, w)

    def cp(self, out, in_, r=(), w=(), eng="dve"):
        if eng == "act":
            self.op("act", lambda e: e.copy(out, in_), r, w)
        else:
            self.op(eng, lambda e: e.tensor_copy(out, in_), r, w)

    def memset(self, ap, val, w=(), eng="pool"):
        self.op(eng, lambda e: e.memset(ap, val), (), w)

    def recip(self, out, in_, r=(), w=()):
        self.op("dve", lambda e: e.reciprocal(out, in_), r, w)

    def emit(self, final=False):
        nc = self.nc
        fence = []
        for key in list(self.semh.keys()):
            if isinstance(key, tuple) and key[0] == "sw" and key not in getattr(self, "swdone", set()):
                fence.append((key, 16))
                self.swdone = getattr(self, "swdone", set()) | {key}
        for i in range(NDMA):
            if self.dma_cnt[i]:
                fence.append((("d", i), self.dma_cnt[i]))
        for e in ENGS:
            if self.cnt[e]:
                fence.append((e, self.cnt[e]))
        qs = self.q
        with nc.Block() as block:
            def run(name, e):
                for fn, waits, ev, dma in qs[name]:
                    for s, v in waits:
                        e.wait_ge(self.semh[s], v)
                    ins = fn(e)
                    ins.then_inc(self.semh[ev[0]], 16 if dma else 1)
                for s, v in fence:
                    if s != name:
                        e.wait_ge(self.semh[s], v)

            @block.tensor
            def _(e):
                run("pe", e)

            @block.scalar
            def _(e):
                run("act", e)

            @block.vector
            def _(e):
                run("dve", e)

            @block.gpsimd
            def _(e):
                run("pool", e)

            @block.sync
            def _(e):
                run("sp", e)
        self.q = {e: [] for e in ENGS}
        for e in ENGS:
            for s, v in fence:
                self.seen[e][s] = v
        self.lastw = {}
        self.rd = {}


def host_consts():
    j = np.arange(128)
    tri = (j[:, None] <= j[None, :]).astype(np.float32)
    triT = (j[:, None] >= j[None, :]).astype(np.float32)
    ident = np.eye(128, dtype=np.float32)
    ones = np.ones((128, 128), np.float32)
    nmf = np.where(j[:, None] <= j[None, :], 0.0, NEG).astype(np.float32)
    nmb = np.where(j[:, None] >= j[None, :], 0.0, NEG).astype(np.float32)
    cb16 = np.concatenate([ident, tri, triT, ones, np.tile(nmf, (1, 4)), np.tile(nmb, (1, 4))], axis=1)
    lg = np.array(LOGG, np.float64)
    dist = np.abs(j[:, None] - j[None, :]).astype(np.float64)
    dtab = np.concatenate([np.exp(dist * lg[h]) for h in range(4)], axis=1)
    xif = np.tile(np.concatenate([np.exp((j + 1) * lg[h]) for h in range(4)])[None, :], (128, 1))
    xib = np.tile(np.concatenate([np.exp((128 - j) * lg[h]) for h in range(4)])[None, :], (128, 1))
    zf = np.stack([np.exp((127 - j) * lg[h]) for h in range(4)], axis=1)
    zb = np.stack([np.exp(j * lg[h]) for h in range(4)], axis=1)
    cf32 = np.concatenate([dtab, xif, xib, zf, zb, ident], axis=1).astype(np.float32)
    return cb16.astype(ml_dtypes.bfloat16), cf32


C16 = dict(ident=0, tri=128, triT=256, ones=384, nmf=512, nmb=1024)
C32 = dict(dtab=0, xif=512, xib=1024, zf=1536, zb=1540, identf=1544)
NC16 = 1536
NC32 = 1672


def rope_tables(seglen, nseg_positions):
    half = 64
    inv = (10000.0 ** (-np.arange(half, dtype=np.float32) / half)).astype(np.float32)
    pos = nseg_positions.astype(np.float32)
    ang = pos[:, None] * inv[None, :]
    cos = np.cos(ang).astype(np.float32)
    sin = np.sin(ang).astype(np.float32)
    return np.concatenate([cos, sin, cos * np.float32(QS), sin * np.float32(QS)], axis=1)


def build(DEPTH, SEG, debug=False, phases="PSMD"):
    NCHS = SEG // 128
    NCH = 2 * NCHS
    NT = 2 * SEG
    NBLK = NT // 512
    XBR = 2 * (SEG + 3)
    nc = bass.Bass("TRN2", target_bir_lowering=False)

    def din(name, shape, dt=F32):
        return nc.dram_tensor(name, list(shape), dt, kind="ExternalInput").ap()

    def dscr(name, shape, dt):
        return nc.dram_tensor(name, list(shape), dt, kind=("ExternalOutput" if debug else "Internal")).ap()

    x_in = din("x", [NT, D])
    flag_in = din("flag", [128, 1])
    rope_in = din("rope", [NT, 256])
    cb16_in = din("cb16", [128, NC16], BF16)
    cf32_in = din("cf32", [128, NC32])
    w_in = din("w_in", [DEPTH, D, 8752])
    w_gate = din("w_gate", [DEPTH, D, 3072])
    w_br = din("w_branch", [DEPTH, 3, D, D])
    w_out = din("w_out", [DEPTH, D, D])
    w_fg = din("w_ffn_gate", [DEPTH, D, FF])
    w_fu = din("w_ffn_up", [DEPTH, D, FF])
    w_fd = din("w_ffn_down", [DEPTH, FF, D])
    vecs = {}
    for name, n in [("norm_mix_g", D), ("b_gate", 3072), ("conv_w", 4 * 1536), ("conv_b", 1536),
                    ("ssd_a_log", 32), ("ssd_dt_bias", 32), ("ssd_d", 16), ("ssd_norm_g", D),
                    ("mlstm_gate_bias", 16), ("mlstm_norm_g", D), ("ret_norm_g", D), ("norm_ffn_g", D)]:
        vecs[name] = din(name, [DEPTH, n])
    fin_g = din("final_norm_g", [D])
    y_out = nc.dram_tensor("y", [NT, D], F32, kind="ExternalOutput").ap()

    wb_in = dscr("wb_in", [DEPTH, D, 8752], BF16) if False else nc.dram_tensor("wb_in", [DEPTH, D, 8752], BF16, kind="Internal").ap()
    wb_gate = nc.dram_tensor("wb_gate", [DEPTH, D, 3072], BF16, kind="Internal").ap()
    wb_br = nc.dram_tensor("wb_br", [DEPTH, 3, D, D], BF16, kind="Internal").ap()
    wb_out = nc.dram_tensor("wb_out", [DEPTH, D, D], BF16, kind="Internal").ap()
    wb_fg = nc.dram_tensor("wb_fg", [DEPTH, D, FF], BF16, kind="Internal").ap()
    wb_fu = nc.dram_tensor("wb_fu", [DEPTH, D, FF], BF16, kind="Internal").ap()
    wb_fd = nc.dram_tensor("wb_fd", [DEPTH, FF, D], BF16, kind="Internal").ap()
    XS = dscr("XS", [NT, D], F32)
    A_ = dscr("A_", [NT, A_COLS], BF16)
    XB = dscr("XB", [XBR, 1536], BF16)
    G_ = dscr("G_", [NT, 48], F32)
    GT = dscr("GT", [NT, 3072], BF16)
    SB = dscr("SBst", [NCH, 128, 3076], BF16)
    MO = dscr("MO", [NT, 3072], BF16)

    def xbrow(tok):
        s = tok // SEG
        return s * (SEG + 3) + 2 + (tok - s * SEG)

    with ExitStack() as es:
        c = Ctx(nc, es)
        c16 = c.sb("c16", [128, NC16], BF16)
        c32 = c.sb("c32", [128, NC32], F32)
        flag = c.sb("flag", [128, 1], F32)
        pbs = [es.enter_context(nc.psum_tensor("pb%d" % i, [128, 512], F32)) for i in range(8)]

        def C16s(name, n=128):
            return c16[:, C16[name]:C16[name] + n]

        ident = C16s("ident")
        identf = c32[:, C32["identf"]:C32["identf"] + 128]

        with ExitStack() as pes:
            c.pes = pes
            c.dma(c16[:], cb16_in, w=["c16"])
            c.dma(c32[:], cf32_in, w=["c32"])
            c.dma(flag[:], flag_in, w=["flag"])
            for src, dst in [(w_in, wb_in), (w_gate, wb_gate), (w_out, wb_out), (w_fg, wb_fg),
                             (w_fu, wb_fu), (w_fd, wb_fd)]:
                for l in range(DEPTH):
                    s2 = src[l].rearrange("k c -> (k c)").rearrange("(r e) -> r e", e=1024)
                    d2 = dst[l].rearrange("k c -> (k c)").rearrange("(r e) -> r e", e=1024)
                    c.dma(d2, s2, w=["wcast"], eng="pool")
            for l in range(DEPTH):
                s2 = w_br[l].rearrange("b k c -> (b k c)").rearrange("(r e) -> r e", e=1024)
                d2 = wb_br[l].rearrange("b k c -> (b k c)").rearrange("(r e) -> r e", e=1024)
                c.dma(d2, s2, w=["wcast"], eng="pool")
            zt = c.sb("zt", [4, 1536], BF16)
            c.memset(zt[:], 0.0, w=["zt"])
            for s in range(2):
                c.dma(XB[s * (SEG + 3):s * (SEG + 3) + 2, :], zt[0:2, :], r=["zt"], w=["XBpad"])
                c.dma(XB[s * (SEG + 3) + 2 + SEG:s * (SEG + 3) + 3 + SEG, :], zt[0:1, :], r=["zt"], w=["XBpad"])
            c.emit()
            c.pes = None

        def bcast_load(t, src1d, key):
            c.dma(t, src1d.partition_broadcast(128), w=[key])

        def rmsnorm_rows(xt, key_x, gt, out, key_out, tmpf, ss, n=D):
            c.act(tmpf, xt, AF.Square, r=[key_x], w=["nrm_tmp", "nrm_ss"], accum_out=ss)
            c.act(ss, ss, AF.Ln, r=["nrm_ss"], w=["nrm_ss"], scale=1.0 / n, bias=EPS)
            c.act(ss, ss, AF.Exp, r=["nrm_ss"], w=["nrm_ss"], scale=-0.5)
            c.stt(out, xt, ss, gt, ALU.mult, ALU.mult, r=[key_x, "nrm_ss", "ltab"], w=[key_out])

        wcount = [0]

        DB = [0, 1, 3, 4, 5, 6, 7]
        pcount = [0]

        def dense(actT, akey, KT, wsrc, ncols, evac, fmajor=False, wt=None):
            wv = wsrc.rearrange("(kt p) c -> p kt c", p=128)
            ngrp = (ncols + 511) // 512

            def load(g):
                c0 = g * 512
                cw = min(512, ncols - c0)
                wi = wcount[0] % 2
                wcount[0] += 1
                c.dma(wt[wi][:, 0:KT, 0:cw], wv[:, :, c0:c0 + cw], r=["wcast"], w=["wt%d" % wi])
                return wi

            nxt = load(0)
            for g in range(ngrp):
                c0 = g * 512
                cw = min(512, ncols - c0)
                wi = nxt
                wtile = wt[wi]
                if g + 1 < ngrp:
                    nxt = load(g + 1)
                if not fmajor:
                    for t in range(4):
                        pi = DB[pcount[0] % len(DB)]
                        pcount[0] += 1
                        pb = pbs[pi]
                        for kt in range(KT):
                            c.mm(pb[:, 0:cw], actT[:, kt, t * 128:(t + 1) * 128], wtile[:, kt, 0:cw],
                                 start=(kt == 0), stop=(kt == KT - 1), r=[akey, "wt%d" % wi], w=["pb%d" % pi])
                        evac(c0, cw, t, pb[:, 0:cw], "pb%d" % pi)
                else:
                    for m in range(cw // 128):
                        pi = DB[pcount[0] % len(DB)]
                        pcount[0] += 1
                        pb = pbs[pi]
                        for kt in range(KT):
                            c.mm(pb[:, :], wtile[:, kt, m * 128:(m + 1) * 128], actT[:, kt, :],
                                 start=(kt == 0), stop=(kt == KT - 1), r=[akey, "wt%d" % wi], w=["pb%d" % pi])
                        evac(c0 + m * 128, 128, None, pb[:, :], "pb%d" % pi)

        def transpose_to(dstT, dkey, src, skey, ntile, t):
            pv = pbs[2][:].bitcast(BF16)
            for k0 in range(0, ntile, 8):
                n = min(8, ntile - k0)
                for k in range(n):
                    c.tr(pv[:, k * 128:(k + 1) * 128], src[:, (k0 + k) * 128:(k0 + k + 1) * 128], ident,
                         r=[skey, "c16"], w=["pb2"])
                c.cp(dstT[:, k0:k0 + n, t * 128:(t + 1) * 128],
                     pv[:, 0:n * 128].rearrange("p (k q) -> p k q", k=n), r=["pb2"], w=[dkey], eng="act")

        for l in range(DEPTH):
            xsrc = x_in if l == 0 else XS
            last = (l == DEPTH - 1)
            with ExitStack() as pes:
              if "P" in phases:
                c.pes = pes
                g_mix = c.sb("g_mix", [128, D])
                bg = c.sb("bg", [128, 3072])
                bcast_load(g_mix[:], vecs["norm_mix_g"][l], "ltab")
                bcast_load(bg[:], vecs["b_gate"][l], "ltab")
                xt = [c.sb("xt%d" % i, [128, D]) for i in range(2)]
                ut = c.sb("ut", [128, D], BF16)
                uT = c.sb("uT", [128, 8, 512], BF16)
                wt = [c.sb("wt%d" % i, [128, 8, 512], BF16) for i in range(2)]
                NOE = 6
                oev = [c.sb("oev%d" % i, [128, 512], BF16) for i in range(NOE)]
                oef = c.sb("oef", [128, 512], F32)
                og = c.sb("og", [128, 48], F32)
                rp = c.sb("rp", [128, 4, 256], F32)
                rt = [c.sb("rt%d" % i, [128, 4, 64], F32) for i in range(4)]
                tmpf = c.sb("tmpf", [128, D], F32)
                ss = c.sb("ss", [128, 1], F32)
                ecnt = [0]
                for b in range(NBLK):
                    r0 = b * 512
                    c.dma(rp[:], rope_in[r0:r0 + 512, :].rearrange("(t p) e -> p t e", p=128), w=["rp"])
                    for t in range(4):
                        xi = t % 2
                        c.dma(xt[xi][:], xsrc[r0 + t * 128:r0 + (t + 1) * 128, :], r=["XS"], w=["xt%d" % xi])
                        rmsnorm_rows(xt[xi][:], "xt%d" % xi, g_mix[:], ut[:], "ut", tmpf[:], ss[:])
                        transpose_to(uT, "uT", ut, "ut", 8, t)

                    def evacA(c0, cw, t, ps, pkey, r0=r0):
                        rows = slice(r0 + t * 128, r0 + (t + 1) * 128)
                        oi = ecnt[0] % NOE
                        ecnt[0] += 1
                        o = oev[oi]
                        okey = "oev%d" % oi
                        if c0 == OFF["mq"]:
                            c.act(o[:], ps, AF.Copy, r=[pkey], w=[okey], scale=QS)
                        elif c0 in (OFF["rq"], OFF["rk"]):
                            co = rp[:, t, 0:64] if c0 == OFF["rq"] else rp[:, t, 128:192]
                            si = rp[:, t, 64:128] if c0 == OFF["rq"] else rp[:, t, 192:256]
                            cob = co.unsqueeze(1).to_broadcast([128, 4, 64])
                            sib = si.unsqueeze(1).to_broadcast([128, 4, 64])
                            p3 = ps.rearrange("p (h d) -> p h d", h=4)
                            o3 = o[:].rearrange("p (h d) -> p h d", h=4)
                            x1 = p3[:, :, 0:64]
                            x2 = p3[:, :, 64:128]
                            c.tt(rt[0][:], x1, cob, ALU.mult, r=[pkey, "rp"], w=["rt0"])
                            c.tt(rt[1][:], x2, sib, ALU.mult, r=[pkey, "rp"], w=["rt1"])
                            c.tt(rt[2][:], x1, sib, ALU.mult, r=[pkey, "rp"], w=["rt2"])
                            c.tt(rt[3][:], x2, cob, ALU.mult, r=[pkey, "rp"], w=["rt3"])
                            c.tt(o3[:, :, 0:64], rt[0][:], rt[1][:], ALU.subtract, r=["rt0", "rt1"], w=[okey], eng="pool")
                            c.tt(o3[:, :, 64:128], rt[2][:], rt[3][:], ALU.add, r=["rt2", "rt3"], w=[okey], eng="pool")
                        else:
                            c.cp(o[:], ps, r=[pkey], w=[okey], eng=("act" if (ecnt[0] % 2) else "dve"))
                        if OFF["xbc"] <= c0 < OFF["mq"]:
                            xr = xbrow(r0 + t * 128)
                            c.dma(XB[xr:xr + 128, c0 - OFF["xbc"]:c0 - OFF["xbc"] + cw], o[:, 0:cw], r=[okey], w=["XB"], eng="act")
                        else:
                            c.dma(A_[rows, c0:c0 + cw], o[:, 0:cw], r=[okey], w=["A_"], eng="act")

                    dense(uT, "uT", 8, wb_in[l][:, 0:A_COLS], A_COLS, evacA, wt=wt)

                    def evacB(c0, cw, t, ps, pkey, r0=r0):
                        rows = slice(r0 + t * 128, r0 + (t + 1) * 128)
                        c.cp(og[:], ps, r=[pkey], w=["og"])
                        c.dma(G_[rows, :], og[:], r=["og"], w=["G_"], eng="act")

                    dense(uT, "uT", 8, wb_in[l][:, A_COLS:8752], 48, evacB, wt=wt)

                    def evacG(c0, cw, t, ps, pkey, r0=r0):
                        rows = slice(r0 + t * 128, r0 + (t + 1) * 128)
                        oi = ecnt[0] % NOE
                        ecnt[0] += 1
                        o = oev[oi]
                        okey = "oev%d" % oi
                        c.tt(oef[:], ps, bg[:, c0:c0 + cw], ALU.add, r=[pkey, "ltab"], w=["oef"])
                        c.act(o[:], oef[:], AF.Sigmoid, r=["oef"], w=[okey])
                        c.dma(GT[rows, c0:c0 + cw], o[:], r=[okey], w=["GT"], eng="act")

                    dense(uT, "uT", 8, wb_gate[l], 3072, evacG, wt=wt)
                hp = c.sb("hp", [4, 1536], BF16)
                hq = c.sb("hq", [4, 1536], BF16)
                c.dma(hp[0:2, :], XB[xbrow(SEG - 2):xbrow(SEG - 2) + 2, :], r=["XB"], w=["hp"])
                c.ts(hq[0:2, :], hp[0:2, :], flag[0:2, 0:1], None, ALU.mult, r=["hp", "flag"], w=["hq"])
                c.dma(XB[(SEG + 3):(SEG + 3) + 2, :], hq[0:2, :], r=["hq"], w=["XB"])
                hp2 = c.sb("hp2", [4, 1536], BF16)
                hq2 = c.sb("hq2", [4, 1536], BF16)
                c.dma(hp2[0:1, :], XB[xbrow(SEG):xbrow(SEG) + 1, :], r=["XB"], w=["hp2"])
                c.ts(hq2[0:1, :], hp2[0:1, :], flag[0:1, 0:1], None, ALU.mult, r=["hp2", "flag"], w=["hq2"])
                c.dma(XB[2 + SEG:3 + SEG, :], hq2[0:1, :], r=["hq2"], w=["XB"])
                c.emit()
                c.pes = None

            for mode in ("S", "M"):
                if mode not in phases:
                    continue
                with ExitStack() as pes:
                    c.pes = pes
                    isM = (mode == "M")
                    cwt = c.sb("cwt", [128, 4, 1536], BF16)
                    cbt = c.sb("cbt", [128, 1536], F32)
                    dtb = c.sb("dtb", [128, 32], F32)
                    aneg = c.sb("aneg", [128, 32], F32)
                    mgb = c.sb("mgb", [128, 16], F32)

                    bcast_load(cbt[:], vecs["conv_b"][l], "ltab")
                    bcast_load(dtb[:], vecs["ssd_dt_bias"][l], "ltab")
                    bcast_load(aneg[:], vecs["ssd_a_log"][l], "ltab")
                    bcast_load(mgb[:], vecs["mlstm_gate_bias"][l], "ltab")
                    c.act(aneg[:], aneg[:], AF.Exp, r=["ltab"], w=["ltab"])
                    c.ts(aneg[:], aneg[:], -1.0, None, ALU.mult, r=["ltab"], w=["ltab"])
                    if isM:
                        dsk = c.sb("dsk", [128, 16], F32)
                        nga = c.sb("nga", [128, D], BF16)
                        ngb = c.sb("ngb", [128, D], BF16)
                        ngc = c.sb("ngc", [128, D], BF16)
                        bcast_load(dsk[:], vecs["ssd_d"][l], "ltab")
                        NGL = [(nga, "ssd_norm_g"), (ngb, "mlstm_norm_g"), (ngc, "ret_norm_g")]
                    xb4 = [c.sb("xb4_%d" % i, [128, 1536], BF16) for i in range(4)]
                    cacc = c.sb("cacc", [128, 1536], F32)
                    for t_ in range(4):
                        c.dma(cacc[:], vecs["conv_w"][l][t_ * 1536:(t_ + 1) * 1536].partition_broadcast(128), w=["cacc"])
                        c.cp(cwt[:, t_, :], cacc[:], r=["cacc"], w=["ltab"], eng="act")
                    ctmp = c.sb("ctmp", [128, 1536], F32)
                    xactL = [c.sb("xact%d" % i, [128, 1536], BF16) for i in range(2)]
                    gtL = [c.sb("gt%d" % i, [128, 48], F32) for i in range(2)]
                    smL = [{n: c.sb("sm%d_%s" % (i, n), [128, 40], F32) for n in
                            ("dtp", "dt", "ld", "li", "csx", "tot", "bw", "wend", "ecs", "dec", "gml", "sp")} for i in range(2)]
                    ldbL = [c.sb("ldb%d" % i, [128, 40], BF16) for i in range(2)]
                    st = c.sb("st", [128, 3076], F32)
                    stbL = [c.sb("stb%d" % i, [128, 3076], BF16) for i in range(2)]
                    vaug = c.sb("vaug", [128, 4, 257], BF16)
                    xw = c.sb("xw", [128, D], BF16)
                    kwL = [c.sb("kw%d" % i, [128, 128], BF16) for i in range(2)]
                    c.memset(vaug[:], 1.0, w=["vaug"])
                    c.memset(st[:], 0.0, w=["st"])
                    c.memset(stbL[0][:], 0.0, w=["stb0"])
                    c.memset(stbL[1][:], 0.0, w=["stb1"])
                    if isM:
                        atL = [c.sb("at%d" % i, [128, A_COLS - 1536], BF16) for i in range(2)]
                        sbl = c.sb("sbl", [128, 3076], BF16)
                        R = c.sb("R", [128, 40, 128], BF16)
                        WL = [c.sb("W%d" % i, [128, 40, 128], BF16) for i in range(2)]
                        Mb = c.sb("Mb", [128, 16, 128], BF16)
                        bcT = c.sb("bcT", [128, 4, 128], BF16)
                        qkT = c.sb("qkT", [128, 16, 128], BF16)
                        f1 = c.sb("f1", [128, D], F32)
                        f2 = c.sb("f2", [128, D], F32)
                        for ngt, ngn in NGL:
                            c.dma(f1[:], vecs[ngn][l].partition_broadcast(128), w=["f1"])
                            c.cp(ngt[:], f1[:], r=["f1"], w=["ltab"], eng="act")
                        jk1 = c.sb("jk1", [128, D], BF16)
                        jk2 = c.sb("jk2", [128, 256], BF16)
                        jk3 = c.sb("jk3", [128, 256], BF16)
                        g0 = c.sb("g0", [128, D], BF16)
                        g1 = c.sb("g1", [128, D], BF16)
                        g2 = c.sb("g2", [128, D], BF16)
                        ry = c.sb("ry", [128, D], F32)
                        Pm = c.sb("Pm", [128, 8, 128], BF16)
                        dd8 = c.sb("dd8", [128, 8], F32)
                        rr8 = c.sb("rr8", [128, 8], F32)
                        er8 = c.sb("er8", [128, 8], F32)
                        hacc = c.sb("hacc", [128, 4, 256], F32)
                        Pr = c.sb("Pr", [128, 4, 128], BF16)
                        qxf = c.sb("qxf", [128, 4, 128], BF16)
                        qxb = c.sb("qxb", [128, 4, 128], BF16)
                        mot = c.sb("mot", [128, 3072], BF16)
                        ss4 = c.sb("ss4", [128, 4], F32)
                        ss4r = c.sb("ss4r", [128, 4], F32)
                        ss1 = c.sb("ss1", [128, 1], F32)
                    else:
                        kv = c.sb("kv", [128, 3072], BF16)

                    order = list(range(NCH)) if isM else list(range(NCH - 1, -1, -1))
                    fo = 0 if isM else 20

                    def loadsA(ch, p):
                        r0 = ch * 128
                        xr = xbrow(r0)
                        for t in range(4):
                            c.dma(xb4[t][:], XB[xr + t - 2:xr + t - 2 + 128, :], r=["XB"], w=["xb4_%d" % t])
                        c.dma(gtL[p][:], G_[r0:r0 + 128, :], r=["G_"], w=["gt%d" % p])

                    def prepA(ch, p):
                        xact = xactL[p]
                        xk = "xact%d" % p
                        gt = gtL[p]
                        gk = "gt%d" % p
                        S = smL[p]
                        ldb = ldbL[p]

                        def k(n):
                            return "%s%d" % (n, p)

                        c.tt(cacc[:], xb4[0][:], cwt[:, 0, :], ALU.mult, r=["xb4_0", "ltab"], w=["cacc"], eng="pool")
                        for t in range(1, 4):
                            c.tt(ctmp[:], xb4[t][:], cwt[:, t, :], ALU.mult, r=["xb4_%d" % t, "ltab"], w=["ctmp"], eng="pool")
                            c.tt(cacc[:], cacc[:], ctmp[:], ALU.add, r=["cacc", "ctmp"], w=["cacc"])
                        c.tt(cacc[:], cacc[:], cbt[:], ALU.add, r=["cacc", "ltab"], w=["cacc"])
                        c.act(xact[:], cacc[:], AF.Silu, r=["cacc"], w=[xk])
                        c.tt(S["dtp"][:, 0:32], gt[:, 0:32], dtb[:], ALU.add, r=[gk, "ltab"], w=[k("dtp")])
                        c.act(S["dtp"][:, 0:32], S["dtp"][:, 0:32], AF.Exp, r=[k("dtp")], w=[k("dtp")])
                        c.act(S["dt"][:, 0:32], S["dtp"][:, 0:32], AF.Ln, r=[k("dtp")], w=[k("dt")], bias=1.0)
                        c.tt(S["gml"][:, 0:16], gt[:, 32:48], mgb[:], ALU.add, r=[gk, "ltab"], w=[k("gml")])
                        c.act(S["sp"][:, 0:8], S["gml"][:, 0:8], AF.Exp, r=[k("gml")], w=[k("sp")], scale=-1.0)
                        c.act(S["sp"][:, 0:8], S["sp"][:, 0:8], AF.Ln, r=[k("sp")], w=[k("sp")], bias=1.0)
                        c.act(S["li"][:, 0:16], S["dt"][:, 0:16], AF.Ln, r=[k("dt")], w=[k("li")])
                        c.act(S["li"][:, 20:36], S["dt"][:, 16:32], AF.Ln, r=[k("dt")], w=[k("li")])
                        c.cp(S["li"][:, 16:20], S["gml"][:, 8:12], r=[k("gml")], w=[k("li")])
                        c.cp(S["li"][:, 36:40], S["gml"][:, 12:16], r=[k("gml")], w=[k("li")])
                        c.tt(S["ld"][:, 0:16], S["dt"][:, 0:16], aneg[:, 0:16], ALU.mult, r=[k("dt"), "ltab"], w=[k("ld")])
                        c.tt(S["ld"][:, 20:36], S["dt"][:, 16:32], aneg[:, 16:32], ALU.mult, r=[k("dt"), "ltab"], w=[k("ld")])
                        c.ts(S["ld"][:, 16:20], S["sp"][:, 0:4], -1.0, None, ALU.mult, r=[k("sp")], w=[k("ld")])
                        c.ts(S["ld"][:, 36:40], S["sp"][:, 4:8], -1.0, None, ALU.mult, r=[k("sp")], w=[k("ld")])
                        c.cp(ldb[:], S["ld"][:], r=[k("ld")], w=[k("ldb")])
                        pc = pbs[0]
                        c.mm(pc[:, 0:40], C16s("tri"), ldb[:], r=["c16", k("ldb")], w=["pb0"])
                        c.mm(pc[:, 40:80], C16s("triT"), ldb[:], r=["c16", k("ldb")], w=["pb0"])
                        c.mm(pc[:, 80:120], C16s("ones"), ldb[:], r=["c16", k("ldb")], w=["pb0"])
                        c.cp(S["csx"][:, 0:20], pc[:, 0:20], r=["pb0"], w=[k("csx")])
                        c.cp(S["csx"][:, 20:40], pc[:, 60:80], r=["pb0"], w=[k("csx")])
                        c.cp(S["tot"][:], pc[:, 80:120], r=["pb0"], w=[k("tot")], eng="act")
                        c.tt(S["bw"][:], S["li"][:], S["csx"][:], ALU.subtract, r=[k("li"), k("csx")], w=[k("bw")])
                        c.tt(S["wend"][:], S["bw"][:], S["tot"][:], ALU.add, r=[k("bw"), k("tot")], w=[k("wend")])
                        c.act(S["wend"][:], S["wend"][:], AF.Exp, r=[k("wend")], w=[k("wend")])
                        c.act(S["ecs"][:], S["csx"][:], AF.Exp, r=[k("csx")], w=[k("ecs")])
                        c.act(S["dec"][:], S["tot"][:], AF.Exp, r=[k("tot")], w=[k("dec")])
                        if isM:
                            W = WL[p]
                            wk = "W%d" % p
                            c.tt(R[:, 0:20, :], ldb[:, 0:20].unsqueeze(2).to_broadcast([128, 20, 128]),
                                 C16s("tri").unsqueeze(1).to_broadcast([128, 20, 128]), ALU.mult,
                                 r=[k("ldb"), "c16"], w=["R"], eng="pool")
                            c.tt(R[:, 20:40, :], ldb[:, 20:40].unsqueeze(2).to_broadcast([128, 20, 128]),
                                 C16s("triT").unsqueeze(1).to_broadcast([128, 20, 128]), ALU.mult,
                                 r=[k("ldb"), "c16"], w=["R"], eng="pool")
                            for g in range(10):
                                pw = pbs[0]
                                nm = C16s("nmf", 512) if g < 5 else C16s("nmb", 512)
                                c.mm(pw[:, :], C16s("ones"), R[:, 4 * g:4 * g + 4, :].rearrange("p a b -> p (a b)"),
                                     start=True, stop=False, r=["c16", "R"], w=["pb0"])
                                c.mm(pw[:, :], ident, nm, start=False, stop=True, r=["c16"], w=["pb0"])
                                for j in range(4):
                                    cc = 4 * g + j
                                    c.act(W[:, cc, :], pw[:, j * 128:(j + 1) * 128], AF.Exp,
                                          r=["pb0", k("bw")], w=[wk], bias=S["bw"][:, cc:cc + 1])
                            c.tt(W[:, 0:16, :], W[:, 0:16, :], W[:, 20:36, :], ALU.add, r=[wk], w=[wk], eng="pool")

                    def mainA(ch, p, pin, pout):
                        r0 = ch * 128
                        xact = xactL[p]
                        xk = "xact%d" % p
                        S = smL[p]

                        def k(n):
                            return "%s%d" % (n, p)

                        stb = stbL[pin]
                        sk = "stb%d" % pin
                        if isM:
                            at = atL[p]
                            AK = "at%d" % p
                            W = WL[p]
                            wk = "W%d" % p
                            c.dma(at[:, 0:1024], A_[r0:r0 + 128, 0:1024], r=["A_"], w=[AK])
                            c.dma(at[:, 1024:A_COLS - 1536], A_[r0:r0 + 128, 2560:A_COLS], r=["A_"], w=[AK])
                            c.dma(sbl[:], SB[ch], r=["SB"], w=["sbl"])
                            mk = at[:, OFF["mk"] - 1536:OFF["mk"] - 1536 + 512]
                            mv = at[:, OFF["mv"] - 1536:OFF["mv"] - 1536 + 1024]
                            rk = at[:, OFF["rk"] - 1536:OFF["rk"] - 1536 + 512]
                            rv = at[:, OFF["rv"] - 1536:OFF["rv"] - 1536 + 1024]
                            kvkey = AK
                        else:
                            c.dma(kv[:, 0:1536], A_[r0:r0 + 128, OFF["mk"]:OFF["mk"] + 1536], r=["A_"], w=["kv"])
                            c.dma(kv[:, 1536:3072], A_[r0:r0 + 128, OFF["rk"]:OFF["rk"] + 1536], r=["A_"], w=["kv"])
                            mk = kv[:, 0:512]
                            mv = kv[:, 512:1536]
                            rk = kv[:, 1536:2048]
                            rv = kv[:, 2048:3072]
                            kvkey = "kv"
                            c.dma(SB[ch], stb[:], r=[sk], w=["SB"])
                        c.cp(vaug[:, :, 0:256], mv.rearrange("p (h v) -> p h v", h=4), r=[kvkey], w=["vaug"], eng="pool")
                        if isM:
                            c.act(g0[:], at[:, 0:1024], AF.Silu, r=[AK], w=["g0"])
                            c.act(g2[:], at[:, OFF["rg"] - 1536:OFF["rg"] - 1536 + 1024], AF.Silu, r=[AK], w=["g2"])
                            c.act(g1[:], at[:, OFF["mo"] - 1536:OFF["mo"] - 1536 + 1024], AF.Tanh, r=[AK], w=["g1"], scale=0.5)

                        if isM:
                            pv = pbs[2][:].bitcast(BF16)
                            for kk_ in range(4):
                                src = xact[:, 1024 + kk_ * 128:1024 + (kk_ + 1) * 128]
                                c.tr(pv[:, kk_ * 128:(kk_ + 1) * 128], src, ident, r=[xk, "c16"], w=["pb2"])
                            c.cp(bcT[:], pv[:, 0:512].rearrange("p (k q) -> p k q", k=4), r=["pb2"], w=["bcT"], eng="act")
                            for grp, base in enumerate([OFF["mq"] - 1536, OFF["rq"] - 1536]):
                                for kk_ in range(8):
                                    src = at[:, base + kk_ * 128:base + (kk_ + 1) * 128]
                                    c.tr(pv[:, kk_ * 128:(kk_ + 1) * 128], src, ident, r=[AK, "c16"], w=["pb2"])
                                c.cp(qkT[:, grp * 8:grp * 8 + 8, :], pv[:, :].rearrange("p (k q) -> p k q", k=8),
                                     r=["pb2"], w=["qkT"], eng="act")
                        c.op("__cut__", None)
                        c.tt(xw[:].rearrange("p (h d) -> p h d", h=16), xact[:, 0:1024].rearrange("p (h d) -> p h d", h=16),
                             S["wend"][:, fo:fo + 16].unsqueeze(2).to_broadcast([128, 16, 64]), ALU.mult,
                             r=[xk, k("wend")], w=["xw"], eng="pool")
                        for g in range(2):
                            c.mm(pbs[4 + g][:, :], xact[:, 1024 + g * 128:1024 + (g + 1) * 128], xw[:, g * 512:(g + 1) * 512],
                                 r=[xk, "xw"], w=["pb%d" % (4 + g)])
                        c.tt(st[:, 0:1024].rearrange("p (h d) -> p h d", h=16), st[:, 0:1024].rearrange("p (h d) -> p h d", h=16),
                             S["dec"][:, fo:fo + 16].unsqueeze(2).to_broadcast([128, 16, 64]), ALU.mult,
                             r=["st", k("dec")], w=["st"])
                        for g in range(2):
                            c.tt(st[:, g * 512:(g + 1) * 512], st[:, g * 512:(g + 1) * 512], pbs[4 + g][:, :], ALU.add,
                                 r=["st", "pb%d" % (4 + g)], w=["st"])
                        mo_ = 16 + fo
                        for h in range(4):
                            kw = kwL[h % 2]
                            kk = "kw%d" % (h % 2)
                            c.ts(kw[:], mk[:, h * 128:(h + 1) * 128], S["wend"][:, mo_ + h:mo_ + h + 1], None, ALU.mult,
                                 r=[kvkey, k("wend")], w=[kk])
                            pi = 4 + (h % 2)
                            c.mm(pbs[pi][:, 0:257], kw[:], vaug[:, h, :], r=[kk, "vaug"], w=["pb%d" % pi])
                            sl = st[:, 1024 + h * 257:1024 + (h + 1) * 257]
                            c.stt(sl, sl, S["dec"][:, mo_ + h:mo_ + h + 1], pbs[pi][:, 0:257], ALU.mult, ALU.add,
                                  r=["st", k("dec"), "pb%d" % pi], w=["st"])
                        zo = C32["zf"] if isM else C32["zb"]
                        for h in range(4):
                            kw = kwL[h % 2]
                            kk = "kw%d" % (h % 2)
                            c.ts(kw[:], rk[:, h * 128:(h + 1) * 128], c32[:, zo + h:zo + h + 1], None, ALU.mult,
                                 r=[kvkey, "c32"], w=[kk])
                            pi = 4 + (h % 2)
                            c.mm(pbs[pi][:, 0:256], kw[:], rv[:, h * 256:(h + 1) * 256], r=[kk, kvkey], w=["pb%d" % pi])
                            sl = st[:, 2052 + h * 256:2052 + (h + 1) * 256]
                            c.stt(sl, sl, float(math.exp(128 * LOGG[h])), pbs[pi][:, 0:256], ALU.mult, ALU.add,
                                  r=["st", "pb%d" % pi], w=["st"])
                        if (isM and ch == NCHS - 1) or ((not isM) and ch == NCHS):
                            c.ts(st[:], st[:], flag[:, 0:1], None, ALU.mult, r=["st", "flag"], w=["st"])
                        c.cp(stbL[pout][:], st[:], r=["st"], w=["stb%d" % pout], eng="act")
                        if not isM:
                            return

                        pg = pbs[3]
                        for g in range(2):
                            c.mm(pg[:, 128 + g * 128:256 + g * 128], bcT[:, g, :], bcT[:, 2 + g, :], r=["bcT"], w=["pb3g"])
                        for g in range(2):
                            c.tt(Mb[:, 8 * g:8 * g + 8, :], W[:, 8 * g:8 * g + 8, :],
                                 pg[:, 128 + g * 128:256 + g * 128].unsqueeze(1).to_broadcast([128, 8, 128]),
                                 ALU.mult, r=[wk, "pb3g"], w=["Mb"])
                        for h in range(16):
                            pyb = pbs[4 + h // 8]
                            c.mm(pyb[:, (h % 8) * 64:(h % 8 + 1) * 64], Mb[:, h, :], xact[:, h * 64:(h + 1) * 64],
                                 r=["Mb", xk], w=["pb%d" % (4 + h // 8)])
                        f1v = f1[:].rearrange("p (h d) -> p h d", h=16)
                        f2v = f2[:].rearrange("p (h d) -> p h d", h=16)
                        c.tt(f1v, xact[:, 0:1024].rearrange("p (h d) -> p h d", h=16),
                             dsk[:].unsqueeze(2).to_broadcast([128, 16, 64]), ALU.mult, r=[xk, "ltab"], w=["f1"], eng="pool")
                        for g in range(2):
                            c.tt(f1[:, g * 512:(g + 1) * 512], f1[:, g * 512:(g + 1) * 512], pbs[4 + g][:, :], ALU.add,
                                 r=["f1", "pb%d" % (4 + g)], w=["f1"])
                        for d_, (stsrc, skey, eo) in enumerate([(stb, sk, 0), (sbl, "sbl", 20)]):
                            for g in range(2):
                                c.mm(pbs[4 + g][:, :], bcT[:, 2 + g, :], stsrc[:, g * 512:(g + 1) * 512],
                                     r=["bcT", skey], w=["pb%d" % (4 + g)])
                            for g in range(2):
                                c.tt(f2v[:, 8 * g:8 * g + 8, :], pbs[4 + g][:, :].rearrange("p (h d) -> p h d", h=8),
                                     S["ecs"][:, eo + 8 * g:eo + 8 * g + 8].unsqueeze(2).to_broadcast([128, 8, 64]), ALU.mult,
                                     r=["pb%d" % (4 + g), k("ecs")], w=["f2"])
                            c.tt(f1[:], f1[:], f2[:], ALU.add, r=["f1", "f2"], w=["f1"], eng="pool")
                        c.tt(f1[:], f1[:], g0[:], ALU.mult, r=["f1", "g0"], w=["f1"])
                        c.act(jk1[:], f1[:], AF.Square, r=["f1"], w=["jk1", "ss1"], accum_out=ss1[:])
                        c.act(ss1[:], ss1[:], AF.Ln, r=["ss1"], w=["ss1"], scale=1.0 / D, bias=EPS)
                        c.act(ss1[:], ss1[:], AF.Exp, r=["ss1"], w=["ss1"], scale=-0.5)
                        c.stt(mot[:, 0:1024], f1[:], ss1[:], nga[:], ALU.mult, ALU.mult, r=["f1", "ss1", "ltab"], w=["motA"])


                    def mainB(ch, p, pin):
                        r0 = ch * 128
                        S = smL[p]

                        def k(n):
                            return "%s%d" % (n, p)

                        stb = stbL[pin]
                        sk = "stb%d" % pin
                        W = WL[p]
                        wk = "W%d" % p
                        at = atL[p]
                        AK = "at%d" % p
                        rv = at[:, OFF["rv"] - 1536:OFF["rv"] - 1536 + 1024]
                        for h in range(4):
                            c.mm(pbs[1][:, h * 128:(h + 1) * 128], qkT[:, 4 + h, :], qkT[:, h, :], r=["qkT"], w=["pb1"])
                        p03 = pbs[1][:, :].rearrange("p (h l) -> p h l", h=4)
                        c.tt(Pm[:, 0:4, :], p03, W[:, 16:20, :], ALU.mult, r=["pb1", wk], w=["Pm"])
                        c.tt(Pm[:, 4:8, :], p03, W[:, 36:40, :], ALU.mult, r=["pb1", wk], w=["Pm"])
                        onec = C16s("ones")[:, 0:1]
                        for d_ in range(2):
                            stsrc = stb if d_ == 0 else sbl
                            skey = sk if d_ == 0 else "sbl"
                            for h in range(4):
                                j = 4 * d_ + h
                                c.mm(pbs[1][:, j:j + 1], Pm[:, j, :], onec, r=["Pm", "c16"], w=["pb1"])
                                c.mm(pbs[1][:, 8 + j:9 + j], qkT[:, h, :], stsrc[:, 1024 + h * 257 + 256:1024 + (h + 1) * 257],
                                     r=["qkT", skey], w=["pb1"])
                        for d_ in range(2):
                            c0_ = 16 if d_ == 0 else 36
                            c.tt(dd8[:, 4 * d_:4 * d_ + 4], pbs[1][:, 8 + 4 * d_:12 + 4 * d_], S["ecs"][:, c0_:c0_ + 4], ALU.mult,
                                 r=["pb1", k("ecs")], w=["dd8"])
                        c.tt(dd8[:], dd8[:], pbs[1][:, 0:8], ALU.add, r=["dd8", "pb1"], w=["dd8"])
                        c.act(rr8[:], dd8[:], AF.Abs, r=["dd8"], w=["rr8"])
                        c.ts(rr8[:], rr8[:], 1.0, None, ALU.max, r=["rr8"], w=["rr8"])
                        c.recip(rr8[:], rr8[:], r=["rr8"], w=["rr8"])
                        for d_ in range(2):
                            c0_ = 16 if d_ == 0 else 36
                            c.tt(er8[:, 4 * d_:4 * d_ + 4], rr8[:, 4 * d_:4 * d_ + 4], S["ecs"][:, c0_:c0_ + 4], ALU.mult,
                                 r=["rr8", k("ecs")], w=["er8"])
                        for d_ in range(2):
                            stsrc = stb if d_ == 0 else sbl
                            skey = sk if d_ == 0 else "sbl"
                            for h in range(4):
                                j = 4 * d_ + h
                                c.mm(pbs[6][:, 0:256], Pm[:, j, :], vaug[:, h, 0:256], r=["Pm", "vaug"], w=["pb6"])
                                if d_ == 0:
                                    c.ts(hacc[:, h, :], pbs[6][:, 0:256], rr8[:, j:j + 1], None, ALU.mult,
                                         r=["pb6", "rr8"], w=["hacc"])
                                else:
                                    c.stt(hacc[:, h, :], pbs[6][:, 0:256], rr8[:, j:j + 1], hacc[:, h, :], ALU.mult, ALU.add,
                                          r=["pb6", "rr8", "hacc"], w=["hacc"])
                                c.mm(pbs[7][:, 0:256], qkT[:, h, :], stsrc[:, 1024 + h * 257:1024 + h * 257 + 256],
                                     r=["qkT", skey], w=["pb7"])
                                c.stt(hacc[:, h, :], pbs[7][:, 0:256], er8[:, j:j + 1], hacc[:, h, :], ALU.mult, ALU.add,
                                      r=["pb7", "er8", "hacc"], w=["hacc"])
                        for h in range(4):
                            c.act(jk2[:], hacc[:, h, :], AF.Square, r=["hacc"], w=["jk2", "ss4"], accum_out=ss4[:, h:h + 1])
                        c.act(ss4[:], ss4[:], AF.Ln, r=["ss4"], w=["ss4"], scale=1.0 / 256, bias=EPS)
                        c.act(ss4[:], ss4[:], AF.Exp, r=["ss4"], w=["ss4"], scale=-0.5, bias=math.log(0.5))
                        c.tt(hacc[:], hacc[:], ss4[:].unsqueeze(2).to_broadcast([128, 4, 256]), ALU.mult, r=["hacc", "ss4"], w=["hacc"])
                        hf = hacc[:].rearrange("p h v -> p (h v)")
                        c.tt(hf, hf, ngb[:], ALU.mult, r=["hacc", "ltab"], w=["hacc"], eng="pool")
                        c.stt(mot[:, 1024:2048], g1[:], 1.0, hf, ALU.add, ALU.mult, r=["hacc", "g1"], w=["motB"])

                        for h in range(4):
                            c.mm(pbs[1][:, h * 128:(h + 1) * 128], qkT[:, 12 + h, :], qkT[:, 8 + h, :], r=["qkT"], w=["pb1"])
                        c.tt(Pr[:].rearrange("p h l -> p (h l)"), pbs[1][:, :], c32[:, C32["dtab"]:C32["dtab"] + 512], ALU.mult,
                             r=["pb1", "c32"], w=["Pr"])
                        c.tt(qxf[:].rearrange("p h l -> p (h l)"), qkT[:, 8:12, :].rearrange("p h l -> p (h l)"),
                             c32[:, C32["xif"]:C32["xif"] + 512], ALU.mult, r=["qkT", "c32"], w=["qxf"], eng="pool")
                        c.tt(qxb[:].rearrange("p h l -> p (h l)"), qkT[:, 8:12, :].rearrange("p h l -> p (h l)"),
                             c32[:, C32["xib"]:C32["xib"] + 512], ALU.mult, r=["qkT", "c32"], w=["qxb"], eng="pool")
                        for h in range(4):
                            pvb = pbs[6 + h % 2]
                            pk = "pb%d" % (6 + h % 2)
                            c.mm(pvb[:, 0:256], Pr[:, h, :], rv[:, h * 256:(h + 1) * 256], start=True, stop=False,
                                 r=["Pr", AK], w=[pk])
                            c.mm(pvb[:, 0:256], qxf[:, h, :], stb[:, 2052 + h * 256:2052 + (h + 1) * 256], start=False, stop=False,
                                 r=["qxf", sk], w=[pk])
                            c.mm(pvb[:, 0:256], qxb[:, h, :], sbl[:, 2052 + h * 256:2052 + (h + 1) * 256], start=False, stop=True,
                                 r=["qxb", "sbl"], w=[pk])
                            c.cp(ry[:, h * 256:(h + 1) * 256], pvb[:, 0:256], r=[pk], w=["ry"], eng="act")
                            c.act(jk3[:], ry[:, h * 256:(h + 1) * 256], AF.Square, r=["ry"], w=["jk3", "ss4r"],
                                  accum_out=ss4r[:, h:h + 1])
                        c.act(ss4r[:], ss4r[:], AF.Ln, r=["ss4r"], w=["ss4r"], scale=1.0 / 256, bias=EPS)
                        c.act(ss4r[:], ss4r[:], AF.Exp, r=["ss4r"], w=["ss4r"], scale=-0.5)
                        c.tt(ry[:].rearrange("p (h v) -> p h v", h=4), ry[:].rearrange("p (h v) -> p h v", h=4),
                             ss4r[:].unsqueeze(2).to_broadcast([128, 4, 256]), ALU.mult, r=["ry", "ss4r"], w=["ry"])
                        c.tt(ry[:], ry[:], ngc[:], ALU.mult, r=["ry", "ltab"], w=["ry"], eng="pool")
                        c.tt(mot[:, 2048:3072], ry[:], g2[:], ALU.mult, r=["ry", "g2"], w=["motC"])

                    n_ = len(order)
                    loadsA(order[0], 0)
                    prepA(order[0], 0)
                    for i, ch in enumerate(order):
                        sA = []
                        if i + 1 < n_:
                            loadsA(order[i + 1], (i + 1) % 2)
                            sA = c.capture(lambda: prepA(order[i + 1], (i + 1) % 2))
                        sB = c.capture(lambda: mainA(ch, i % 2, i % 2, (i + 1) % 2))
                        cut = [j for j, o_ in enumerate(sB) if o_[0] == "__cut__"][0]
                        sH, sB = sB[:cut], sB[cut + 1:]
                        sC = c.capture(lambda: mainB(ch, i % 2, i % 2)) if isM else []
                        nA = len(sA) * len(sH) // max(1, len(sH) + len(sB))
                        c.replay_interleaved([sA[:nA], sH])
                        c.replay_interleaved([sA[nA:], sB, sC])
                        if isM:
                            c.dma(MO[ch * 128:ch * 128 + 128, :], mot[:], r=["motA", "motB", "motC"],
                                  w=["MO", "motA", "motB", "motC"])
                    c.emit()
                    c.pes = None

            with ExitStack() as pes:
              if "D" in phases:
                c.pes = pes
                nfg = c.sb("nfg", [128, D])
                fng = c.sb("fng", [128, D])
                bcast_load(nfg[:], vecs["norm_ffn_g"][l], "ltab")
                bcast_load(fng[:], fin_g, "ltab")
                wt = [c.sb("wtd%d" % i, [128, 22, 512], BF16) for i in range(2)]
                mo4 = c.sb("mo4", [128, 4, 3072], BF16)
                gt4 = c.sb("gt4", [128, 4, 3072], BF16)
                x4 = c.sb("x4", [128, 4, D], F32)
                brT = c.sb("brT", [128, 8, 512], BF16)
                mrg = c.sb("mrg", [128, 4, D], F32)
                tmpd = c.sb("tmpd", [128, 512], F32)
                mrb = c.sb("mrb", [128, D], BF16)
                hT = c.sb("hT", [128, 8, 512], BF16)
                actT = c.sb("actT", [128, 22, 512], BF16)
                tmpf = c.sb("tmpfd", [128, D], F32)
                ss = c.sb("ssd_", [128, 1], F32)
                yo = c.sb("yo", [128, D], F32)
                for b in range(NBLK):
                    r0 = b * 512
                    c.dma(mo4[:], MO[r0:r0 + 512, :].rearrange("(t p) e -> p t e", p=128), r=["MO"], w=["mo4"])
                    c.dma(gt4[:], GT[r0:r0 + 512, :].rearrange("(t p) e -> p t e", p=128), r=["GT"], w=["gt4"])
                    c.dma(x4[:], xsrc[r0:r0 + 512, :].rearrange("(t p) e -> p t e", p=128), r=["XS"], w=["x4"])
                    for br in range(3):
                        for t in range(4):
                            transpose_to(brT, "brT", mo4[:, t, br * 1024:(br + 1) * 1024], "mo4", 8, t)

                        def evacBr(c0, cw, t, ps, pkey, br=br):
                            gsl = gt4[:, t, br * 1024 + c0:br * 1024 + c0 + cw]
                            if br == 0:
                                c.tt(mrg[:, t, c0:c0 + cw], ps, gsl, ALU.mult, r=[pkey, "gt4"], w=["mrg"])
                            else:
                                c.tt(tmpd[:, 0:cw], ps, gsl, ALU.mult, r=[pkey, "gt4"], w=["tmpd"])
                                c.tt(mrg[:, t, c0:c0 + cw], mrg[:, t, c0:c0 + cw], tmpd[:, 0:cw], ALU.add,
                                     r=["mrg", "tmpd"], w=["mrg"], eng="pool")

                        dense(brT, "brT", 8, wb_br[l][br], D, evacBr, wt=wt)
                    for t in range(4):
                        c.cp(mrb[:], mrg[:, t, :], r=["mrg"], w=["mrb"], eng="act")
                        transpose_to(brT, "brT", mrb, "mrb", 8, t)

                    def evacO(c0, cw, t, ps, pkey):
                        c.tt(x4[:, t, c0:c0 + cw], x4[:, t, c0:c0 + cw], ps, ALU.add, r=["x4", pkey], w=["x4"])

                    dense(brT, "brT", 8, wb_out[l], D, evacO, wt=wt)
                    for t in range(4):
                        rmsnorm_rows(x4[:, t, :], "x4", nfg[:], mrb[:], "mrb", tmpf[:], ss[:])
                        transpose_to(hT, "hT", mrb, "mrb", 8, t)

                    def evacFg(c0, cw, t, ps, pkey):
                        c.act(actT[:, c0 // 128, :], ps, AF.Silu, r=[pkey], w=["actT"])

                    dense(hT, "hT", 8, wb_fg[l], FF, evacFg, fmajor=True, wt=wt)

                    def evacFu(c0, cw, t, ps, pkey):
                        c.tt(actT[:, c0 // 128, :], ps, actT[:, c0 // 128, :], ALU.mult, r=[pkey, "actT"], w=["actT"])

                    dense(hT, "hT", 8, wb_fu[l], FF, evacFu, fmajor=True, wt=wt)

                    def evacFd(c0, cw, t, ps, pkey):
                        c.tt(x4[:, t, c0:c0 + cw], x4[:, t, c0:c0 + cw], ps, ALU.add, r=["x4", pkey], w=["x4"])

                    dense(actT, "actT", 22, wb_fd[l], D, evacFd, wt=wt)
                    if last:
                        for t in range(4):
                            rmsnorm_rows(x4[:, t, :], "x4", fng[:], yo[:], "yo", tmpf[:], ss[:])
                            c.dma(y_out[r0 + t * 128:r0 + (t + 1) * 128, :], yo[:], r=["yo"], w=["y"], eng="act")
                    else:
                        c.dma(XS[r0:r0 + 512, :].rearrange("(t p) e -> p t e", p=128), x4[:], r=["x4"], w=["XS"], eng="act")
                c.emit()
                c.pes = None
    return nc


_PERM = None


def _perm():
    global _PERM
    if _PERM is None:
        o = dict(z=0, xbc=1024, dt=2560, mq=2592, mk=3104, mv=3616, mo=4640, mg=5664, rq=5680, rk=6192, rv=6704, rg=7728)
        n = dict(z=1024, xbc=1536, dt=32, mq=512, mk=512, mv=1024, mo=1024, mg=16, rq=512, rk=512, rv=1024, rg=1024)
        idx = []
        for k in ("z", "xbc", "mq", "mk", "mv", "mo", "rq", "rk", "rv", "rg", "dt"):
            idx.extend(range(o[k], o[k] + n[k]))
        mgp = [4, 5, 6, 7, 12, 13, 14, 15, 0, 1, 2, 3, 8, 9, 10, 11]
        idx.extend([o["mg"] + i for i in mgp])
        _PERM = (np.array(idx), np.array(mgp))
    return _PERM


def run_cores(core_x, core_flag, core_pos, weights, DEPTH, SEG, debug=False, phases="PSMD", trace=False):
    perm, mgp = _perm()
    cb16, cf32 = host_consts()
    nc = build(DEPTH, SEG, debug=debug, phases=phases)
    w = weights
    shared = {
        "cb16": cb16, "cf32": cf32,
        "w_in": np.ascontiguousarray(w["w_in"][:, :, perm]),
        "w_gate": w["w_gate"], "w_branch": w["w_branch"], "w_out": w["w_out"],
        "w_ffn_gate": w["w_ffn_gate"], "w_ffn_up": w["w_ffn_up"], "w_ffn_down": w["w_ffn_down"],
        "norm_mix_g": w["norm_mix_g"], "b_gate": w["b_gate"],
        "conv_w": np.ascontiguousarray(w["conv_w"].reshape(DEPTH, -1)), "conv_b": w["conv_b"],
        "ssd_a_log": np.ascontiguousarray(w["ssd_a_log"].reshape(DEPTH, -1)),
        "ssd_dt_bias": np.ascontiguousarray(w["ssd_dt_bias"].reshape(DEPTH, -1)),
        "ssd_d": w["ssd_d"], "ssd_norm_g": w["ssd_norm_g"],
        "mlstm_gate_bias": np.ascontiguousarray(w["mlstm_gate_bias"].reshape(DEPTH, -1)[:, mgp]),
        "mlstm_norm_g": w["mlstm_norm_g"], "ret_norm_g": w["ret_norm_g"], "norm_ffn_g": w["norm_ffn_g"],
        "final_norm_g": w["final_norm_g"],
    }
    shared = {k: np.ascontiguousarray(np.asarray(v)) for k, v in shared.items()}
    in_maps = []
    for i in range(len(core_x)):
        m = dict(shared)
        m["x"] = np.ascontiguousarray(core_x[i], dtype=np.float32)
        m["flag"] = np.full((128, 1), core_flag[i], np.float32)
        m["rope"] = rope_tables(SEG, core_pos[i])
        in_maps.append(m)
    res = run_bass_kernel_spmd(nc, in_maps, core_ids=list(range(len(core_x))), trace=trace)
    if trace:
        print("EXEC_TIME_NS", res.exec_time_ns, "nops", nc.n_instructions if hasattr(nc, "n_instructions") else None)
    return res.results


def kernel(**inputs):
    xp = np.asarray(inputs["x_prompt"], dtype=np.float32)
    xsm = np.asarray(inputs["x_sample"], dtype=np.float32)
    SEG = 4096
    DEPTH = 4
    core_x, core_flag, core_pos = [], [], []
    for i in range(4):
        core_x.append(xsm[i])
        core_flag.append(1.0)
        core_pos.append(np.arange(8192))
    for i in range(2):
        core_x.append(np.concatenate([xp[2 * i], xp[2 * i + 1]], axis=0))
        core_flag.append(0.0)
        core_pos.append(np.concatenate([np.arange(4096), np.arange(4096)]))
    for i in range(2):
        core_x.append(core_x[4 + i])
        core_flag.append(0.0)
        core_pos.append(core_pos[4 + i])
    weights = {k: np.asarray(v) for k, v in inputs.items() if k not in ("x_prompt", "x_sample")}
    res = run_cores(core_x, core_flag, core_pos, weights, DEPTH, SEG)
    y_sample = np.stack([res[i]["y"] for i in range(4)], axis=0).astype(np.float32)
    yp = []
    for i in range(2):
        yy = res[4 + i]["y"]
        yp.append(yy[0:4096])
        yp.append(yy[4096:8192])
    y_prompt = np.stack(yp, axis=0).astype(np.float32)
    return (y_prompt, y_sample)
```

```python
import math
import numpy as np
import ml_dtypes
from contextlib import ExitStack
import concourse.bass as bass
import concourse.mybir as mybir
from concourse.bass_utils import run_bass_kernel_spmd

F32 = mybir.dt.float32
BF16 = mybir.dt.bfloat16
AF = mybir.ActivationFunctionType
ALU = mybir.AluOpType

ENGS = ("pe", "act", "dve", "pool", "sp")
NDMA = 24
INTERLEAVE = True
NOSELF = False
EPS = 1e-6
D = 1024
FF = 2816
NEG = -60000.0
QS = 128 ** -0.5
A_COLS = 8704
OFF = dict(z=0, xbc=1024, mq=2560, mk=3072, mv=3584, mo=4608, rq=5632, rk=6144, rv=6656, rg=7680)
LOGG = [math.log(1.0 - math.exp(v)) for v in
        np.linspace(math.log(1.0 / 32.0), math.log(1.0 / 512.0), 4, dtype=np.float32).astype(np.float64)]


class Ctx:
    def __init__(self, nc, es):
        self.nc = nc
        self.es = es
        self.q = {e: [] for e in ENGS}
        self.cnt = {e: 0 for e in ENGS}
        self.seen = {e: {} for e in ENGS}
        self.lastw = {}
        self.rd = {}
        self.semh = {}
        for e in ENGS:
            self.semh[e] = es.enter_context(nc.semaphore("s_" + e))
        for i in range(NDMA):
            self.semh[("d", i)] = es.enter_context(nc.semaphore("s_d%d" % i))
        self.nsw = 0
        self.dma_cnt = [0] * NDMA
        self.dma_last = [None] * NDMA
        self.dma_rr = 0
        self.nops = 0
        self.pes = None
        self.cap = None

    def capture(self, f):
        prev = self.cap
        self.cap = []
        f()
        out = self.cap
        self.cap = prev
        return out

    def replay_interleaved(self, streams):
        streams = [s_ for s_ in streams if s_]
        if not streams:
            return
        n = max(len(s_) for s_ in streams)
        pos = [0] * len(streams)
        for k in range(1, n + 1):
            for si, s_ in enumerate(streams):
                tgt = (k * len(s_) + n - 1) // n
                while pos[si] < tgt:
                    self.op(*s_[pos[si]])
                    pos[si] += 1

    def sb(self, name, shape, dtype=F32):
        st = self.pes if self.pes is not None else self.es
        self.nsb = getattr(self, "nsb", 0) + 1
        return st.enter_context(self.nc.sbuf_tensor("sb%d_%s" % (self.nsb, name), list(shape), dtype))

    def op(self, eng, fn, reads=(), writes=(), dma=False):
        if self.cap is not None:
            self.cap.append((eng, fn, tuple(reads), tuple(writes), dma))
            return
        evs = {}

        def add(ev):
            if ev is None:
                return
            s, v = ev
            if evs.get(s, 0) < v:
                evs[s] = v

        for k in reads:
            add(self.lastw.get(k))
        for k in writes:
            add(self.lastw.get(k))
            for s, v in self.rd.get(k, {}).items():
                add((s, v))
        if dma and eng == "pool":
            key = ("sw", self.nsw)
            self.nsw += 1
            self.semh[key] = self.es.enter_context(self.nc.semaphore("s_sw%d" % key[1]))
            ev = (key, 16)
        elif dma:
            i = self.dma_rr
            self.dma_rr = (i + 1) % NDMA
            add(self.dma_last[i])
            self.dma_cnt[i] += 16
            ev = (("d", i), self.dma_cnt[i])
            self.dma_last[i] = ev
        else:
            self.cnt[eng] += 1
            ev = (eng, self.cnt[eng])
        waits = []
        seen = self.seen[eng]
        for s, v in evs.items():
            if eng == "pe" and s == "pe":
                continue
            if NOSELF and s == eng:
                continue
            if seen.get(s, 0) < v:
                waits.append((s, v))
                seen[s] = v
        self.q[eng].append((fn, waits, ev, dma))
        for k in writes:
            self.lastw[k] = ev
            self.rd[k] = {}
        for k in reads:
            d = self.rd.setdefault(k, {})
            s, v = ev
            if d.get(s, 0) < v:
                d[s] = v
        self.nops += 1

    def dma(self, out, in_, r=(), w=(), eng="sp"):
        self.op(eng, lambda e: e.dma_start(out=out, in_=in_), r, w, dma=True)

    def mm(self, out, lhsT, rhs, start=True, stop=True, r=(), w=()):
        self.op("pe", lambda e: e.matmul(out, lhsT, rhs, start=start, stop=stop), r, w)

    def tr(self, out, in_, ident, r=(), w=()):
        self.op("pe", lambda e: e.transpose(out, in_, ident), r, w)

    def act(self, out, in_, func, r=(), w=(), **kw):
        self.op("act", lambda e: e.activation(out, in_, func, **kw), r, w)

    def tt(self, out, in0, in1, op, r=(), w=(), eng="dve"):
        self.op(eng, lambda e: e.tensor_tensor(out, in0, in1, op), r, w)

    def ts(self, out, in0, s1, s2, op0, op1=None, r=(), w=(), eng="dve"):
        if op1 is None:
            self.op(eng, lambda e: e.tensor_scalar(out, in0, s1, None, op0), r, w)
        else:
            self.op(eng, lambda e: e.tensor_scalar(out, in0, s1, s2, op0, op1), r, w)

    def stt(self, out, in0, sc, in1, op0, op1, r=(), w=()):
        self.op("dve", lambda e: e.scalar_tensor_tensor(out, in0, sc, in1, op0, op1), r, w)

    def cp(self, out, in_, r=(), w=(), eng="dve"):
        if eng == "act":
            self.op("act", lambda e: e.copy(out, in_), r, w)
        else:
            self.op(eng, lambda e: e.tensor_copy(out, in_), r, w)

    def memset(self, ap, val, w=(), eng="pool"):
        self.op(eng, lambda e: e.memset(ap, val), (), w)

    def recip(self, out, in_, r=(), w=()):
        self.op("dve", lambda e: e.reciprocal(out, in_), r, w)

    def emit(self, final=False):
        nc = self.nc
        fence = []
        for key in list(self.semh.keys()):
            if isinstance(key, tuple) and key[0] == "sw" and key not in getattr(self, "swdone", set()):
                fence.append((key, 16))
                self.swdone = getattr(self, "swdone", set()) | {key}
        for i in range(NDMA):
            if self.dma_cnt[i]:
                fence.append((("d", i), self.dma_cnt[i]))
        for e in ENGS:
            if self.cnt[e]:
                fence.append((e, self.cnt[e]))
        qs = self.q
        with nc.Block() as block:
            def run(name, e):
                for fn, waits, ev, dma in qs[name]:
                    for s, v in waits:
                        e.wait_ge(self.semh[s], v)
                    ins = fn(e)
                    ins.then_inc(self.semh[ev[0]], 16 if dma else 1)
                for s, v in fence:
                    if s != name:
                        e.wait_ge(self.semh[s], v)

            @block.tensor
            def _(e):
                run("pe", e)

            @block.scalar
            def _(e):
                run("act", e)

            @block.vector
            def _(e):
                run("dve", e)

            @block.gpsimd
            def _(e):
                run("pool", e)

            @block.sync
            def _(e):
                run("sp", e)
        self.q = {e: [] for e in ENGS}
        for e in ENGS:
            for s, v in fence:
                self.seen[e][s] = v
        self.lastw = {}
        self.rd = {}


def host_consts():
    j = np.arange(128)
    tri = (j[:, None] <= j[None, :]).astype(np.float32)
    triT = (j[:, None] >= j[None, :]).astype(np.float32)
    ident = np.eye(128, dtype=np.float32)
    ones = np.ones((128, 128), np.float32)
    nmf = np.where(j[:, None] <= j[None, :], 0.0, NEG).astype(np.float32)
    nmb = np.where(j[:, None] >= j[None, :], 0.0, NEG).astype(np.float32)
    cb16 = np.concatenate([ident, tri, triT, ones, np.tile(nmf, (1, 4)), np.tile(nmb, (1, 4))], axis=1)
    lg = np.array(LOGG, np.float64)
    dist = np.abs(j[:, None] - j[None, :]).astype(np.float64)
    dtab = np.concatenate([np.exp(dist * lg[h]) for h in range(4)], axis=1)
    xif = np.tile(np.concatenate([np.exp((j + 1) * lg[h]) for h in range(4)])[None, :], (128, 1))
    xib = np.tile(np.concatenate([np.exp((128 - j) * lg[h]) for h in range(4)])[None, :], (128, 1))
    zf = np.stack([np.exp((127 - j) * lg[h]) for h in range(4)], axis=1)
    zb = np.stack([np.exp(j * lg[h]) for h in range(4)], axis=1)
    cf32 = np.concatenate([dtab, xif, xib, zf, zb, ident], axis=1).astype(np.float32)
    return cb16.astype(ml_dtypes.bfloat16), cf32


C16 = dict(ident=0, tri=128, triT=256, ones=384, nmf=512, nmb=1024)
C32 = dict(dtab=0, xif=512, xib=1024, zf=1536, zb=1540, identf=1544)
NC16 = 1536
NC32 = 1672


def rope_tables(seglen, nseg_positions):
    half = 64
    inv = (10000.0 ** (-np.arange(half, dtype=np.float32) / half)).astype(np.float32)
    pos = nseg_positions.astype(np.float32)
    ang = pos[:, None] * inv[None, :]
    cos = np.cos(ang).astype(np.float32)
    sin = np.sin(ang).astype(np.float32)
    return np.concatenate([cos, sin, cos * np.float32(QS), sin * np.float32(QS)], axis=1)


def build(DEPTH, SEG, debug=False, phases="PSMD"):
    NCHS = SEG // 128
    NCH = 2 * NCHS
    NT = 2 * SEG
    NBLK = NT // 512
    XBR = 2 * (SEG + 3)
    nc = bass.Bass("TRN2", target_bir_lowering=False)

    def din(name, shape, dt=F32):
        return nc.dram_tensor(name, list(shape), dt, kind="ExternalInput").ap()

    def dscr(name, shape, dt):
        return nc.dram_tensor(name, list(shape), dt, kind=("ExternalOutput" if debug else "Internal")).ap()

    x_in = din("x", [NT, D])
    flag_in = din("flag", [128, 1])
    rope_in = din("rope", [NT, 256])
    cb16_in = din("cb16", [128, NC16], BF16)
    cf32_in = din("cf32", [128, NC32])
    w_in = din("w_in", [DEPTH, D, 8752])
    w_gate = din("w_gate", [DEPTH, D, 3072])
    w_br = din("w_branch", [DEPTH, 3, D, D])
    w_out = din("w_out", [DEPTH, D, D])
    w_fg = din("w_ffn_gate", [DEPTH, D, FF])
    w_fu = din("w_ffn_up", [DEPTH, D, FF])
    w_fd = din("w_ffn_down", [DEPTH, FF, D])
    vecs = {}
    for name, n in [("norm_mix_g", D), ("b_gate", 3072), ("conv_w", 4 * 1536), ("conv_b", 1536),
                    ("ssd_a_log", 32), ("ssd_dt_bias", 32), ("ssd_d", 16), ("ssd_norm_g", D),
                    ("mlstm_gate_bias", 16), ("mlstm_norm_g", D), ("ret_norm_g", D), ("norm_ffn_g", D)]:
        vecs[name] = din(name, [DEPTH, n])
    fin_g = din("final_norm_g", [D])
    y_out = nc.dram_tensor("y", [NT, D], F32, kind="ExternalOutput").ap()

    wb_in = dscr("wb_in", [DEPTH, D, 8752], BF16) if False else nc.dram_tensor("wb_in", [DEPTH, D, 8752], BF16, kind="Internal").ap()
    wb_gate = nc.dram_tensor("wb_gate", [DEPTH, D, 3072], BF16, kind="Internal").ap()
    wb_br = nc.dram_tensor("wb_br", [DEPTH, 3, D, D], BF16, kind="Internal").ap()
    wb_out = nc.dram_tensor("wb_out", [DEPTH, D, D], BF16, kind="Internal").ap()
    wb_fg = nc.dram_tensor("wb_fg", [DEPTH, D, FF], BF16, kind="Internal").ap()
    wb_fu = nc.dram_tensor("wb_fu", [DEPTH, D, FF], BF16, kind="Internal").ap()
    wb_fd = nc.dram_tensor("wb_fd", [DEPTH, FF, D], BF16, kind="Internal").ap()
    XS = dscr("XS", [NT, D], F32)
    A_ = dscr("A_", [NT, A_COLS], BF16)
    XB = dscr("XB", [XBR, 1536], BF16)
    G_ = dscr("G_", [NT, 48], F32)
    GT = dscr("GT", [NT, 3072], BF16)
    SB = dscr("SBst", [NCH, 128, 3076], BF16)
    MO = dscr("MO", [NT, 3072], BF16)

    def xbrow(tok):
        s = tok // SEG
        return s * (SEG + 3) + 2 + (tok - s * SEG)

    with ExitStack() as es:
        c = Ctx(nc, es)
        c16 = c.sb("c16", [128, NC16], BF16)
        c32 = c.sb("c32", [128, NC32], F32)
        flag = c.sb("flag", [128, 1], F32)
        pbs = [es.enter_context(nc.psum_tensor("pb%d" % i, [128, 512], F32)) for i in range(8)]

        def C16s(name, n=128):
            return c16[:, C16[name]:C16[name] + n]

        ident = C16s("ident")
        identf = c32[:, C32["identf"]:C32["identf"] + 128]

        with ExitStack() as pes:
            c.pes = pes
            c.dma(c16[:], cb16_in, w=["c16"])
            c.dma(c32[:], cf32_in, w=["c32"])
            c.dma(flag[:], flag_in, w=["flag"])
            for src, dst in [(w_in, wb_in), (w_gate, wb_gate), (w_out, wb_out), (w_fg, wb_fg),
                             (w_fu, wb_fu), (w_fd, wb_fd)]:
                for l in range(DEPTH):
                    s2 = src[l].rearrange("k c -> (k c)").rearrange("(r e) -> r e", e=1024)
                    d2 = dst[l].rearrange("k c -> (k c)").rearrange("(r e) -> r e", e=1024)
                    c.dma(d2, s2, w=["wcast"], eng="pool")
            for l in range(DEPTH):
                s2 = w_br[l].rearrange("b k c -> (b k c)").rearrange("(r e) -> r e", e=1024)
                d2 = wb_br[l].rearrange("b k c -> (b k c)").rearrange("(r e) -> r e", e=1024)
                c.dma(d2, s2, w=["wcast"], eng="pool")
            zt = c.sb("zt", [4, 1536], BF16)
            c.memset(zt[:], 0.0, w=["zt"])
            for s in range(2):
                c.dma(XB[s * (SEG + 3):s * (SEG + 3) + 2, :], zt[0:2, :], r=["zt"], w=["XBpad"])
                c.dma(XB[s * (SEG + 3) + 2 + SEG:s * (SEG + 3) + 3 + SEG, :], zt[0:1, :], r=["zt"], w=["XBpad"])
            c.emit()
            c.pes = None

        def bcast_load(t, src1d, key):
            c.dma(t, src1d.partition_broadcast(128), w=[key])

        def rmsnorm_rows(xt, key_x, gt, out, key_out, tmpf, ss, n=D):
            c.act(tmpf, xt, AF.Square, r=[key_x], w=["nrm_tmp", "nrm_ss"], accum_out=ss)
            c.act(ss, ss, AF.Ln, r=["nrm_ss"], w=["nrm_ss"], scale=1.0 / n, bias=EPS)
            c.act(ss, ss, AF.Exp, r=["nrm_ss"], w=["nrm_ss"], scale=-0.5)
            c.stt(out, xt, ss, gt, ALU.mult, ALU.mult, r=[key_x, "nrm_ss", "ltab"], w=[key_out])

        wcount = [0]

        DB = [0, 1, 3, 4, 5, 6, 7]
        pcount = [0]

        pending = [None]

        def dense(actT, akey, KT, wsrc, ncols, evac, fmajor=False, wt=None, wkey=None, nxt=None):
            wv = wsrc.rearrange("(kt p) c -> p kt c", p=128)
            ngrp = (ncols + 511) // 512

            def load(g):
                c0 = g * 512
                cw = min(512, ncols - c0)
                wi = wcount[0] % 2
                wcount[0] += 1
                c.dma(wt[wi][:, 0:KT, 0:cw], wv[:, :, c0:c0 + cw], r=["wcast"], w=["wt%d" % wi])
                return wi

            if pending[0] is not None and wkey is not None and pending[0][0] == wkey:
                nxtw = pending[0][1]
            else:
                nxtw = load(0)
            pending[0] = None
            for g in range(ngrp):
                c0 = g * 512
                cw = min(512, ncols - c0)
                wi = nxtw
                wtile = wt[wi]
                if g + 1 < ngrp:
                    nxtw = load(g + 1)
                elif nxt is not None:
                    wsrc2, KT2, ncols2, wkey2 = nxt
                    wv2 = wsrc2.rearrange("(kt p) c -> p kt c", p=128)
                    cw2 = min(512, ncols2)
                    wi2 = wcount[0] % 2
                    wcount[0] += 1
                    c.dma(wt[wi2][:, 0:KT2, 0:cw2], wv2[:, :, 0:cw2], r=["wcast"], w=["wt%d" % wi2])
                    pending[0] = (wkey2, wi2)
                if not fmajor:
                    for t in range(4):
                        pi = DB[pcount[0] % len(DB)]
                        pcount[0] += 1
                        pb = pbs[pi]
                        for kt in range(KT):
                            c.mm(pb[:, 0:cw], actT[:, kt, t * 128:(t + 1) * 128], wtile[:, kt, 0:cw],
                                 start=(kt == 0), stop=(kt == KT - 1), r=[akey, "wt%d" % wi], w=["pb%d" % pi])
                        evac(c0, cw, t, pb[:, 0:cw], "pb%d" % pi)
                else:
                    for m in range(cw // 128):
                        pi = DB[pcount[0] % len(DB)]
                        pcount[0] += 1
                        pb = pbs[pi]
                        for kt in range(KT):
                            c.mm(pb[:, :], wtile[:, kt, m * 128:(m + 1) * 128], actT[:, kt, :],
                                 start=(kt == 0), stop=(kt == KT - 1), r=[akey, "wt%d" % wi], w=["pb%d" % pi])
                        evac(c0 + m * 128, 128, None, pb[:, :], "pb%d" % pi)

        def transpose_to(dstT, dkey, src, skey, ntile, t):
            pv = pbs[2][:].bitcast(BF16)
            for k0 in range(0, ntile, 8):
                n = min(8, ntile - k0)
                for k in range(n):
                    c.tr(pv[:, k * 128:(k + 1) * 128], src[:, (k0 + k) * 128:(k0 + k + 1) * 128], ident,
                         r=[skey, "c16"], w=["pb2"])
                c.cp(dstT[:, k0:k0 + n, t * 128:(t + 1) * 128],
                     pv[:, 0:n * 128].rearrange("p (k q) -> p k q", k=n), r=["pb2"], w=[dkey], eng="act")

        for l in range(DEPTH):
            xsrc = x_in if l == 0 else XS
            last = (l == DEPTH - 1)
            def setup_mixer(mode):
                if True:
                    isM = (mode == "M")
                    cwt = c.sb("cwt", [128, 4, 1536], BF16)
                    cbt = c.sb("cbt", [128, 1536], F32)
                    dtb = c.sb("dtb", [128, 32], F32)
                    aneg = c.sb("aneg", [128, 32], F32)
                    mgb = c.sb("mgb", [128, 16], F32)

                    bcast_load(cbt[:], vecs["conv_b"][l], "ltab")
                    bcast_load(dtb[:], vecs["ssd_dt_bias"][l], "ltab")
                    bcast_load(aneg[:], vecs["ssd_a_log"][l], "ltab")
                    bcast_load(mgb[:], vecs["mlstm_gate_bias"][l], "ltab")
                    c.act(aneg[:], aneg[:], AF.Exp, r=["ltab"], w=["ltab"])
                    c.ts(aneg[:], aneg[:], -1.0, None, ALU.mult, r=["ltab"], w=["ltab"])
                    if isM:
                        dsk = c.sb("dsk", [128, 16], F32)
                        nga = c.sb("nga", [128, D], BF16)
                        ngb = c.sb("ngb", [128, D], BF16)
                        ngc = c.sb("ngc", [128, D], BF16)
                        bcast_load(dsk[:], vecs["ssd_d"][l], "ltab")
                        NGL = [(nga, "ssd_norm_g"), (ngb, "mlstm_norm_g"), (ngc, "ret_norm_g")]
                    xb4 = [c.sb("xb4_%d" % i, [128, 1536], BF16) for i in range(4)]
                    cacc = c.sb("cacc", [128, 1536], F32)
                    for t_ in range(4):
                        c.dma(cacc[:], vecs["conv_w"][l][t_ * 1536:(t_ + 1) * 1536].partition_broadcast(128), w=["cacc"])
                        c.cp(cwt[:, t_, :], cacc[:], r=["cacc"], w=["ltab"], eng="act")
                    ctmp = c.sb("ctmp", [128, 1536], F32)
                    xactL = [c.sb("xact%d" % i, [128, 1536], BF16) for i in range(2)]
                    gtL = [c.sb("gt%d" % i, [128, 48], F32) for i in range(2)]
                    smL = [{n: c.sb("sm%d_%s" % (i, n), [128, 40], F32) for n in
                            ("dtp", "dt", "ld", "li", "csx", "tot", "bw", "wend", "ecs", "dec", "gml", "sp")} for i in range(2)]
                    ldbL = [c.sb("ldb%d" % i, [128, 40], BF16) for i in range(2)]
                    st = c.sb("st", [128, 3076], F32)
                    stbL = [c.sb("stb%d" % i, [128, 3076], BF16) for i in range(2)]
                    vaug = c.sb("vaug", [128, 4, 257], BF16)
                    xw = c.sb("xw", [128, D], BF16)
                    kwL = [c.sb("kw%d" % i, [128, 128], BF16) for i in range(2)]
                    c.memset(vaug[:], 1.0, w=["vaug"])
                    c.memset(st[:], 0.0, w=["st"])
                    c.memset(stbL[0][:], 0.0, w=["stb0"])
                    c.memset(stbL[1][:], 0.0, w=["stb1"])
                    if isM:
                        atL = [c.sb("at%d" % i, [128, A_COLS - 1536], BF16) for i in range(2)]
                        sbl = c.sb("sbl", [128, 3076], BF16)
                        R = c.sb("R", [128, 40, 128], BF16)
                        WL = [c.sb("W%d" % i, [128, 40, 128], BF16) for i in range(2)]
                        Mb = c.sb("Mb", [128, 16, 128], BF16)
                        Gb = c.sb("Gb", [128, 256], BF16)
                        bcT = c.sb("bcT", [128, 4, 128], BF16)
                        qkT = c.sb("qkT", [128, 16, 128], BF16)
                        f1 = c.sb("f1", [128, D], F32)
                        f2 = c.sb("f2", [128, D], F32)
                        for ngt, ngn in NGL:
                            c.dma(f1[:], vecs[ngn][l].partition_broadcast(128), w=["f1"])
                            c.cp(ngt[:], f1[:], r=["f1"], w=["ltab"], eng="act")
                        jk1 = c.sb("jk1", [128, D], BF16)
                        jk2 = c.sb("jk2", [128, 256], BF16)
                        jk3 = c.sb("jk3", [128, 256], BF16)
                        g0 = c.sb("g0", [128, D], BF16)
                        g1 = c.sb("g1", [128, D], BF16)
                        g2 = c.sb("g2", [128, D], BF16)
                        ry = c.sb("ry", [128, D], F32)
                        Pm = c.sb("Pm", [128, 8, 128], BF16)
                        dd8 = c.sb("dd8", [128, 8], F32)
                        rr8 = c.sb("rr8", [128, 8], F32)
                        er8 = c.sb("er8", [128, 8], F32)
                        hacc = c.sb("hacc", [128, 4, 256], F32)
                        Pr = c.sb("Pr", [128, 4, 128], BF16)
                        qxf = c.sb("qxf", [128, 4, 128], BF16)
                        qxb = c.sb("qxb", [128, 4, 128], BF16)
                        mot = c.sb("mot", [128, 3072], BF16)
                        ss4 = c.sb("ss4", [128, 4], F32)
                        ss4r = c.sb("ss4r", [128, 4], F32)
                        ss1 = c.sb("ss1", [128, 1], F32)
                    else:
                        kv = c.sb("kv", [128, 3072], BF16)

                    order = list(range(NCH)) if isM else list(range(NCH - 1, -1, -1))
                    fo = 0 if isM else 20

                    def loadsA(ch, p):
                        r0 = ch * 128
                        xr = xbrow(r0)
                        for t in range(4):
                            c.dma(xb4[t][:], XB[xr + t - 2:xr + t - 2 + 128, :], r=["XB"], w=["xb4_%d" % t])
                        c.dma(gtL[p][:], G_[r0:r0 + 128, :], r=["G_"], w=["gt%d" % p])

                    def prepA(ch, p):
                        xact = xactL[p]
                        xk = "xact%d" % p
                        gt = gtL[p]
                        gk = "gt%d" % p
                        S = smL[p]
                        ldb = ldbL[p]

                        def k(n):
                            return "%s%d" % (n, p)

                        c.tt(cacc[:], xb4[0][:], cwt[:, 0, :], ALU.mult, r=["xb4_0", "ltab"], w=["cacc"], eng="pool")
                        for t in range(1, 4):
                            c.tt(ctmp[:], xb4[t][:], cwt[:, t, :], ALU.mult, r=["xb4_%d" % t, "ltab"], w=["ctmp"], eng="pool")
                            c.tt(cacc[:], cacc[:], ctmp[:], ALU.add, r=["cacc", "ctmp"], w=["cacc"])
                        c.tt(cacc[:], cacc[:], cbt[:], ALU.add, r=["cacc", "ltab"], w=["cacc"])
                        c.act(xact[:], cacc[:], AF.Silu, r=["cacc"], w=[xk])
                        c.tt(S["dtp"][:, 0:32], gt[:, 0:32], dtb[:], ALU.add, r=[gk, "ltab"], w=[k("dtp")])
                        c.act(S["dtp"][:, 0:32], S["dtp"][:, 0:32], AF.Exp, r=[k("dtp")], w=[k("dtp")])
                        c.act(S["dt"][:, 0:32], S["dtp"][:, 0:32], AF.Ln, r=[k("dtp")], w=[k("dt")], bias=1.0)
                        c.tt(S["gml"][:, 0:16], gt[:, 32:48], mgb[:], ALU.add, r=[gk, "ltab"], w=[k("gml")])
                        c.act(S["sp"][:, 0:8], S["gml"][:, 0:8], AF.Exp, r=[k("gml")], w=[k("sp")], scale=-1.0)
                        c.act(S["sp"][:, 0:8], S["sp"][:, 0:8], AF.Ln, r=[k("sp")], w=[k("sp")], bias=1.0)
                        c.act(S["li"][:, 0:16], S["dt"][:, 0:16], AF.Ln, r=[k("dt")], w=[k("li")])
                        c.act(S["li"][:, 20:36], S["dt"][:, 16:32], AF.Ln, r=[k("dt")], w=[k("li")])
                        c.cp(S["li"][:, 16:20], S["gml"][:, 8:12], r=[k("gml")], w=[k("li")])
                        c.cp(S["li"][:, 36:40], S["gml"][:, 12:16], r=[k("gml")], w=[k("li")])
                        c.tt(S["ld"][:, 0:16], S["dt"][:, 0:16], aneg[:, 0:16], ALU.mult, r=[k("dt"), "ltab"], w=[k("ld")])
                        c.tt(S["ld"][:, 20:36], S["dt"][:, 16:32], aneg[:, 16:32], ALU.mult, r=[k("dt"), "ltab"], w=[k("ld")])
                        c.ts(S["ld"][:, 16:20], S["sp"][:, 0:4], -1.0, None, ALU.mult, r=[k("sp")], w=[k("ld")])
                        c.ts(S["ld"][:, 36:40], S["sp"][:, 4:8], -1.0, None, ALU.mult, r=[k("sp")], w=[k("ld")])
                        c.cp(ldb[:], S["ld"][:], r=[k("ld")], w=[k("ldb")])
                        pc = pbs[0]
                        c.mm(pc[:, 0:40], C16s("tri"), ldb[:], r=["c16", k("ldb")], w=["pb0"])
                        c.mm(pc[:, 40:80], C16s("triT"), ldb[:], r=["c16", k("ldb")], w=["pb0"])
                        c.mm(pc[:, 80:120], C16s("ones"), ldb[:], r=["c16", k("ldb")], w=["pb0"])
                        c.cp(S["csx"][:, 0:20], pc[:, 0:20], r=["pb0"], w=[k("csx")])
                        c.cp(S["csx"][:, 20:40], pc[:, 60:80], r=["pb0"], w=[k("csx")])
                        c.cp(S["tot"][:], pc[:, 80:120], r=["pb0"], w=[k("tot")], eng="act")
                        c.tt(S["bw"][:], S["li"][:], S["csx"][:], ALU.subtract, r=[k("li"), k("csx")], w=[k("bw")])
                        c.tt(S["wend"][:], S["bw"][:], S["tot"][:], ALU.add, r=[k("bw"), k("tot")], w=[k("wend")])
                        c.act(S["wend"][:], S["wend"][:], AF.Exp, r=[k("wend")], w=[k("wend")])
                        c.act(S["ecs"][:], S["csx"][:], AF.Exp, r=[k("csx")], w=[k("ecs")])
                        c.act(S["dec"][:], S["tot"][:], AF.Exp, r=[k("tot")], w=[k("dec")])
                        if isM:
                            W = WL[p]
                            wk = "W%d" % p
                            c.tt(R[:, 0:20, :], ldb[:, 0:20].unsqueeze(2).to_broadcast([128, 20, 128]),
                                 C16s("tri").unsqueeze(1).to_broadcast([128, 20, 128]), ALU.mult,
                                 r=[k("ldb"), "c16"], w=["R"], eng="pool")
                            c.tt(R[:, 20:40, :], ldb[:, 20:40].unsqueeze(2).to_broadcast([128, 20, 128]),
                                 C16s("triT").unsqueeze(1).to_broadcast([128, 20, 128]), ALU.mult,
                                 r=[k("ldb"), "c16"], w=["R"], eng="pool")
                            for g in range(10):
                                pw = pbs[0]
                                nm = C16s("nmf", 512) if g < 5 else C16s("nmb", 512)
                                c.mm(pw[:, :], C16s("ones"), R[:, 4 * g:4 * g + 4, :].rearrange("p a b -> p (a b)"),
                                     start=True, stop=False, r=["c16", "R"], w=["pb0"])
                                c.mm(pw[:, :], ident, nm, start=False, stop=True, r=["c16"], w=["pb0"])
                                for j in range(4):
                                    cc = 4 * g + j
                                    c.act(W[:, cc, :], pw[:, j * 128:(j + 1) * 128], AF.Exp,
                                          r=["pb0", k("bw")], w=[wk], bias=S["bw"][:, cc:cc + 1])
                            c.tt(W[:, 0:16, :], W[:, 0:16, :], W[:, 20:36, :], ALU.add, r=[wk], w=[wk], eng="pool")

                    def mainA(ch, p, pin, pout):
                        r0 = ch * 128
                        xact = xactL[p]
                        xk = "xact%d" % p
                        S = smL[p]

                        def k(n):
                            return "%s%d" % (n, p)

                        stb = stbL[pin]
                        sk = "stb%d" % pin
                        if isM:
                            at = atL[p]
                            AK = "at%d" % p
                            W = WL[p]
                            wk = "W%d" % p
                            c.dma(at[:, 0:1024], A_[r0:r0 + 128, 0:1024], r=["A_"], w=[AK])
                            c.dma(at[:, 1024:A_COLS - 1536], A_[r0:r0 + 128, 2560:A_COLS], r=["A_"], w=[AK])
                            c.dma(sbl[:], SB[ch], r=["SB"], w=["sbl"])
                            mk = at[:, OFF["mk"] - 1536:OFF["mk"] - 1536 + 512]
                            mv = at[:, OFF["mv"] - 1536:OFF["mv"] - 1536 + 1024]
                            rk = at[:, OFF["rk"] - 1536:OFF["rk"] - 1536 + 512]
                            rv = at[:, OFF["rv"] - 1536:OFF["rv"] - 1536 + 1024]
                            kvkey = AK
                        else:
                            c.dma(kv[:, 0:1536], A_[r0:r0 + 128, OFF["mk"]:OFF["mk"] + 1536], r=["A_"], w=["kv"])
                            c.dma(kv[:, 1536:3072], A_[r0:r0 + 128, OFF["rk"]:OFF["rk"] + 1536], r=["A_"], w=["kv"])
                            mk = kv[:, 0:512]
                            mv = kv[:, 512:1536]
                            rk = kv[:, 1536:2048]
                            rv = kv[:, 2048:3072]
                            kvkey = "kv"
                            c.dma(SB[ch], stb[:], r=[sk], w=["SB"])
                        c.cp(vaug[:, :, 0:256], mv.rearrange("p (h v) -> p h v", h=4), r=[kvkey], w=["vaug"], eng="pool")
                        if isM:
                            c.act(g0[:], at[:, 0:1024], AF.Silu, r=[AK], w=["g0"])
                            c.act(g2[:], at[:, OFF["rg"] - 1536:OFF["rg"] - 1536 + 1024], AF.Silu, r=[AK], w=["g2"])
                            c.act(g1[:], at[:, OFF["mo"] - 1536:OFF["mo"] - 1536 + 1024], AF.Tanh, r=[AK], w=["g1"], scale=0.5)

                        if isM:
                            pv = pbs[2][:].bitcast(BF16)
                            for kk_ in range(4):
                                src = xact[:, 1024 + kk_ * 128:1024 + (kk_ + 1) * 128]
                                c.tr(pv[:, kk_ * 128:(kk_ + 1) * 128], src, ident, r=[xk, "c16"], w=["pb2"])
                            c.cp(bcT[:], pv[:, 0:512].rearrange("p (k q) -> p k q", k=4), r=["pb2"], w=["bcT"], eng="act")
                            for grp, base in enumerate([OFF["mq"] - 1536, OFF["rq"] - 1536]):
                                for kk_ in range(8):
                                    src = at[:, base + kk_ * 128:base + (kk_ + 1) * 128]
                                    c.tr(pv[:, kk_ * 128:(kk_ + 1) * 128], src, ident, r=[AK, "c16"], w=["pb2"])
                                c.cp(qkT[:, grp * 8:grp * 8 + 8, :], pv[:, :].rearrange("p (k q) -> p k q", k=8),
                                     r=["pb2"], w=["qkT"], eng="act")
                        c.op("__cut__", None)
                        c.tt(xw[:].rearrange("p (h d) -> p h d", h=16), xact[:, 0:1024].rearrange("p (h d) -> p h d", h=16),
                             S["wend"][:, fo:fo + 16].unsqueeze(2).to_broadcast([128, 16, 64]), ALU.mult,
                             r=[xk, k("wend")], w=["xw"], eng="pool")
                        for g in range(2):
                            c.mm(pbs[4 + g][:, :], xact[:, 1024 + g * 128:1024 + (g + 1) * 128], xw[:, g * 512:(g + 1) * 512],
                                 r=[xk, "xw"], w=["pb%d" % (4 + g)])
                        c.tt(st[:, 0:1024].rearrange("p (h d) -> p h d", h=16), st[:, 0:1024].rearrange("p (h d) -> p h d", h=16),
                             S["dec"][:, fo:fo + 16].unsqueeze(2).to_broadcast([128, 16, 64]), ALU.mult,
                             r=["st", k("dec")], w=["st"])
                        for g in range(2):
                            c.tt(st[:, g * 512:(g + 1) * 512], st[:, g * 512:(g + 1) * 512], pbs[4 + g][:, :], ALU.add,
                                 r=["st", "pb%d" % (4 + g)], w=["st"])
                        mo_ = 16 + fo
                        for h in range(4):
                            kw = kwL[h % 2]
                            kk = "kw%d" % (h % 2)
                            c.ts(kw[:], mk[:, h * 128:(h + 1) * 128], S["wend"][:, mo_ + h:mo_ + h + 1], None, ALU.mult,
                                 r=[kvkey, k("wend")], w=[kk])
                            pi = 4 + (h % 2)
                            c.mm(pbs[pi][:, 0:257], kw[:], vaug[:, h, :], r=[kk, "vaug"], w=["pb%d" % pi])
                            sl = st[:, 1024 + h * 257:1024 + (h + 1) * 257]
                            c.stt(sl, sl, S["dec"][:, mo_ + h:mo_ + h + 1], pbs[pi][:, 0:257], ALU.mult, ALU.add,
                                  r=["st", k("dec"), "pb%d" % pi], w=["st"])
                        zo = C32["zf"] if isM else C32["zb"]
                        for h in range(4):
                            kw = kwL[h % 2]
                            kk = "kw%d" % (h % 2)
                            c.ts(kw[:], rk[:, h * 128:(h + 1) * 128], c32[:, zo + h:zo + h + 1], None, ALU.mult,
                                 r=[kvkey, "c32"], w=[kk])
                            pi = 4 + (h % 2)
                            c.mm(pbs[pi][:, 0:256], kw[:], rv[:, h * 256:(h + 1) * 256], r=[kk, kvkey], w=["pb%d" % pi])
                            sl = st[:, 2052 + h * 256:2052 + (h + 1) * 256]
                            c.stt(sl, sl, float(math.exp(128 * LOGG[h])), pbs[pi][:, 0:256], ALU.mult, ALU.add,
                                  r=["st", "pb%d" % pi], w=["st"])
                        if (isM and ch == NCHS - 1) or ((not isM) and ch == NCHS):
                            c.ts(st[:], st[:], flag[:, 0:1], None, ALU.mult, r=["st", "flag"], w=["st"])
                        c.cp(stbL[pout][:], st[:], r=["st"], w=["stb%d" % pout], eng="act")
                        if not isM:
                            return

                        pg = pbs[3]
                        for g in range(2):
                            c.mm(pg[:, 128 + g * 128:256 + g * 128], bcT[:, g, :], bcT[:, 2 + g, :], r=["bcT"], w=["pb3g"])
                        c.cp(Gb[:], pg[:, 128:384], r=["pb3g"], w=["Gb"], eng="act")
                        for g in range(2):
                            c.tt(Mb[:, 8 * g:8 * g + 8, :], W[:, 8 * g:8 * g + 8, :],
                                 Gb[:, g * 128:(g + 1) * 128].unsqueeze(1).to_broadcast([128, 8, 128]),
                                 ALU.mult, r=[wk, "Gb"], w=["Mb"])
                        for h in range(16):
                            pyb = pbs[4 + h // 8]
                            c.mm(pyb[:, (h % 8) * 64:(h % 8 + 1) * 64], Mb[:, h, :], xact[:, h * 64:(h + 1) * 64],
                                 r=["Mb", xk], w=["pb%d" % (4 + h // 8)])
                        f1v = f1[:].rearrange("p (h d) -> p h d", h=16)
                        f2v = f2[:].rearrange("p (h d) -> p h d", h=16)
                        c.tt(f1v, xact[:, 0:1024].rearrange("p (h d) -> p h d", h=16),
                             dsk[:].unsqueeze(2).to_broadcast([128, 16, 64]), ALU.mult, r=[xk, "ltab"], w=["f1"], eng="pool")
                        for g in range(2):
                            c.tt(f1[:, g * 512:(g + 1) * 512], f1[:, g * 512:(g + 1) * 512], pbs[4 + g][:, :], ALU.add,
                                 r=["f1", "pb%d" % (4 + g)], w=["f1"])
                        for d_, (stsrc, skey, eo) in enumerate([(stb, sk, 0), (sbl, "sbl", 20)]):
                            for g in range(2):
                                c.mm(pbs[4 + g][:, :], bcT[:, 2 + g, :], stsrc[:, g * 512:(g + 1) * 512],
                                     r=["bcT", skey], w=["pb%d" % (4 + g)])
                            for g in range(2):
                                c.tt(f2v[:, 8 * g:8 * g + 8, :], pbs[4 + g][:, :].rearrange("p (h d) -> p h d", h=8),
                                     S["ecs"][:, eo + 8 * g:eo + 8 * g + 8].unsqueeze(2).to_broadcast([128, 8, 64]), ALU.mult,
                                     r=["pb%d" % (4 + g), k("ecs")], w=["f2"])
                            c.tt(f1[:], f1[:], f2[:], ALU.add, r=["f1", "f2"], w=["f1"], eng="pool")
                        c.tt(f1[:], f1[:], g0[:], ALU.mult, r=["f1", "g0"], w=["f1"])
                        c.act(jk1[:], f1[:], AF.Square, r=["f1"], w=["jk1", "ss1"], accum_out=ss1[:])
                        c.act(ss1[:], ss1[:], AF.Ln, r=["ss1"], w=["ss1"], scale=1.0 / D, bias=EPS)
                        c.act(ss1[:], ss1[:], AF.Exp, r=["ss1"], w=["ss1"], scale=-0.5)
                        c.stt(mot[:, 0:1024], f1[:], ss1[:], nga[:], ALU.mult, ALU.mult, r=["f1", "ss1", "ltab"], w=["motA"])


                    def mainB(ch, p, pin):
                        r0 = ch * 128
                        S = smL[p]

                        def k(n):
                            return "%s%d" % (n, p)

                        stb = stbL[pin]
                        sk = "stb%d" % pin
                        W = WL[p]
                        wk = "W%d" % p
                        at = atL[p]
                        AK = "at%d" % p
                        rv = at[:, OFF["rv"] - 1536:OFF["rv"] - 1536 + 1024]
                        for h in range(4):
                            c.mm(pbs[1][:, h * 128:(h + 1) * 128], qkT[:, 4 + h, :], qkT[:, h, :], r=["qkT"], w=["pb1"])
                        p03 = pbs[1][:, :].rearrange("p (h l) -> p h l", h=4)
                        c.tt(Pm[:, 0:4, :], p03, W[:, 16:20, :], ALU.mult, r=["pb1", wk], w=["Pm"])
                        c.tt(Pm[:, 4:8, :], p03, W[:, 36:40, :], ALU.mult, r=["pb1", wk], w=["Pm"])
                        onec = C16s("ones")[:, 0:1]
                        for d_ in range(2):
                            stsrc = stb if d_ == 0 else sbl
                            skey = sk if d_ == 0 else "sbl"
                            for h in range(4):
                                j = 4 * d_ + h
                                c.mm(pbs[1][:, j:j + 1], Pm[:, j, :], onec, r=["Pm", "c16"], w=["pb1"])
                                c.mm(pbs[1][:, 8 + j:9 + j], qkT[:, h, :], stsrc[:, 1024 + h * 257 + 256:1024 + (h + 1) * 257],
                                     r=["qkT", skey], w=["pb1"])
                        for d_ in range(2):
                            c0_ = 16 if d_ == 0 else 36
                            c.tt(dd8[:, 4 * d_:4 * d_ + 4], pbs[1][:, 8 + 4 * d_:12 + 4 * d_], S["ecs"][:, c0_:c0_ + 4], ALU.mult,
                                 r=["pb1", k("ecs")], w=["dd8"])
                        c.tt(dd8[:], dd8[:], pbs[1][:, 0:8], ALU.add, r=["dd8", "pb1"], w=["dd8"])
                        c.act(rr8[:], dd8[:], AF.Abs, r=["dd8"], w=["rr8"])
                        c.ts(rr8[:], rr8[:], 1.0, None, ALU.max, r=["rr8"], w=["rr8"])
                        c.recip(rr8[:], rr8[:], r=["rr8"], w=["rr8"])
                        for d_ in range(2):
                            c0_ = 16 if d_ == 0 else 36
                            c.tt(er8[:, 4 * d_:4 * d_ + 4], rr8[:, 4 * d_:4 * d_ + 4], S["ecs"][:, c0_:c0_ + 4], ALU.mult,
                                 r=["rr8", k("ecs")], w=["er8"])
                        for d_ in range(2):
                            stsrc = stb if d_ == 0 else sbl
                            skey = sk if d_ == 0 else "sbl"
                            for h in range(4):
                                j = 4 * d_ + h
                                c.mm(pbs[6][:, 0:256], Pm[:, j, :], vaug[:, h, 0:256], r=["Pm", "vaug"], w=["pb6"])
                                if d_ == 0:
                                    c.ts(hacc[:, h, :], pbs[6][:, 0:256], rr8[:, j:j + 1], None, ALU.mult,
                                         r=["pb6", "rr8"], w=["hacc"])
                                else:
                                    c.stt(hacc[:, h, :], pbs[6][:, 0:256], rr8[:, j:j + 1], hacc[:, h, :], ALU.mult, ALU.add,
                                          r=["pb6", "rr8", "hacc"], w=["hacc"])
                                c.mm(pbs[7][:, 0:256], qkT[:, h, :], stsrc[:, 1024 + h * 257:1024 + h * 257 + 256],
                                     r=["qkT", skey], w=["pb7"])
                                c.stt(hacc[:, h, :], pbs[7][:, 0:256], er8[:, j:j + 1], hacc[:, h, :], ALU.mult, ALU.add,
                                      r=["pb7", "er8", "hacc"], w=["hacc"])
                        for h in range(4):
                            c.act(jk2[:], hacc[:, h, :], AF.Square, r=["hacc"], w=["jk2", "ss4"], accum_out=ss4[:, h:h + 1])
                        c.act(ss4[:], ss4[:], AF.Ln, r=["ss4"], w=["ss4"], scale=1.0 / 256, bias=EPS)
                        c.act(ss4[:], ss4[:], AF.Exp, r=["ss4"], w=["ss4"], scale=-0.5, bias=math.log(0.5))
                        c.tt(hacc[:], hacc[:], ss4[:].unsqueeze(2).to_broadcast([128, 4, 256]), ALU.mult, r=["hacc", "ss4"], w=["hacc"])
                        hf = hacc[:].rearrange("p h v -> p (h v)")
                        c.tt(hf, hf, ngb[:], ALU.mult, r=["hacc", "ltab"], w=["hacc"], eng="pool")
                        c.stt(mot[:, 1024:2048], g1[:], 1.0, hf, ALU.add, ALU.mult, r=["hacc", "g1"], w=["motB"])

                        for h in range(4):
                            c.mm(pbs[1][:, h * 128:(h + 1) * 128], qkT[:, 12 + h, :], qkT[:, 8 + h, :], r=["qkT"], w=["pb1"])
                        c.tt(Pr[:].rearrange("p h l -> p (h l)"), pbs[1][:, :], c32[:, C32["dtab"]:C32["dtab"] + 512], ALU.mult,
                             r=["pb1", "c32"], w=["Pr"])
                        c.tt(qxf[:].rearrange("p h l -> p (h l)"), qkT[:, 8:12, :].rearrange("p h l -> p (h l)"),
                             c32[:, C32["xif"]:C32["xif"] + 512], ALU.mult, r=["qkT", "c32"], w=["qxf"], eng="pool")
                        c.tt(qxb[:].rearrange("p h l -> p (h l)"), qkT[:, 8:12, :].rearrange("p h l -> p (h l)"),
                             c32[:, C32["xib"]:C32["xib"] + 512], ALU.mult, r=["qkT", "c32"], w=["qxb"], eng="pool")
                        for h in range(4):
                            pvb = pbs[6 + h % 2]
                            pk = "pb%d" % (6 + h % 2)
                            c.mm(pvb[:, 0:256], Pr[:, h, :], rv[:, h * 256:(h + 1) * 256], start=True, stop=False,
                                 r=["Pr", AK], w=[pk])
                            c.mm(pvb[:, 0:256], qxf[:, h, :], stb[:, 2052 + h * 256:2052 + (h + 1) * 256], start=False, stop=False,
                                 r=["qxf", sk], w=[pk])
                            c.mm(pvb[:, 0:256], qxb[:, h, :], sbl[:, 2052 + h * 256:2052 + (h + 1) * 256], start=False, stop=True,
                                 r=["qxb", "sbl"], w=[pk])
                            c.cp(ry[:, h * 256:(h + 1) * 256], pvb[:, 0:256], r=[pk], w=["ry"], eng="act")
                            c.act(jk3[:], ry[:, h * 256:(h + 1) * 256], AF.Square, r=["ry"], w=["jk3", "ss4r"],
                                  accum_out=ss4r[:, h:h + 1])
                        c.act(ss4r[:], ss4r[:], AF.Ln, r=["ss4r"], w=["ss4r"], scale=1.0 / 256, bias=EPS)
                        c.act(ss4r[:], ss4r[:], AF.Exp, r=["ss4r"], w=["ss4r"], scale=-0.5)
                        c.tt(ry[:].rearrange("p (h v) -> p h v", h=4), ry[:].rearrange("p (h v) -> p h v", h=4),
                             ss4r[:].unsqueeze(2).to_broadcast([128, 4, 256]), ALU.mult, r=["ry", "ss4r"], w=["ry"])
                        c.tt(ry[:], ry[:], ngc[:], ALU.mult, r=["ry", "ltab"], w=["ry"], eng="pool")
                        c.tt(mot[:, 2048:3072], ry[:], g2[:], ALU.mult, r=["ry", "g2"], w=["motC"])

                    def drive(chs, it0):
                        if not chs:
                            return
                        loadsA(chs[0], it0 % 2)
                        prepA(chs[0], it0 % 2)
                        for j, ch in enumerate(chs):
                            it = it0 + j
                            sA = []
                            if j + 1 < len(chs):
                                loadsA(chs[j + 1], (it + 1) % 2)
                                sA = c.capture(lambda: prepA(chs[j + 1], (it + 1) % 2))
                            sB = c.capture(lambda: mainA(ch, it % 2, it % 2, (it + 1) % 2))
                            cut = [q for q, o_ in enumerate(sB) if o_[0] == "__cut__"][0]
                            sH, sB = sB[:cut], sB[cut + 1:]
                            sC = c.capture(lambda: mainB(ch, it % 2, it % 2)) if isM else []
                            nA = len(sA) * len(sH) // max(1, len(sH) + len(sB))
                            c.replay_interleaved([sA[:nA], sH])
                            c.replay_interleaved([sA[nA:], sB, sC])
                            if isM:
                                c.dma(MO[ch * 128:ch * 128 + 128, :], mot[:], r=["motA", "motB", "motC"],
                                      w=["MO", "motA", "motB", "motC"])

                    return drive, order

            with ExitStack() as pes:
              if "P" in phases:
                c.pes = pes
                g_mix = c.sb("g_mix", [128, D])
                bg = c.sb("bg", [128, 3072])
                bcast_load(g_mix[:], vecs["norm_mix_g"][l], "ltab")
                bcast_load(bg[:], vecs["b_gate"][l], "ltab")
                xt = [c.sb("xt%d" % i, [128, D]) for i in range(2)]
                ut = c.sb("ut", [128, D], BF16)
                uT = c.sb("uT", [128, 8, 512], BF16)
                wt = [c.sb("wt%d" % i, [128, 8, 512], BF16) for i in range(2)]
                NOE = 6
                oev = [c.sb("oev%d" % i, [128, 512], BF16) for i in range(NOE)]
                oef = c.sb("oef", [128, 512], F32)
                og = c.sb("og", [128, 48], F32)
                rp = c.sb("rp", [128, 4, 256], F32)
                rt = [c.sb("rt%d" % i, [128, 4, 64], F32) for i in range(4)]
                tmpf = c.sb("tmpf", [128, D], F32)
                ss = c.sb("ss", [128, 1], F32)
                ecnt = [0]
                hp = c.sb("hp", [4, 1536], BF16)
                hq = c.sb("hq", [4, 1536], BF16)
                hp2 = c.sb("hp2", [4, 1536], BF16)
                hq2 = c.sb("hq2", [4, 1536], BF16)
                DB[:] = [1, 3, 6, 7]
                driveS, orderS = setup_mixer("S")

                def pblock(b):
                    r0 = b * 512
                    c.dma(rp[:], rope_in[r0:r0 + 512, :].rearrange("(t p) e -> p t e", p=128), w=["rp"])
                    for t in range(4):
                        xi = t % 2
                        c.dma(xt[xi][:], xsrc[r0 + t * 128:r0 + (t + 1) * 128, :], r=["XS"], w=["xt%d" % xi])
                        rmsnorm_rows(xt[xi][:], "xt%d" % xi, g_mix[:], ut[:], "ut", tmpf[:], ss[:])
                        transpose_to(uT, "uT", ut, "ut", 8, t)

                    def evacA(c0, cw, t, ps, pkey, r0=r0):
                        rows = slice(r0 + t * 128, r0 + (t + 1) * 128)
                        oi = ecnt[0] % NOE
                        ecnt[0] += 1
                        o = oev[oi]
                        okey = "oev%d" % oi
                        if c0 == OFF["mq"]:
                            c.act(o[:], ps, AF.Copy, r=[pkey], w=[okey], scale=QS)
                        elif c0 in (OFF["rq"], OFF["rk"]):
                            co = rp[:, t, 0:64] if c0 == OFF["rq"] else rp[:, t, 128:192]
                            si = rp[:, t, 64:128] if c0 == OFF["rq"] else rp[:, t, 192:256]
                            cob = co.unsqueeze(1).to_broadcast([128, 4, 64])
                            sib = si.unsqueeze(1).to_broadcast([128, 4, 64])
                            p3 = ps.rearrange("p (h d) -> p h d", h=4)
                            o3 = o[:].rearrange("p (h d) -> p h d", h=4)
                            x1 = p3[:, :, 0:64]
                            x2 = p3[:, :, 64:128]
                            c.tt(rt[0][:], x1, cob, ALU.mult, r=[pkey, "rp"], w=["rt0"])
                            c.tt(rt[1][:], x2, sib, ALU.mult, r=[pkey, "rp"], w=["rt1"])
                            c.tt(rt[2][:], x1, sib, ALU.mult, r=[pkey, "rp"], w=["rt2"])
                            c.tt(rt[3][:], x2, cob, ALU.mult, r=[pkey, "rp"], w=["rt3"])
                            c.tt(o3[:, :, 0:64], rt[0][:], rt[1][:], ALU.subtract, r=["rt0", "rt1"], w=[okey], eng="pool")
                            c.tt(o3[:, :, 64:128], rt[2][:], rt[3][:], ALU.add, r=["rt2", "rt3"], w=[okey], eng="pool")
                        else:
                            c.cp(o[:], ps, r=[pkey], w=[okey], eng=("act" if (ecnt[0] % 2) else "dve"))
                        if OFF["xbc"] <= c0 < OFF["mq"]:
                            xr = xbrow(r0 + t * 128)
                            c.dma(XB[xr:xr + 128, c0 - OFF["xbc"]:c0 - OFF["xbc"] + cw], o[:, 0:cw], r=[okey], w=["XB"], eng="act")
                        else:
                            c.dma(A_[rows, c0:c0 + cw], o[:, 0:cw], r=[okey], w=["A_"], eng="act")

                    dense(uT, "uT", 8, wb_in[l][:, 0:A_COLS], A_COLS, evacA, wt=wt, wkey="A",
                          nxt=(wb_in[l][:, A_COLS:8752], 8, 48, "B"))

                    def evacB(c0, cw, t, ps, pkey, r0=r0):
                        rows = slice(r0 + t * 128, r0 + (t + 1) * 128)
                        c.cp(og[:], ps, r=[pkey], w=["og"])
                        c.dma(G_[rows, :], og[:], r=["og"], w=["G_"], eng="act")

                    dense(uT, "uT", 8, wb_in[l][:, A_COLS:8752], 48, evacB, wt=wt, wkey="B",
                          nxt=(wb_gate[l], 8, 3072, "G"))

                    def evacG(c0, cw, t, ps, pkey, r0=r0):
                        rows = slice(r0 + t * 128, r0 + (t + 1) * 128)
                        oi = ecnt[0] % NOE
                        ecnt[0] += 1
                        o = oev[oi]
                        okey = "oev%d" % oi
                        c.tt(oef[:], ps, bg[:, c0:c0 + cw], ALU.add, r=[pkey, "ltab"], w=["oef"])
                        c.act(o[:], oef[:], AF.Sigmoid, r=["oef"], w=[okey])
                        c.dma(GT[rows, c0:c0 + cw], o[:], r=[okey], w=["GT"], eng="act")

                    dense(uT, "uT", 8, wb_gate[l], 3072, evacG, wt=wt, wkey="G",
                          nxt=((wb_in[l][:, 0:A_COLS], 8, A_COLS, "A") if b > 0 else None))
                itS = 0
                doneS = 0
                for b in range(NBLK - 1, -1, -1):
                    opsP = c.capture(lambda: pblock(b))
                    grp = []
                    while doneS < len(orderS) and orderS[doneS] >= 4 * (b + 1) + 1:
                        grp.append(orderS[doneS])
                        doneS += 1
                    opsS = c.capture(lambda: driveS(grp, itS)) if grp else []
                    itS += len(grp)
                    c.replay_interleaved([opsP, opsS])
                    if b == NBLK // 2:
                        c.dma(hp2[0:1, :], XB[xbrow(SEG):xbrow(SEG) + 1, :], r=["XB"], w=["hp2"])
                        c.ts(hq2[0:1, :], hp2[0:1, :], flag[0:1, 0:1], None, ALU.mult, r=["hp2", "flag"], w=["hq2"])
                        c.dma(XB[2 + SEG:3 + SEG, :], hq2[0:1, :], r=["hq2"], w=["XB"])
                    if b == NBLK // 2 - 1:
                        c.dma(hp[0:2, :], XB[xbrow(SEG - 2):xbrow(SEG - 2) + 2, :], r=["XB"], w=["hp"])
                        c.ts(hq[0:2, :], hp[0:2, :], flag[0:2, 0:1], None, ALU.mult, r=["hp", "flag"], w=["hq"])
                        c.dma(XB[(SEG + 3):(SEG + 3) + 2, :], hq[0:2, :], r=["hq"], w=["XB"])
                driveS(orderS[doneS:], itS)
                DB[:] = [0, 1, 3, 4, 5, 6, 7]
                c.emit()
                c.pes = None

            if "M" in phases:
                with ExitStack() as pes:
                    c.pes = pes
                    driveM, orderM = setup_mixer("M")
                    driveM(orderM, 0)
                    c.emit()
                    c.pes = None

            with ExitStack() as pes:
              if "D" in phases:
                c.pes = pes
                nfg = c.sb("nfg", [128, D])
                fng = c.sb("fng", [128, D])
                bcast_load(nfg[:], vecs["norm_ffn_g"][l], "ltab")
                bcast_load(fng[:], fin_g, "ltab")
                wt = [c.sb("wtd%d" % i, [128, 22, 512], BF16) for i in range(2)]
                mo4 = c.sb("mo4", [128, 4, 3072], BF16)
                gt4 = c.sb("gt4", [128, 4, 3072], BF16)
                x4 = c.sb("x4", [128, 4, D], F32)
                brT = c.sb("brT", [128, 8, 512], BF16)
                mrg = c.sb("mrg", [128, 4, D], F32)
                tmpd = c.sb("tmpd", [128, 512], F32)
                mrb = c.sb("mrb", [128, D], BF16)
                hT = c.sb("hT", [128, 8, 512], BF16)
                actT = c.sb("actT", [128, 22, 512], BF16)
                tmpf = c.sb("tmpfd", [128, D], F32)
                ss = c.sb("ssd_", [128, 1], F32)
                yo = c.sb("yo", [128, D], F32)
                for b in range(NBLK):
                    r0 = b * 512
                    c.dma(mo4[:], MO[r0:r0 + 512, :].rearrange("(t p) e -> p t e", p=128), r=["MO"], w=["mo4"])
                    c.dma(gt4[:], GT[r0:r0 + 512, :].rearrange("(t p) e -> p t e", p=128), r=["GT"], w=["gt4"])
                    c.dma(x4[:], xsrc[r0:r0 + 512, :].rearrange("(t p) e -> p t e", p=128), r=["XS"], w=["x4"])
                    for br in range(3):
                        for t in range(4):
                            transpose_to(brT, "brT", mo4[:, t, br * 1024:(br + 1) * 1024], "mo4", 8, t)

                        def evacBr(c0, cw, t, ps, pkey, br=br):
                            gsl = gt4[:, t, br * 1024 + c0:br * 1024 + c0 + cw]
                            if br == 0:
                                c.tt(mrg[:, t, c0:c0 + cw], ps, gsl, ALU.mult, r=[pkey, "gt4"], w=["mrg"])
                            else:
                                c.tt(tmpd[:, 0:cw], ps, gsl, ALU.mult, r=[pkey, "gt4"], w=["tmpd"])
                                c.tt(mrg[:, t, c0:c0 + cw], mrg[:, t, c0:c0 + cw], tmpd[:, 0:cw], ALU.add,
                                     r=["mrg", "tmpd"], w=["mrg"], eng="pool")

                        dense(brT, "brT", 8, wb_br[l][br], D, evacBr, wt=wt, wkey="br%d" % br,
                              nxt=((wb_br[l][br + 1], 8, D, "br%d" % (br + 1)) if br < 2 else (wb_out[l], 8, D, "out")))
                    for t in range(4):
                        c.cp(mrb[:], mrg[:, t, :], r=["mrg"], w=["mrb"], eng="act")
                        transpose_to(brT, "brT", mrb, "mrb", 8, t)

                    def evacO(c0, cw, t, ps, pkey):
                        c.tt(x4[:, t, c0:c0 + cw], x4[:, t, c0:c0 + cw], ps, ALU.add, r=["x4", pkey], w=["x4"])

                    dense(brT, "brT", 8, wb_out[l], D, evacO, wt=wt, wkey="out", nxt=(wb_fg[l], 8, FF, "fg"))
                    for t in range(4):
                        rmsnorm_rows(x4[:, t, :], "x4", nfg[:], mrb[:], "mrb", tmpf[:], ss[:])
                        transpose_to(hT, "hT", mrb, "mrb", 8, t)

                    def evacFg(c0, cw, t, ps, pkey):
                        c.act(actT[:, c0 // 128, :], ps, AF.Silu, r=[pkey], w=["actT"])

                    dense(hT, "hT", 8, wb_fg[l], FF, evacFg, fmajor=True, wt=wt, wkey="fg", nxt=(wb_fu[l], 8, FF, "fu"))

                    def evacFu(c0, cw, t, ps, pkey):
                        c.tt(actT[:, c0 // 128, :], ps, actT[:, c0 // 128, :], ALU.mult, r=[pkey, "actT"], w=["actT"])

                    dense(hT, "hT", 8, wb_fu[l], FF, evacFu, fmajor=True, wt=wt, wkey="fu", nxt=(wb_fd[l], 22, D, "fd"))

                    def evacFd(c0, cw, t, ps, pkey):
                        c.tt(x4[:, t, c0:c0 + cw], x4[:, t, c0:c0 + cw], ps, ALU.add, r=["x4", pkey], w=["x4"])

                    dense(actT, "actT", 22, wb_fd[l], D, evacFd, wt=wt, wkey="fd",
                          nxt=((wb_br[l][0], 8, D, "br0") if b + 1 < NBLK else None))
                    if last:
                        for t in range(4):
                            rmsnorm_rows(x4[:, t, :], "x4", fng[:], yo[:], "yo", tmpf[:], ss[:])
                            c.dma(y_out[r0 + t * 128:r0 + (t + 1) * 128, :], yo[:], r=["yo"], w=["y"], eng="act")
                    else:
                        c.dma(XS[r0:r0 + 512, :].rearrange("(t p) e -> p t e", p=128), x4[:], r=["x4"], w=["XS"], eng="act")
                c.emit()
                c.pes = None
    return nc


_PERM = None


def _perm():
    global _PERM
    if _PERM is None:
        o = dict(z=0, xbc=1024, dt=2560, mq=2592, mk=3104, mv=3616, mo=4640, mg=5664, rq=5680, rk=6192, rv=6704, rg=7728)
        n = dict(z=1024, xbc=1536, dt=32, mq=512, mk=512, mv=1024, mo=1024, mg=16, rq=512, rk=512, rv=1024, rg=1024)
        idx = []
        for k in ("z", "xbc", "mq", "mk", "mv", "mo", "rq", "rk", "rv", "rg", "dt"):
            idx.extend(range(o[k], o[k] + n[k]))
        mgp = [4, 5, 6, 7, 12, 13, 14, 15, 0, 1, 2, 3, 8, 9, 10, 11]
        idx.extend([o["mg"] + i for i in mgp])
        _PERM = (np.array(idx), np.array(mgp))
    return _PERM


def run_cores(core_x, core_flag, core_pos, weights, DEPTH, SEG, debug=False, phases="PSMD", trace=False):
    perm, mgp = _perm()
    cb16, cf32 = host_consts()
    nc = build(DEPTH, SEG, debug=debug, phases=phases)
    w = weights
    shared = {
        "cb16": cb16, "cf32": cf32,
        "w_in": np.ascontiguousarray(w["w_in"][:, :, perm]),
        "w_gate": w["w_gate"], "w_branch": w["w_branch"], "w_out": w["w_out"],
        "w_ffn_gate": w["w_ffn_gate"], "w_ffn_up": w["w_ffn_up"], "w_ffn_down": w["w_ffn_down"],
        "norm_mix_g": w["norm_mix_g"], "b_gate": w["b_gate"],
        "conv_w": np.ascontiguousarray(w["conv_w"].reshape(DEPTH, -1)), "conv_b": w["conv_b"],
        "ssd_a_log": np.ascontiguousarray(w["ssd_a_log"].reshape(DEPTH, -1)),
        "ssd_dt_bias": np.ascontiguousarray(w["ssd_dt_bias"].reshape(DEPTH, -1)),
        "ssd_d": w["ssd_d"], "ssd_norm_g": w["ssd_norm_g"],
        "mlstm_gate_bias": np.ascontiguousarray(w["mlstm_gate_bias"].reshape(DEPTH, -1)[:, mgp]),
        "mlstm_norm_g": w["mlstm_norm_g"], "ret_norm_g": w["ret_norm_g"], "norm_ffn_g": w["norm_ffn_g"],
        "final_norm_g": w["final_norm_g"],
    }
    shared = {k: np.ascontiguousarray(np.asarray(v)) for k, v in shared.items()}
    in_maps = []
    for i in range(len(core_x)):
        m = dict(shared)
        m["x"] = np.ascontiguousarray(core_x[i], dtype=np.float32)
        m["flag"] = np.full((128, 1), core_flag[i], np.float32)
        m["rope"] = rope_tables(SEG, core_pos[i])
        in_maps.append(m)
    res = run_bass_kernel_spmd(nc, in_maps, core_ids=list(range(len(core_x))), trace=trace)
    if trace:
        print("EXEC_TIME_NS", res.exec_time_ns, "nops", nc.n_instructions if hasattr(nc, "n_instructions") else None)
    return res.results


def kernel(**inputs):
    xp = np.asarray(inputs["x_prompt"], dtype=np.float32)
    xsm = np.asarray(inputs["x_sample"], dtype=np.float32)
    SEG = 4096
    DEPTH = 4
    core_x, core_flag, core_pos = [], [], []
    for i in range(4):
        core_x.append(xsm[i])
        core_flag.append(1.0)
        core_pos.append(np.arange(8192))
    for i in range(2):
        core_x.append(np.concatenate([xp[2 * i], xp[2 * i + 1]], axis=0))
        core_flag.append(0.0)
        core_pos.append(np.concatenate([np.arange(4096), np.arange(4096)]))
    for i in range(2):
        core_x.append(core_x[4 + i])
        core_flag.append(0.0)
        core_pos.append(core_pos[4 + i])
    weights = {k: np.asarray(v) for k, v in inputs.items() if k not in ("x_prompt", "x_sample")}
    res = run_cores(core_x, core_flag, core_pos, weights, DEPTH, SEG)
    y_sample = np.stack([res[i]["y"] for i in range(4)], axis=0).astype(np.float32)
    yp = []
    for i in range(2):
        yy = res[4 + i]["y"]
        yp.append(yy[0:4096])
        yp.append(yy[4096:8192])
    y_prompt = np.stack(yp, axis=0).astype(np.float32)
    return (y_prompt, y_sample)
```
